# Optimizing a Trainium2 kernel written in Bass

```python
import math
import jax, jax.numpy as jnp
from jax import lax
import numpy as np

D_MODEL = 1024
BATCH = 8
SEQ = 2048
DEPTH = 2
DEC_BATCH = 128
DEC_SEQ = 8
PAST_LEN = 16384
PAGE_SIZE = 128

D_MIX = D_MODEL
D_CONV = D_MIX // 2
CONV_GROUPS = 8
CONV_W = 31
D_V = D_MIX - D_CONV
GLA_HEADS = 4
D_K = D_V // 2
HEAD_K = D_K // GLA_HEADS
HEAD_V = D_V // GLA_HEADS
GATE_RANK = 16
GATE_NORM = 16.0
CHUNK = 64
EPS = 1e-6
SPLITS_SIZES = (D_CONV, D_CONV, D_CONV, D_K, D_K, D_V, D_V, GATE_RANK)
D_IN = sum(SPLITS_SIZES)
SPLIT_POINTS = tuple(int(s) for s in np.cumsum(SPLITS_SIZES)[:-1])

kernel_name = "hymba_conformer_conv_gla_decoder_step"


def rms_norm(x, g):
    xf = x.astype(jnp.float32)
    y = xf * lax.rsqrt(jnp.mean(xf * xf, axis=-1, keepdims=True) + EPS)
    return (y * g.astype(jnp.float32)).astype(x.dtype)


def group_layer_norm(x, g, b):
    shp = x.shape
    xf = x.astype(jnp.float32).reshape(shp[:-1] + (CONV_GROUPS, shp[-1] // CONV_GROUPS))
    mu = jnp.mean(xf, axis=-1, keepdims=True)
    var = jnp.mean(jnp.square(xf - mu), axis=-1, keepdims=True)
    y = ((xf - mu) * lax.rsqrt(var + EPS)).reshape(shp)
    return (y * g.astype(jnp.float32) + b.astype(jnp.float32)).astype(x.dtype)


def depthwise_causal_conv(u, buf, w, b):
    full = jnp.concatenate([buf.astype(u.dtype), u], axis=1)
    y = lax.conv_general_dilated(full, w[:, None, :].astype(u.dtype), window_strides=(1,),
                                 padding='VALID', dimension_numbers=('NWC', 'WIO', 'NWC'),
                                 feature_group_count=u.shape[-1])
    return y + b.astype(u.dtype), full[:, -(CONV_W - 1):, :]


def gla_chunked(q, k, v, g, S0):
    B, T, H, dk = q.shape
    c = math.gcd(T, CHUNK)
    n = T // c

    def to_chunks(a):
        return a.astype(jnp.float32).reshape(B, n, c, H, a.shape[-1]).transpose(1, 0, 3, 2, 4)

    qc = to_chunks(q) * (dk ** -0.5)
    kc, vc, gc = to_chunks(k), to_chunks(v), to_chunks(g)
    mask = jnp.tril(jnp.ones((c, c), dtype=bool))

    def step(S, inp):
        qi, ki, vi, gi = inp
        bcum = jnp.cumsum(gi, axis=-2)
        q_t = qi * jnp.exp(bcum)
        k_t = ki * jnp.exp(-bcum)
        attn = jnp.where(mask, jnp.einsum('bhtd,bhsd->bhts', q_t, k_t), 0.0)
        o = jnp.einsum('bhtd,bhdv->bhtv', q_t, S) + jnp.einsum('bhts,bhsv->bhtv', attn, vi)
        b_last = bcum[:, :, -1, :]
        k_dec = ki * jnp.exp(b_last[:, :, None, :] - bcum)
        S = S * jnp.exp(b_last)[..., None] + jnp.einsum('bhsd,bhsv->bhdv', k_dec, vi)
        return S, o

    S, o = lax.scan(step, S0.astype(jnp.float32), (qc, kc, vc, gc))
    o = o.transpose(1, 0, 3, 2, 4).reshape(B, T, H, v.shape[-1])
    return o, S


def mixer_layer(x, conv_buf, S0, norm_g, w_in, w_alpha, b_alpha, conv_w, conv_b,
                cn_g, cn_b, w_pw, b_pw, gla_g, w_out):
    B, T, _ = x.shape
    h = rms_norm(x, norm_g)
    proj = jnp.einsum('btd,de->bte', h, w_in.astype(h.dtype))
    a, gl, zc, q, k, v, zg, lr = jnp.split(proj, SPLIT_POINTS, axis=-1)
    u = a * jax.nn.sigmoid(gl)
    cv, new_buf = depthwise_causal_conv(u, conv_buf, conv_w, conv_b)
    cv = jax.nn.silu(group_layer_norm(cv, cn_g, cn_b))
    cv = jnp.einsum('btc,ce->bte', cv, w_pw.astype(cv.dtype)) + b_pw.astype(cv.dtype)
    cv = cv * jax.nn.silu(zc)
    gk = jax.nn.log_sigmoid((jnp.einsum('btr,rk->btk', lr, w_alpha.astype(lr.dtype))
                             + b_alpha.astype(lr.dtype)).astype(jnp.float32)) / GATE_NORM
    o, S = gla_chunked(q.reshape(B, T, GLA_HEADS, HEAD_K), k.reshape(B, T, GLA_HEADS, HEAD_K),
                       v.reshape(B, T, GLA_HEADS, HEAD_V), gk.reshape(B, T, GLA_HEADS, HEAD_K), S0)
    o = o * lax.rsqrt(jnp.mean(o * o, axis=-1, keepdims=True) + EPS) * gla_g.astype(jnp.float32)
    o = o.reshape(B, T, D_V).astype(x.dtype) * jax.nn.silu(zg)
    out = jnp.einsum('bte,ed->btd', jnp.concatenate([cv, o], axis=-1), w_out.astype(x.dtype))
    return x + out, new_buf, S


def setup_inputs(seed: int = 0) -> dict:
    key = jax.random.key(seed)
    ks = jax.random.split(key, 20)
    f = jnp.float32
    nrm = lambda k_, shp, s: jax.random.normal(k_, shp, f) * s
    return {
        "x_prompt": nrm(ks[0], (BATCH, SEQ, D_MODEL), 1.0),
        "x_sample": nrm(ks[1], (DEC_BATCH, DEC_SEQ, D_MODEL), 1.0),
        "cache_conv": nrm(ks[2], (DEPTH, DEC_BATCH, CONV_W - 1, D_CONV), 0.5),
        "state_gla": nrm(ks[3], (DEPTH, DEC_BATCH, GLA_HEADS, HEAD_K, HEAD_V), 0.5),
        "norm_g": 1.0 + nrm(ks[4], (DEPTH, D_MODEL), 0.02),
        "w_in": nrm(ks[5], (DEPTH, D_MODEL, D_IN), D_MODEL ** -0.5),
        "w_alpha": nrm(ks[6], (DEPTH, GATE_RANK, D_K), GATE_RANK ** -0.5),
        "b_alpha": nrm(ks[7], (DEPTH, D_K), 0.1),
        "conv_w": nrm(ks[8], (DEPTH, CONV_W, D_CONV), CONV_W ** -0.5),
        "conv_b": nrm(ks[9], (DEPTH, D_CONV), 0.02),
        "cn_g": 1.0 + nrm(ks[10], (DEPTH, D_CONV), 0.02),
        "cn_b": nrm(ks[11], (DEPTH, D_CONV), 0.02),
        "w_pw": nrm(ks[12], (DEPTH, D_CONV, D_CONV), D_CONV ** -0.5),
        "b_pw": nrm(ks[13], (DEPTH, D_CONV), 0.02),
        "gla_g": 1.0 + nrm(ks[14], (DEPTH, HEAD_V), 0.02),
        "w_out": nrm(ks[15], (DEPTH, D_MIX, D_MODEL), D_MIX ** -0.5),
        "final_g": 1.0 + nrm(ks[16], (D_MODEL,), 0.02),
    }


def reference(x_prompt, x_sample, cache_conv, state_gla, norm_g, w_in, w_alpha, b_alpha,
              conv_w, conv_b, cn_g, cn_b, w_pw, b_pw, gla_g, w_out, final_g):
    hp, hs = x_prompt, x_sample
    conv_p, gla_p, conv_s, gla_s = [], [], [], []
    for l in range(DEPTH):
        lw = (norm_g[l], w_in[l], w_alpha[l], b_alpha[l], conv_w[l], conv_b[l],
              cn_g[l], cn_b[l], w_pw[l], b_pw[l], gla_g[l], w_out[l])
        buf0 = jnp.zeros((BATCH, CONV_W - 1, D_CONV), hp.dtype)
        S0 = jnp.zeros((BATCH, GLA_HEADS, HEAD_K, HEAD_V), jnp.float32)
        hp, bp, Sp = mixer_layer(hp, buf0, S0, *lw)
        hs, bs, Ss = mixer_layer(hs, cache_conv[l], state_gla[l], *lw)
        conv_p.append(bp); gla_p.append(Sp); conv_s.append(bs); gla_s.append(Ss)
    y_prompt = rms_norm(hp, final_g)
    y_sample = rms_norm(hs, final_g)
    return (y_prompt, y_sample, jnp.stack(conv_p), jnp.stack(gla_p), jnp.stack(conv_s), jnp.stack(gla_s))
```

```python
from contextlib import ExitStack
import numpy as np
import concourse.bass as bass
import concourse.mybir as mybir
from concourse.bass_utils import run_bass_kernel_spmd

F32 = mybir.dt.float32
BF16 = mybir.dt.bfloat16
ALU = mybir.AluOpType
AF = mybir.ActivationFunctionType

NCORES = 8
SG = 2
NT = 17
NPOOL = 0
EPS = 1e-6
DBG_STAGE = 99
DBG_LAYERS = (0, 1)
DBG_TILES = tuple(range(17))
SYNC_SAME_ENGINE_ALL = True
SLACK_F, SLACK_C, SLACK_B, ORDER = 0, 0, 0, "bfc"
TAPS_PER_YIELD = 1
CONV_YIELD_TAPS = 8
CONV_DELAY = 0
PIPE_T, PIPE_D1, PIPE_D2 = 16, 18, 46
SAMPLE_EXTRA = 12
LAYER_GAP = 26
CACHE_ROUND = 120
B2_DELAY = 0
NPE = 25
SEM_EPOCH = 1000


class Prog:
    def __init__(self):
        self.ops = []
        self.dma_groups = {}

    WATCH = ("hn", "hT", "lrT", "spb", "NG", "Eq", "Ek", "Ed", "qt", "kt", "kdT", "kd", "v_sb", "szg", "F0", "F1",
             "F3", "Am", "xb", "sq", "catC", "pb0", "pb1", "pb2", "pb3", "pb4", "pb5", "pb6", "pb7",
             "ufull0", "ufull1", "ufull2", "ucat")

    def add(self, eng, fn, reads=(), writes=(), dma=None):
        tag = getattr(self, "tag", None)
        lw = self.__dict__.setdefault("_lw", {})
        if tag is not None:
            tile = tag[1:]
            for r in reads:
                if r in self.WATCH and r in lw and lw[r] is not None and lw[r] != tile:
                    raise RuntimeError(f"pipeline hazard: {tag} reads {r} last written by tile {lw[r]}")
        for w in writes:
            lw[w] = tag[1:] if tag is not None else None
        self.ops.append(dict(eng=eng, fn=fn, reads=tuple(reads), writes=tuple(writes), dma=dma))

    def dma_group(self, name, nsems):
        self.dma_groups[name] = [nsems, 0]

    def build(self, nc, es):
        ops = self.ops
        n = len(ops)
        deps = [set() for _ in range(n)]
        raw = [set() for _ in range(n)]
        dma_sem_of = [None] * n
        last_on_sem = {}
        for i, op in enumerate(ops):
            if op["dma"] is not None:
                g = self.dma_groups[op["dma"]]
                key = (op["dma"], g[1] % g[0])
                g[1] += 1
                dma_sem_of[i] = key
                if key in last_on_sem:
                    deps[i].add(last_on_sem[key])
                last_on_sem[key] = i
        last_w = {}
        readers = {}
        for i, op in enumerate(ops):
            for r in op["reads"]:
                if r in last_w:
                    deps[i].add(last_w[r])
                    raw[i].add(last_w[r])
            for w in op["writes"]:
                if w in last_w:
                    deps[i].add(last_w[w])
                for j in readers.get(w, ()):
                    if j != i:
                        deps[i].add(j)
            for w in op["writes"]:
                last_w[w] = i
                readers[w] = []
            for r in op["reads"]:
                readers.setdefault(r, []).append(i)
        for i, op in enumerate(ops):
            keep = set()
            for j in deps[i]:
                oj = ops[j]
                if oj["dma"] is None and op["dma"] is None and oj["eng"] == op["eng"]:
                    if op["eng"] == "pe":
                        continue
                    if (not SYNC_SAME_ENGINE_ALL) and j not in raw[i]:
                        continue
                keep.add(j)
            deps[i] = keep
        needs_inc = [False] * n
        for i in range(n):
            for j in deps[i]:
                needs_inc[j] = True
        event = [None] * n
        cnt = {}
        for i, op in enumerate(ops):
            if op["dma"] is not None:
                key = ("dma",) + dma_sem_of[i]
                cnt[key] = cnt.get(key, 0) + 16
                event[i] = (key, cnt[key])
            elif needs_inc[i]:
                tot = cnt.get(("tot", op["eng"]), 0)
                cnt[("tot", op["eng"])] = tot + 1
                key = ("eng", op["eng"], tot // SEM_EPOCH)
                cnt[key] = cnt.get(key, 0) + 1
                event[i] = (key, cnt[key])
        cnt = {k: v for k, v in cnt.items() if k[0] != "tot"}
        sem_keys = sorted(set(e[0] for e in event if e is not None), key=str)
        sems = {}
        for k in sem_keys:
            sems[k] = es.enter_context(nc.semaphore("s_" + "_".join(str(x) for x in k)))
        self.n_sems = len(sem_keys)
        self.max_sem = max(cnt.values()) if cnt else 0
        per_eng = {}
        seen = {}
        for i, op in enumerate(ops):
            e = op["eng"]
            waits = {}
            for j in deps[i]:
                k, v = event[j]
                if v > waits.get(k, 0):
                    waits[k] = v
            wl = []
            for k, v in waits.items():
                if seen.get((e, k), 0) >= v:
                    continue
                seen[(e, k)] = v
                wl.append((k, v))
            per_eng.setdefault(e, []).append((i, wl))

        def emit(engname, engobj):
            for i, wl in per_eng.get(engname, []):
                for k, v in wl:
                    engobj.wait_ge(sems[k], v)
                ins = ops[i]["fn"](engobj)
                ev = event[i]
                if ev is not None:
                    ins.then_inc(sems[ev[0]], 16 if ev[0][0] == "dma" else 1)

        with nc.Block() as block:
            @block.tensor
            def _(eng):
                emit("pe", eng)

            @block.scalar
            def _(eng):
                emit("act", eng)

            @block.vector
            def _(eng):
                emit("dve", eng)

            @block.gpsimd
            def _(eng):
                emit("pool", eng)

            @block.sync
            def _(eng):
                emit("sp", eng)


C_A, C_GL, C_ZC, C_Q, C_K, C_V, C_ZG, C_LR = 0, 512, 1024, 1536, 1792, 2048, 2560, 3072
WIN_GROUPS = [("a", 0, 512), ("gl", 512, 1024), ("zc", 1024, 1536), ("q", 1536, 1792),
              ("k", 1792, 2048), ("v", 2048, 2560), ("zg", 2560, 3072), ("lr", 3072, 3088)]


def build_nc(dbg=False):
    nc = bass.Bass("TRN2", target_bir_lowering=False)
    din = lambda name, shape: nc.dram_tensor(name, shape, F32, kind="ExternalInput").ap()
    dout = lambda name, shape: nc.dram_tensor(name, shape, F32, kind="ExternalOutput").ap()
    x_d = din("x", [NT * 128, 1024])
    cc_d = din("cc", [2, 16, 30, 512])
    sg_d = din("sg", [2, 16, 4, 64, 128])
    norm_g = din("norm_g", [2, 1024])
    w_in = din("w_in", [2, 1024, 3088])
    w_alpha = din("w_alpha", [2, 16, 256])
    b_alpha = din("b_alpha", [2, 256])
    conv_w = din("conv_w", [2, 31, 512])
    conv_b = din("conv_b", [2, 512])
    cn_g = din("cn_g", [2, 512])
    cn_b = din("cn_b", [2, 512])
    w_pw = din("w_pw", [2, 512, 512])
    b_pw = din("b_pw", [2, 512])
    gla_g = din("gla_g", [2, 128])
    w_out = din("w_out", [2, 1024, 1024])
    final_g = din("final_g", [1024])
    y_d = dout("y", [NT * 128, 1024])
    ncp_d = dout("ncp", [2, 30, 512])
    nsp_d = dout("nsp", [2, 4, 64, 128])
    ncs_d = dout("ncs", [2, 16, 30, 512])
    nss_d = dout("nss", [2, 16, 4, 64, 128])

    P = Prog()
    P.dma_group("xld", 6)
    P.dma_group("wld", 6)
    P.dma_group("pld", 4)
    P.dma_group("st", 6)
    P.dma_group("sst", 2)
    P.dma_group("sld", 2)
    P.dma_group("cld", 2)
    outs = []

    with ExitStack() as es:
        def sb(name, shape, dt):
            return es.enter_context(nc.sbuf_tensor(name, shape, dt))

        X = sb("X", [128, NT, 1024], F32)
        win = sb("win", [128, 8, 3088], BF16)
        wpw = sb("wpw", [128, 4, 512], BF16)
        wout = sb("wout", [128, 8, 1024], BF16)
        walpha = sb("walpha", [16, 256], BF16)
        fgb = sb("fgb", [128, 1024], F32)
        gcol = sb("gcol", [128, 8], F32)
        convb = sb("convb", [128, 4], F32)
        cng = sb("cng", [128, 4], F32)
        cnb = sb("cnb", [128, 4], F32)
        bpw = sb("bpw", [128, 4], F32)
        glag = sb("glag", [128, 1], F32)
        balpha = sb("balpha", [128, 2], F32)
        nb = sb("nb", [128, 2], F32)
        convw = sb("convw", [128, 4, 31], F32)
        craw = sb("craw", [31, 512], F32)
        ident = sb("ident", [128, 128], BF16)
        identf = sb("identf", [128, 128], F32)
        tri = sb("tri", [128, 128], BF16)
        smask = sb("smask", [128, 128], BF16)
        blk64 = sb("blk64", [128, 128], BF16)
        ones128 = sb("ones128", [128, 128], BF16)
        onesf = sb("onesf", [128, 128], F32)
        epscol = sb("epscol", [128, 1], F32)
        halfcol = sb("halfcol", [128, 1], F32)
        selm = sb("selm", [128, 16], F32)
        ss = sb("ss", [128, 64], F32)
        rstd = sb("rstd", [128, 64], F32)
        nend = sb("nend", [128, 2], F32)
        hn = sb("hn", [128, 1024], BF16)
        hT = sb("hT", [128, 8, 128], BF16)
        lrT = sb("lrT", [16, 128], BF16)
        spb = sb("spb", [128, 2, 128], F32)
        NG = sb("NG", [128, 2, 128], F32)
        Eq = sb("Eq", [128, 2, 128], F32)
        Ek = sb("Ek", [128, 2, 128], F32)
        Ed = sb("Ed", [128, 2, 128], F32)
        qt = sb("qt", [128, 2, 128], BF16)
        kt = sb("kt", [128, 2, 128], BF16)
        kdT = sb("kdT", [128, 2, 128], BF16)
        kd = sb("kd", [128, 256], BF16)
        v_sb = sb("v_sb", [128, 512], BF16)
        szg = sb("szg", [128, 4, 128], BF16)
        AmB = sb("AmB", [128, 512], BF16)
        Am = AmB[:].rearrange("q (c t) -> q c t", t=128)
        osq = AmB
        kdm = AmB[:].rearrange("q (s c) -> q s c", c=256)
        S = sb("S", [128, 2, 128], F32)
        Sbf = sb("Sbf", [128, 2, 128], BF16)
        F0 = sb("F0", [128, 4, 128], F32)
        F1 = sb("F1", [128, 4, 128], F32)
        F3 = sb("F3", [128, 512], F32)
        cf = F3[:, 0:128]
        S0 = F1[:].rearrange("q (p s) v -> q p s v", s=SG)
        accD2 = [sb(f"accD{s}", [128, 4, 128], F32) for s in range(2)]
        ufull = [sb(f"ufull{s}", [128, 4, 158], BF16) for s in range(3)]
        szc2 = [sb(f"szc{s}", [128, 4, 128], BF16) for s in range(3)]
        diag2 = sb("diag2", [128, 4, NPE, 64], BF16)
        I2 = sb("I2", [128, 64], BF16)
        xb = sb("xb", [128, 4, 128], BF16)
        sq = sb("sq", [128, 4, 128], BF16)
        cvs = xb
        catO3 = [sb(f"catO{s}", [128, 4, 128], BF16) for s in range(3)]
        catC = sb("catC", [128, 4, 128], BF16)
        S0bf = sb("S0bf", [128, 2, SG, 128], BF16)
        ucat = sb("ucat", [128, 4, 16, 38], BF16)
        cstage = sb("cstage", [120, 4, 512], BF16)

        psum = es.enter_context(nc.psum_tensor("psum", [128, 8 * 512], F32))
        PB = [psum[:, b * 512:(b + 1) * 512] for b in range(8)]
        psum_b = psum.bitcast(BF16)
        PBb = [psum_b[:, b * 1024:(b + 1) * 1024] for b in range(8)]

        def pb4(b):
            return PB[b].rearrange("q (c t) -> q c t", t=128)

        P.add("pool", lambda e: e.memset(cf, 0.0), writes=["F3"])
        P.add("pool", lambda e: e.affine_select(out=cf, in_=cf, pattern=[[-1, 128]],
                                                 compare_op=ALU.not_equal, fill=1.0, base=0,
                                                 channel_multiplier=1), reads=["F3"], writes=["F3"])
        P.add("dve", lambda e: e.tensor_copy(out=identf[:], in_=cf), reads=["F3"], writes=["identf"])
        P.add("dve", lambda e: e.tensor_copy(out=ident[:], in_=cf), reads=["F3"], writes=["ident"])
        P.add("dve", lambda e: e.tensor_tensor(out=I2[:], in0=ident[:, 0:64], in1=ident[:, 64:128], op=ALU.add),
              reads=["ident"], writes=["I2"])
        P.add("pool", lambda e: e.memset(cf, 1.0), reads=["F3"], writes=["F3"])
        P.add("pool", lambda e: e.affine_select(out=cf, in_=cf, pattern=[[1, 128]],
                                                 compare_op=ALU.is_ge, fill=0.0, base=0,
                                                 channel_multiplier=-1), reads=["F3"], writes=["F3"])
        P.add("dve", lambda e: e.tensor_copy(out=tri[:], in_=cf), reads=["F3"], writes=["tri"])
        cf3 = cf.rearrange("q (a b) -> q a b", b=8)
        P.add("pool", lambda e: e.affine_select(out=cf3, in_=cf3, pattern=[[-8, 16], [0, 8]],
                                                 compare_op=ALU.is_ge, fill=0.0, base=0,
                                                 channel_multiplier=1), reads=["F3"], writes=["F3"])
        P.add("pool", lambda e: e.affine_select(out=cf3, in_=cf3, pattern=[[8, 16], [0, 8]],
                                                 compare_op=ALU.is_ge, fill=0.0, base=7,
                                                 channel_multiplier=-1), reads=["F3"], writes=["F3"])
        P.add("dve", lambda e: e.tensor_copy(out=smask[:], in_=cf), reads=["F3"], writes=["smask"])
        P.add("pool", lambda e: e.memset(selm[:], 1.0), writes=["selm"])
        P.add("pool", lambda e: e.affine_select(out=selm[:], in_=selm[:], pattern=[[-8, 16]],
                                                 compare_op=ALU.is_ge, fill=0.0, base=0,
                                                 channel_multiplier=1), reads=["selm"], writes=["selm"])
        P.add("pool", lambda e: e.affine_select(out=selm[:], in_=selm[:], pattern=[[8, 16]],
                                                 compare_op=ALU.is_ge, fill=0.0, base=7,
                                                 channel_multiplier=-1), reads=["selm"], writes=["selm"])
        P.add("dve", lambda e: e.memset(blk64[:], 0.0), writes=["blk64"])
        P.add("dve", lambda e: e.memset(blk64[0:64, 0:64], 1.0 / 64), writes=["blk64"])
        P.add("dve", lambda e: e.memset(blk64[64:128, 64:128], 1.0 / 64), writes=["blk64"])
        P.add("dve", lambda e: e.memset(ones128[:], 1.0 / 128), writes=["ones128"])
        P.add("dve", lambda e: e.memset(onesf[:], 1.0), writes=["onesf"])
        P.add("dve", lambda e: e.memset(epscol[:], EPS), writes=["epscol"])
        P.add("dve", lambda e: e.memset(halfcol[:], 0.5), writes=["halfcol"])
        P.add("dve", lambda e: e.memset(ss[:], 0.0), writes=["ss"])
        for s in range(3):
            P.add("dve", lambda e, s=s: e.memset(ufull[s][:], 0.0), writes=[f"ufull{s}"])

        def load_x():
            for i in [NT - 1] + list(range(NT - 1)):
                P.add("sp", lambda e, i=i: e.dma_start(out=X[:, i, :], in_=x_d[i * 128:(i + 1) * 128, :]),
                      writes=[f"X{i}"], dma="xld")
            P.add("sp", lambda e: e.dma_start(out=fgb[:], in_=final_g.partition_broadcast(128)),
                  writes=["fgb"], dma="pld")

        def small_col(dst, src_ap, pat, key, **kw):
            P.add("sp", lambda e: e.dma_start(out=dst, in_=src_ap.rearrange(pat, **kw),
                                              allow_slow_non_contiguous=True),
                  writes=[key], dma="pld")

        def setup_a(L):
            small_col(gcol[:], norm_g[L], "(k q) -> q k", "gcol", q=128)
            small_col(balpha[:], b_alpha[L], "(k q) -> q k", "balpha", q=128)
            small_col(glag[:], gla_g[L], "(q o) -> q o", "glag", o=1)
            P.add("dve", lambda e: e.tensor_scalar(out=nb[:], in0=balpha[:], scalar1=-1.0, scalar2=None, op0=ALU.mult),
                  reads=["balpha"], writes=["nb"])
            win_v = w_in[L].rearrange("(k q) e -> q k e", q=128)
            order = ["lr", "q", "k", "v", "zg", "a", "gl", "zc"]
            grp = {g: (c0, c1) for (g, c0, c1) in WIN_GROUPS}
            for idx, g in enumerate(order):
                c0, c1 = grp[g]
                P.add("pool", lambda e, c0=c0, c1=c1: e.dma_start(out=win[:, :, c0:c1], in_=win_v[:, :, c0:c1]),
                      writes=[f"win_{g}"], dma="wld")
                if idx == 0:
                    P.add("pool", lambda e: e.dma_start(out=walpha[:], in_=w_alpha[L]), writes=["walpha"], dma="wld")
            for g in order:
                c0, c1 = grp[g]
                P.add("dve", lambda e, c0=c0, c1=c1: e.tensor_tensor(
                    out=win[:, :, c0:c1], in0=win[:, :, c0:c1],
                    in1=gcol[:].unsqueeze(2).to_broadcast([128, 8, c1 - c0]), op=ALU.mult),
                    reads=[f"win_{g}", "gcol"], writes=[f"win_{g}"])

        def setup_b(L):
            small_col(convb[:], conv_b[L], "(k q) -> q k", "convb", q=128)
            P.add("sp", lambda e: e.dma_start(out=craw[:], in_=conv_w[L]), writes=["craw"], dma="pld")
            cw_ps = PB[7][:, 0:124].rearrange("q (c j) -> q c j", j=31)
            for c in range(4):
                P.add("pe", lambda e, c=c: e.transpose(out=cw_ps[:, c, :], in_=craw[0:31, c * 128:(c + 1) * 128],
                                                        identity=identf[0:31, 0:31]),
                      reads=["craw", "identf"], writes=["pb7"])
            P.add("act", lambda e: e.copy(out=convw[:], in_=cw_ps), reads=["pb7"], writes=["convw"])
            for c in range(4):
                P.add("dve", lambda e, c=c: e.tensor_tensor(
                    out=diag2[:, c], in0=I2[:].unsqueeze(1).to_broadcast([128, NPE, 64]),
                    in1=convw[:, c, 0:NPE].unsqueeze(2).to_broadcast([128, NPE, 64]), op=ALU.mult),
                    reads=["I2", "convw"], writes=["diag"])

        def setup_c(L):
            small_col(cng[:], cn_g[L], "(k q) -> q k", "cng", q=128)
            small_col(cnb[:], cn_b[L], "(k q) -> q k", "cnb", q=128)
            small_col(bpw[:], b_pw[L], "(k q) -> q k", "bpw", q=128)
            P.add("pool", lambda e: e.dma_start(out=wpw[:], in_=w_pw[L].rearrange("(k q) e -> q k e", q=128)),
                  writes=["wpw"], dma="wld")
            P.add("pool", lambda e: e.dma_start(out=wout[:], in_=w_out[L].rearrange("(k q) e -> q k e", q=128)),
                  writes=["wout"], dma="wld")

        def cache_dma(L):
            for g in range(4):
                P.add("pool", lambda e, g=g: e.dma_start(
                    out=cstage[:, g, :], in_=cc_d[L, 4 * g:4 * g + 4].rearrange("b r c -> (b r) c")),
                    writes=["cstage"], dma="cld")
            P.add("sp", lambda e: e.dma_start(out=ncs_d[L, :, 0:22, :], in_=cc_d[L, :, 8:30, :]),
                  writes=[f"o_ncs_old{L}"], dma="st")
            outs.append(f"o_ncs_old{L}")

        def setup_cache(L):
            for g in range(4):
                tp = PBb[7][:, 0:480].rearrange("q (c r) -> q c r", r=120)
                for c in range(4):
                    P.add("pe", lambda e, g=g, c=c, tp=tp: e.transpose(
                        out=tp[:, c, :], in_=cstage[0:120, g, c * 128:(c + 1) * 128], identity=ident[0:120, 0:120]),
                        reads=["cstage", "ident"], writes=["pb7"])
                yield
                P.add("act", lambda e, g=g, tp=tp: e.copy(
                    out=ucat[:, :, 4 * g:4 * g + 4, 0:30],
                    in_=tp.rearrange("q c (b r) -> q c b r", r=30)),
                    reads=["pb7"], writes=["ucat"])
                yield

        def front(L, i):
            P.tag = ("t", L, i)
            sample = (i == NT - 1)
            Xi = X[:, i, :]
            xk = f"X{i}"
            col = L * NT + i
            gi = L * NT + i
            gp = L * NT + (i + 1) % NT
            szc = szc2[gp % 3]
            szck = f"szc{gp % 3}"
            catO = catO3[gp % 3]
            catk = f"catT{gp % 3}"
            P.add("act", lambda e: e.activation(out=hn[:], in_=Xi, func=AF.Square, accum_out=ss[:, col:col + 1]),
                  reads=[xk, "ss"], writes=["hn", f"ss{col}"])
            P.add("act", lambda e: e.activation(out=rstd[:, col:col + 1], in_=ss[:, col:col + 1], func=AF.Ln,
                                                 bias=epscol[:, 0:1], scale=1.0 / 1024),
                  reads=[f"ss{col}", "epscol"], writes=[f"rs{col}"])
            P.add("act", lambda e: e.activation(out=rstd[:, col:col + 1], in_=rstd[:, col:col + 1], func=AF.Exp,
                                                 scale=-0.5),
                  reads=[f"rs{col}"], writes=[f"rs{col}"])
            P.add("act", lambda e: e.activation(out=hn[:], in_=Xi, func=AF.Copy, scale=rstd[:, col:col + 1]),
                  reads=[xk, f"rs{col}"], writes=["hn"])
            hT_ps = PBb[0].rearrange("q (k t) -> q k t", t=128)
            for k in range(8):
                P.add("pe", lambda e, k=k: e.transpose(out=hT_ps[:, k, :], in_=hn[:, k * 128:(k + 1) * 128],
                                                        identity=ident[:]),
                      reads=["hn", "ident"], writes=["pb0"])
            yield
            P.add("act", lambda e: e.copy(out=hT[:], in_=hT_ps), reads=["pb0"], writes=["hT"])
            yield

            def proj_fm(bank, c0, nchunks, *wkeys):
                o = pb4(bank)
                for c in range(nchunks):
                    for k in range(8):
                        P.add("pe", lambda e, c=c, k=k: e.matmul(
                            o[:, c, :], lhsT=win[:, k, c0 + c * 128:c0 + (c + 1) * 128], rhs=hT[:, k, :],
                            start=(k == 0), stop=(k == 7)),
                            reads=["hT"] + list(wkeys), writes=[f"pb{bank}"])

            for k in range(8):
                P.add("pe", lambda e, k=k: e.matmul(PB[1][0:16, 0:128], lhsT=win[:, k, C_LR:C_LR + 16], rhs=hT[:, k, :],
                                                     start=(k == 0), stop=(k == 7)),
                      reads=["hT", "win_lr"], writes=["pb1"])
            P.add("act", lambda e: e.copy(out=lrT[:], in_=PB[1][0:16, 0:128]), reads=["pb1"], writes=["lrT"])
            z_ps = PB[1][:, 128:384].rearrange("q (c t) -> q c t", t=128)
            for p in range(2):
                P.add("pe", lambda e, p=p: e.matmul(z_ps[:, p, :], lhsT=walpha[0:16, p * 128:(p + 1) * 128],
                                                     rhs=lrT[0:16, :], start=True, stop=True),
                      reads=["lrT", "walpha"], writes=["pb1"])
            yield
            for p in range(2):
                P.add("act", lambda e, p=p: e.activation(out=spb[:, p, :], in_=z_ps[:, p, :], func=AF.Exp,
                                                          bias=nb[:, p:p + 1], scale=-1.0),
                      reads=["pb1", "nb"], writes=["spb"])
            proj_fm(2, C_Q, 4, "win_q", "win_k")
            yield
            P.add("act", lambda e: e.activation(out=spb[:], in_=spb[:], func=AF.Ln, bias=1.0, scale=1.0),
                  reads=["spb"], writes=["spb"])
            for k in range(8):
                P.add("pe", lambda e, k=k: e.matmul(PB[0], lhsT=hT[:, k, :], rhs=win[:, k, C_V:C_V + 512],
                                                     start=(k == 0), stop=(k == 7)),
                      reads=["hT", "win_v"], writes=["pb0"])
            yield
            for p in range(2):
                P.add("dve", lambda e, p=p: e.tensor_tensor_scan(out=NG[:, p, :], data0=onesf[:], data1=spb[:, p, :],
                                                                  initial=0.0, op0=ALU.mult, op1=ALU.add),
                      reads=["spb", "onesf"], writes=["NG"])
            proj_fm(1, C_ZG, 4, "win_zg")
            yield
            P.add("act", lambda e: e.copy(out=v_sb[:], in_=PB[0]), reads=["pb0"], writes=["v_sb"])
            if not sample:
                P.add("act", lambda e: e.activation(out=Eq[:], in_=NG[:], func=AF.Exp, scale=-1.0 / 16),
                      reads=["NG"], writes=["Eq"])
                P.add("act", lambda e: e.activation(out=Ek[:], in_=NG[:], func=AF.Exp, scale=1.0 / 16),
                      reads=["NG"], writes=["Ek"])
                P.add("dve", lambda e: e.tensor_scalar(out=nend[:].unsqueeze(2), in0=NG[:, :, 127:128], scalar1=-1.0 / 16,
                                                        scalar2=None, op0=ALU.mult),
                      reads=["NG"], writes=["nend"])
                for p in range(2):
                    P.add("act", lambda e, p=p: e.activation(out=Ed[:, p, :], in_=NG[:, p, :], func=AF.Exp,
                                                              bias=nend[:, p:p + 1], scale=1.0 / 16),
                          reads=["NG", "nend"], writes=["Ed"])
            else:
                for p in range(2):
                    NGp = NG[:, p, :].rearrange("q (a b) -> q a b", b=8)
                    Dp = spb[:, p, :].rearrange("q (a b) -> q a b", b=8)
                    D2p = Ed[:, p, :].rearrange("q (a b) -> q a b", b=8)
                    P.add("dve", lambda e, NGp=NGp, Dp=Dp: e.tensor_copy(out=Dp[:, 0:1, :], in_=NGp[:, 0:1, :]),
                          reads=["NG"], writes=["spb"])
                    P.add("dve", lambda e, NGp=NGp, Dp=Dp: e.tensor_tensor(
                        out=Dp[:, 1:16, :], in0=NGp[:, 1:16, :],
                        in1=NGp[:, 0:15, 7:8].to_broadcast([128, 15, 8]), op=ALU.subtract),
                        reads=["NG"], writes=["spb"])
                    P.add("dve", lambda e, NGp=NGp, D2p=D2p: e.tensor_tensor(
                        out=D2p, in0=NGp, in1=NGp[:, :, 7:8].to_broadcast([128, 16, 8]), op=ALU.subtract),
                        reads=["NG"], writes=["Ed"])
                P.add("act", lambda e: e.activation(out=Eq[:], in_=spb[:], func=AF.Exp, scale=-1.0 / 16),
                      reads=["spb"], writes=["Eq"])
                P.add("act", lambda e: e.activation(out=Ek[:], in_=spb[:], func=AF.Exp, scale=1.0 / 16),
                      reads=["spb"], writes=["Ek"])
                P.add("act", lambda e: e.activation(out=Ed[:], in_=Ed[:], func=AF.Exp, scale=1.0 / 16),
                      reads=["Ed"], writes=["Ed"])
            proj_fm(3, C_A, 4, "win_a")
            yield
            qk = pb4(2)
            P.add("dve", lambda e: e.scalar_tensor_tensor(out=qt[:], in0=qk[:, 0:2, :], scalar=0.125, in1=Eq[:],
                                                           op0=ALU.mult, op1=ALU.mult),
                  reads=["pb2", "Eq"], writes=["qt"])
            P.add("dve", lambda e: e.tensor_tensor(out=kt[:], in0=qk[:, 2:4, :], in1=Ek[:], op=ALU.mult),
                  reads=["pb2", "Ek"], writes=["kt"])
            P.add("dve", lambda e: e.tensor_tensor(out=kdT[:], in0=qk[:, 2:4, :], in1=Ed[:], op=ALU.mult),
                  reads=["pb2", "Ed"], writes=["kdT"])
            proj_fm(0, C_GL, 4, "win_gl")
            yield
            P.add("act", lambda e: e.activation(out=szg[:], in_=pb4(1), func=AF.Silu), reads=["pb1"], writes=["szg"])
            yield
            proj_fm(1, C_ZC, 4, "win_zc")
            P.add("act", lambda e: e.activation(out=F0[:], in_=pb4(0), func=AF.Tanh, scale=0.5), reads=["pb0"], writes=["F0"])
            P.add("act", lambda e: e.activation(out=F0[:], in_=F0[:], func=AF.Identity, bias=halfcol[:, 0:1], scale=0.5),
                  reads=["F0", "halfcol"], writes=["F0"])
            yield
            uf = ufull[gi % 3]
            ufk = f"ufull{gi % 3}"
            ufo = ufull[(gi + 1) % 3]
            ufok = f"ufull{(gi + 1) % 3}"
            if i == 0:
                P.add("dve", lambda e: e.memset(uf[:, :, 0:30], 0.0), writes=[ufk])
            if not sample:
                P.add("dve", lambda e: e.tensor_tensor(out=uf[:, :, 30:158], in0=pb4(3), in1=F0[:], op=ALU.mult),
                      reads=["pb3", "F0"], writes=[ufk])
                if i < NT - 2:
                    P.add("act", lambda e: e.copy(out=ufo[:, :, 0:30], in_=uf[:, :, 128:158]),
                          reads=[ufk], writes=[ufok])
            else:
                P.add("dve", lambda e: e.tensor_tensor(
                    out=ucat[:, :, :, 30:38], in0=pb4(3).rearrange("q c (b r) -> q c b r", r=8),
                    in1=F0[:].rearrange("q c (b r) -> q c b r", r=8), op=ALU.mult),
                    reads=["pb3", "F0"], writes=["ucat"])
            need_u32 = sample or (i == NT - 2)
            if need_u32:
                P.add("dve", lambda e: e.tensor_tensor(out=F0[:], in0=pb4(3), in1=F0[:], op=ALU.mult),
                      reads=["pb3", "F0"], writes=["F0"])
            P.add("act", lambda e: e.activation(out=szc[:], in_=pb4(1), func=AF.Silu), reads=["pb1"], writes=[szck])
            yield
            if need_u32:
                uT_ps = PB[2]
                for c in range(4):
                    P.add("pe", lambda e, c=c: e.transpose(out=uT_ps[:, c * 128:(c + 1) * 128], in_=F0[:, c, :],
                                                            identity=identf[:]),
                          reads=["F0", "identf"], writes=["pb2"])
                P.add("act", lambda e: e.copy(out=F3[:], in_=uT_ps), reads=["pb2"], writes=["F3"])
                if sample:
                    for b in range(16):
                        P.add("sp", lambda e, b=b: e.dma_start(out=ncs_d[L, b, 22:30, :], in_=F3[b * 8:(b + 1) * 8, :]),
                              reads=["F3"], writes=[f"o_ncs{L}_{b}"], dma="st")
                        outs.append(f"o_ncs{L}_{b}")
                else:
                    P.add("sp", lambda e: e.dma_start(out=ncp_d[L], in_=F3[98:128, :]),
                          reads=["F3"], writes=[f"o_ncp{L}"], dma="st")
                    outs.append(f"o_ncp{L}")
                yield

            kd_ps = PBb[4][:, 0:256]
            for p in range(2):
                P.add("pe", lambda e, p=p: e.transpose(out=kd_ps[:, p * 128:(p + 1) * 128], in_=kdT[:, p, :],
                                                        identity=ident[:]),
                      reads=["kdT", "ident"], writes=["pb4"])
            yield
            P.add("act", lambda e: e.copy(out=kd[:], in_=kd_ps), reads=["pb4"], writes=["kd"])
            yield
            A_ps = [pb4(4), pb4(5)]
            for h in (0, 2, 1, 3):
                p, hl = h // 2, h % 2
                P.add("pe", lambda e, h=h, p=p, hl=hl: e.matmul(
                    A_ps[hl][:, p, :], lhsT=kt[hl * 64:(hl + 1) * 64, p, :], rhs=qt[hl * 64:(hl + 1) * 64, p, :],
                    start=True, stop=True),
                    reads=["kt", "qt"], writes=[f"pb{4 if hl == 0 else 5}"])
            yield
            mk = smask if sample else tri
            mkk = "smask" if sample else "tri"
            Am_v = Am.rearrange("q (p hl) t -> q hl p t", hl=2)
            for hl in range(2):
                P.add("dve", lambda e, hl=hl: e.tensor_tensor(
                    out=Am_v[:, hl], in0=A_ps[hl][:, 0:2, :],
                    in1=mk[:].unsqueeze(1).to_broadcast([128, 2, 128]), op=ALU.mult),
                    reads=[f"pb{4 if hl == 0 else 5}", mkk], writes=["Am"])
            yield
            o_ps = pb4(4)
            first = (i == 0)
            for h in range(4):
                p, hl = h // 2, h % 2
                use_S = (not sample) and (not first)
                if use_S:
                    P.add("pe", lambda e, h=h, p=p, hl=hl: e.matmul(
                        o_ps[:, h, :], lhsT=Sbf[hl * 64:(hl + 1) * 64, p, :], rhs=qt[hl * 64:(hl + 1) * 64, p, :],
                        start=True, stop=False),
                        reads=["Sbf", "qt"], writes=["pb4"])
                P.add("pe", lambda e, h=h, use_S=use_S: e.matmul(
                    o_ps[:, h, :], lhsT=v_sb[:, h * 128:(h + 1) * 128], rhs=Am[:, h, :],
                    start=(not use_S), stop=True),
                    reads=["v_sb", "Am"], writes=["pb4"])
            yield
            if not sample:
                Pp = PB[5][:, 0:256].rearrange("q (c v) -> q c v", v=128)
                for h in range(4):
                    p, hl = h // 2, h % 2
                    P.add("pe", lambda e, h=h, p=p, hl=hl: e.matmul(
                        Pp[hl * 64:(hl + 1) * 64, p, :], lhsT=kd[:, h * 64:(h + 1) * 64],
                        rhs=v_sb[:, h * 128:(h + 1) * 128], start=True, stop=True),
                        reads=["kd", "v_sb"], writes=["pb5"])
                yield
                if first:
                    P.add("dve", lambda e: e.tensor_copy(out=S[:], in_=Pp), reads=["pb5"], writes=["S"])
                else:
                    P.add("dve", lambda e: e.tensor_tensor(out=S[:], in0=S[:],
                                                            in1=Eq[:, :, 127:128].to_broadcast([128, 2, 128]),
                                                            op=ALU.mult),
                          reads=["S", "Eq"], writes=["S"])
                    P.add("dve", lambda e: e.tensor_tensor(out=S[:], in0=S[:], in1=Pp, op=ALU.add),
                          reads=["S", "pb5"], writes=["S"])
                if i < NT - 2:
                    P.add("act", lambda e: e.copy(out=Sbf[:], in_=S[:]), reads=["S"], writes=["Sbf"])
                else:
                    P.add("sp", lambda e: e.dma_start(
                        out=nsp_d[L].rearrange("(p hl) d v -> (hl d) p v", hl=2), in_=S[:]),
                        reads=["S"], writes=[f"o_nsp{L}"], dma="st")
                    outs.append(f"o_nsp{L}")
                osrc, osk = o_ps, ["pb4"]
            else:
                oi_ps = [pb4(2), pb4(3)]
                oik = ["pb2", "pb3"]
                Eq4 = Eq[:].rearrange("q p (b r) -> q p b r", r=8)
                sview = "s hl d v -> (hl d) s v"
                for g in range(16 // SG):
                    for p in range(2):
                        P.add("sp", lambda e, g=g, p=p: e.dma_start(
                            out=S0[:, p], in_=sg_d[L, SG * g:SG * g + SG, 2 * p:2 * p + 2].rearrange(sview)),
                            writes=["F1"], dma="sld")
                    P.add("act", lambda e: e.copy(out=S0bf[:], in_=S0), reads=["F1"], writes=["S0bf"])
                    for s in range(SG):
                        seq = SG * g + s
                        P.add("dve", lambda e, s=s, seq=seq: e.tensor_scalar(
                            out=kdm[:, s, :], in0=kd[:], scalar1=selm[:, seq:seq + 1], scalar2=None, op0=ALU.mult),
                            reads=["kd", "selm"], writes=["Am"])
                    Pp = PB[5][:, 0:2 * SG * 128].rearrange("q (p s v) -> q p s v", s=SG, v=128)
                    for s in range(SG):
                        seq = SG * g + s
                        for h in (0, 2, 1, 3):
                            p, hl = h // 2, h % 2
                            P.add("pe", lambda e, s=s, seq=seq, h=h, p=p, hl=hl: e.matmul(
                                oi_ps[hl][:, p, seq * 8:(seq + 1) * 8], lhsT=S0bf[hl * 64:(hl + 1) * 64, p, s, :],
                                rhs=qt[hl * 64:(hl + 1) * 64, p, seq * 8:(seq + 1) * 8], start=True, stop=True),
                                reads=["S0bf", "qt"], writes=[oik[hl]])
                        for h in range(4):
                            p, hl = h // 2, h % 2
                            P.add("pe", lambda e, s=s, h=h, p=p, hl=hl: e.matmul(
                                Pp[hl * 64:(hl + 1) * 64, p, s, :], lhsT=kdm[:, s, h * 64:(h + 1) * 64],
                                rhs=v_sb[:, h * 128:(h + 1) * 128], start=True, stop=True),
                                reads=["Am", "v_sb"], writes=["pb5"])
                    P.add("dve", lambda e, g=g: e.tensor_tensor(
                        out=S0, in0=S0, in1=Eq4[:, :, SG * g:SG * g + SG, 7:8].to_broadcast([128, 2, SG, 128]),
                        op=ALU.mult), reads=["F1", "Eq", "S0bf"], writes=["F1"])
                    P.add("dve", lambda e: e.tensor_tensor(out=S0, in0=S0, in1=Pp, op=ALU.add),
                          reads=["F1", "pb5"], writes=["F1"])
                    for p in range(2):
                        P.add("sp", lambda e, g=g, p=p: e.dma_start(
                            out=nss_d[L, SG * g:SG * g + SG, 2 * p:2 * p + 2].rearrange(sview), in_=S0[:, p]),
                            reads=["F1"], writes=[f"o_nss{L}_{g}_{p}"], dma="sst")
                        outs.append(f"o_nss{L}_{g}_{p}")
                    yield
                F3_v = F3[:].rearrange("q (p hl t) -> q hl p t", hl=2, t=128)
                for hl in range(2):
                    P.add("act", lambda e, hl=hl: e.copy(out=F3_v[:, hl], in_=oi_ps[hl][:, 0:2, :]),
                          reads=[oik[hl]], writes=["F3"])
                osum = F3[:].rearrange("q (c t) -> q c t", t=128)
                P.add("dve", lambda e: e.tensor_tensor(out=osum, in0=o_ps, in1=osum, op=ALU.add),
                      reads=["pb4", "F3"], writes=["F3"])
                osrc, osk = osum, ["F3"]
            yield
            P.add("act", lambda e: e.activation(out=osq[:].rearrange("q (c t) -> q c t", t=128), in_=osrc, func=AF.Square),
                  reads=osk, writes=["Am"])
            yield
            P.add("pe", lambda e: e.matmul(PB[5], lhsT=ones128[:], rhs=osq[:], start=True, stop=True),
                  reads=["Am", "ones128"], writes=["pb5"])
            yield
            P.add("act", lambda e: e.activation(out=F1[:], in_=pb4(5), func=AF.Ln, bias=epscol[:, 0:1], scale=1.0),
                  reads=["pb5", "epscol"], writes=["F1"])
            P.add("act", lambda e: e.activation(out=F1[:], in_=F1[:], func=AF.Exp, scale=-0.5),
                  reads=["F1"], writes=["F1"])
            yield
            P.add("dve", lambda e: e.tensor_tensor(out=F1[:], in0=osrc, in1=F1[:], op=ALU.mult),
                  reads=osk + ["F1"], writes=["F1"])
            P.add("dve", lambda e: e.scalar_tensor_tensor(out=catO[:], in0=F1[:], scalar=glag[:, 0:1], in1=szg[:],
                                                           op0=ALU.mult, op1=ALU.mult),
                  reads=["F1", "glag", "szg"], writes=[catk + "_o"])

        def back1(L, i):
            sample = (i == NT - 1)
            P.tag = ("c", L, i)
            gi = L * NT + i
            gp = L * NT + (i + 1) % NT
            uf = ufull[gi % 3]
            srck = "ucat" if sample else f"ufull{gi % 3}"
            accD = accD2[gp % 2]
            ak = f"acc{gp % 2}_"

            def ush(c, j):
                if sample:
                    return ucat[:, c, :, j:j + 8]
                return uf[:, c, j:j + 128]

            def accv(c):
                if sample:
                    return accD[:, c, :].rearrange("q (b r) -> q b r", r=8)
                return accD[:, c, :]

            cv_ps = pb4(7)
            for _ in range(CONV_DELAY):
                yield
            for c in range(4):
                for j in range(NPE):
                    for hl in range(2):
                        r0, r1 = hl * 64, (hl + 1) * 64
                        P.add("pe", lambda e, c=c, j=j, r0=r0, r1=r1: e.matmul(
                            cv_ps[r0:r1, c, :], lhsT=diag2[r0:r1, c, j, :], rhs=ush(c, j)[r0:r1],
                            start=(j == 0), stop=(j == NPE - 1)),
                            reads=[srck, "diag"], writes=["pb7"])
                    if j % CONV_YIELD_TAPS == CONV_YIELD_TAPS - 1:
                        yield
                yield
            for c in range(4):
                P.add("act", lambda e, c=c: e.activation(out=accD[:, c, :], in_=cv_ps[:, c, :], func=AF.Identity,
                                                          bias=convb[:, c:c + 1], scale=1.0),
                      reads=["pb7", "convb"], writes=[f"{ak}{c}"])
            yield
            for j in range(NPE, 31):
                for c in range(4):
                    P.add("dve", lambda e, c=c, j=j: e.scalar_tensor_tensor(
                        out=accv(c), in0=ush(c, j), scalar=convw[:, c, j:j + 1], in1=accv(c),
                        op0=ALU.mult, op1=ALU.add),
                        reads=[srck, "convw", f"{ak}{c}"], writes=[f"{ak}{c}"])
                if (j - NPE) % TAPS_PER_YIELD == TAPS_PER_YIELD - 1:
                    yield

        def back2(L, i):
            last_layer = (L == 1)
            P.tag = ("b", L, i)
            Xi = X[:, i, :]
            xk = f"X{i}"
            gp = L * NT + (i + 1) % NT
            szc = szc2[gp % 3]
            szck = f"szc{gp % 3}"
            catO = catO3[gp % 3]
            catk = f"catT{gp % 3}"
            xf = accD2[gp % 2]
            acck = [f"acc{gp % 2}_{c}" for c in range(4)]
            for _ in range(B2_DELAY):
                yield
            P.add("act", lambda e: e.copy(out=xb[:], in_=xf[:]), reads=acck, writes=["xb"])
            yield
            mean_ps = pb4(6)
            for c in range(4):
                P.add("pe", lambda e, c=c: e.matmul(mean_ps[:, c, :], lhsT=blk64[:], rhs=xb[:, c, :], start=True, stop=True),
                      reads=["xb", "blk64"], writes=["pb6"])
            yield
            P.add("dve", lambda e: e.tensor_tensor(out=xf[:], in0=xf[:], in1=mean_ps, op=ALU.subtract),
                  reads=acck + ["pb6"], writes=acck)
            yield
            P.add("act", lambda e: e.activation(out=sq[:], in_=xf[:], func=AF.Square), reads=acck, writes=["sq"])
            yield
            var_ps = pb4(6)
            for c in range(4):
                P.add("pe", lambda e, c=c: e.matmul(var_ps[:, c, :], lhsT=blk64[:], rhs=sq[:, c, :], start=True, stop=True),
                      reads=["sq", "blk64"], writes=["pb6"])
            yield
            P.add("act", lambda e: e.activation(out=var_ps, in_=var_ps, func=AF.Ln, bias=epscol[:, 0:1], scale=1.0),
                  reads=["pb6", "epscol"], writes=["pb6"])
            P.add("act", lambda e: e.activation(out=var_ps, in_=var_ps, func=AF.Exp, scale=-0.5),
                  reads=["pb6"], writes=["pb6"])
            yield
            P.add("dve", lambda e: e.tensor_tensor(out=xf[:], in0=xf[:], in1=var_ps, op=ALU.mult),
                  reads=acck + ["pb6"], writes=acck)
            yield
            for c in range(4):
                P.add("act", lambda e, c=c: e.activation(out=cvs[:, c, :], in_=xf[:, c, :], func=AF.Silu,
                                                          bias=cnb[:, c:c + 1], scale=cng[:, c:c + 1]),
                      reads=acck + ["cng", "cnb"], writes=["xb"])
            yield
            pw_ps = pb4(6)
            for oc in range(4):
                for k in range(4):
                    P.add("pe", lambda e, oc=oc, k=k: e.matmul(
                        pw_ps[:, oc, :], lhsT=wpw[:, k, oc * 128:(oc + 1) * 128], rhs=cvs[:, k, :],
                        start=(k == 0), stop=(k == 3)),
                        reads=["xb", "wpw"], writes=["pb6"])
            yield
            for oc in range(4):
                P.add("dve", lambda e, oc=oc: e.scalar_tensor_tensor(
                    out=catC[:, oc, :], in0=pw_ps[:, oc, :], scalar=bpw[:, oc:oc + 1], in1=szc[:, oc, :],
                    op0=ALU.add, op1=ALU.mult),
                    reads=["pb6", "bpw", szck], writes=["catC"])
            yield
            for half in range(2):
                for k in range(8):
                    P.add("pe", lambda e, half=half, k=k: e.matmul(
                        PB[6], lhsT=(catC[:, k, :] if k < 4 else catO[:, k - 4, :]), rhs=wout[:, k, half * 512:(half + 1) * 512],
                        start=(k == 0), stop=(k == 7)),
                        reads=["catC", catk + "_o", "wout"], writes=["pb6"])
                yield
                P.add("dve", lambda e, half=half: e.tensor_tensor(
                    out=X[:, i, half * 512:(half + 1) * 512], in0=X[:, i, half * 512:(half + 1) * 512], in1=PB[6],
                    op=ALU.add),
                    reads=[xk, "pb6"], writes=[xk])
                yield
            if last_layer:
                fc = 2 * NT + i
                P.add("act", lambda e: e.activation(out=sq[:].rearrange("q c t -> q (c t)"), in_=Xi[:, 0:512], func=AF.Square,
                                                     accum_out=ss[:, fc:fc + 1]),
                      reads=[xk, "ss"], writes=["sq", f"ss{fc}a"])
                P.add("act", lambda e: e.activation(out=sq[:].rearrange("q c t -> q (c t)"), in_=Xi[:, 512:1024], func=AF.Square,
                                                     accum_out=rstd[:, fc:fc + 1]),
                      reads=[xk, "ss"], writes=["sq", f"ss{fc}b"])
                P.add("dve", lambda e: e.tensor_tensor(out=ss[:, fc:fc + 1], in0=ss[:, fc:fc + 1], in1=rstd[:, fc:fc + 1],
                                                        op=ALU.add),
                      reads=[f"ss{fc}a", f"ss{fc}b"], writes=[f"ss{fc}"])
                yield
                P.add("act", lambda e: e.activation(out=rstd[:, fc:fc + 1], in_=ss[:, fc:fc + 1], func=AF.Ln,
                                                     bias=epscol[:, 0:1], scale=1.0 / 1024),
                      reads=[f"ss{fc}", "epscol"], writes=[f"rs{fc}"])
                P.add("act", lambda e: e.activation(out=rstd[:, fc:fc + 1], in_=rstd[:, fc:fc + 1], func=AF.Exp, scale=-0.5),
                      reads=[f"rs{fc}"], writes=[f"rs{fc}"])
                yield
                P.add("dve", lambda e: e.scalar_tensor_tensor(out=Xi, in0=Xi, scalar=rstd[:, fc:fc + 1], in1=fgb[:],
                                                               op0=ALU.mult, op1=ALU.mult),
                      reads=[xk, f"rs{fc}", "fgb"], writes=[xk])
                P.add("sp", lambda e: e.dma_start(out=y_d[i * 128:(i + 1) * 128, :], in_=Xi),
                      reads=[xk], writes=[f"o_y{i}"], dma="st")
                outs.append(f"o_y{i}")

        cache_dma(0)
        setup_a(0)
        load_x()
        setup_b(0)
        setup_c(0)
        flags = {("a", 0), ("b", 0), ("c", 0)}
        sched = []
        for L in DBG_LAYERS:
            base = L * (PIPE_T * (NT - 1) + SAMPLE_EXTRA + LAYER_GAP)
            for i in range(NT):
                t0 = base + (0 if i == NT - 1 else SAMPLE_EXTRA + PIPE_T * (i + 1))
                sched.append([t0 + PIPE_D2, 0, L, i, back2(L, i), ("c", L)])
                sched.append([t0, 1, L, i, front(L, i), ("a", L)])
                sched.append([t0 + PIPE_D1, 2, L, i, back1(L, i), ("b", L)])
            if L == 0:
                sched.append([2, 3, 0, 0, setup_cache(0), ("b", 0)])
        active = []
        conv_done = set()
        rnd = 0
        while sched or active:
            for ent in [e for e in sched if e[0] <= rnd]:
                if ent[5] not in flags:
                    continue
                if ent[1] == 0 and (ent[2], ent[3]) not in conv_done:
                    continue
                sched.remove(ent)
                active.append(ent)
            active.sort(key=lambda e: (e[1], e[2], (e[3] + 1) % NT))
            for ent in list(active):
                P.tag = ("bfcs"[ent[1]], ent[2], ent[3]) if ent[1] < 3 else None
                try:
                    next(ent[4])
                except StopIteration:
                    active.remove(ent)
                    kind, L, i = ent[1], ent[2], ent[3]
                    if kind == 2:
                        conv_done.add((L, i))
                    if kind == 3 and 1 in DBG_LAYERS:
                        P.tag = None
                        cache_dma(1)
                    if L == 0 and i == NT - 2 and kind < 3 and 1 in DBG_LAYERS:
                        P.tag = None
                        if kind == 1:
                            setup_a(1)
                            flags.add(("a", 1))
                            nominal = PIPE_T * (NT - 1) + SAMPLE_EXTRA + LAYER_GAP
                            delta = max(0, rnd + 1 - nominal)
                            for e2 in sched:
                                if e2[2] == 1:
                                    e2[0] += delta
                        elif kind == 2:
                            setup_b(1)
                            for _ in setup_cache(1):
                                pass
                            flags.add(("b", 1))
                        elif kind == 0:
                            setup_c(1)
                            flags.add(("c", 1))
            rnd += 1
        P.tag = None
        P.add("sp", lambda e: e.nop(), reads=list(outs))
        P.build(nc, es)
    return nc, P


_CACHE = {}


def kernel(x_prompt, x_sample, cache_conv, state_gla, norm_g, w_in, w_alpha, b_alpha,
           conv_w, conv_b, cn_g, cn_b, w_pw, b_pw, gla_g, w_out, final_g):
    f = lambda a: np.ascontiguousarray(np.asarray(a, dtype=np.float32))
    x_prompt, x_sample, cache_conv, state_gla = f(x_prompt), f(x_sample), f(cache_conv), f(state_gla)
    shared = dict(norm_g=f(norm_g), w_in=f(w_in), w_alpha=f(w_alpha), b_alpha=f(b_alpha), conv_w=f(conv_w),
                  conv_b=f(conv_b), cn_g=f(cn_g), cn_b=f(cn_b), w_pw=f(w_pw), b_pw=f(b_pw), gla_g=f(gla_g),
                  w_out=f(w_out), final_g=f(final_g))
    if "nc" not in _CACHE:
        _CACHE["nc"] = build_nc()[0]
    nc = _CACHE["nc"]
    in_maps = []
    for c in range(NCORES):
        xs = x_sample[16 * c:16 * (c + 1)].reshape(128, 1024)
        m = dict(shared)
        m["x"] = np.ascontiguousarray(np.concatenate([x_prompt[c], xs], axis=0))
        m["cc"] = np.ascontiguousarray(cache_conv[:, 16 * c:16 * (c + 1)])
        m["sg"] = np.ascontiguousarray(state_gla[:, 16 * c:16 * (c + 1)])
        in_maps.append(m)
    res = run_bass_kernel_spmd(nc, in_maps, core_ids=list(range(NCORES)))
    R = res.results
    y_prompt = np.stack([R[c]["y"][:2048] for c in range(NCORES)], axis=0)
    y_sample = np.concatenate([R[c]["y"][2048:].reshape(16, 8, 1024) for c in range(NCORES)], axis=0)
    ncp = np.stack([R[c]["ncp"] for c in range(NCORES)], axis=1)
    nsp = np.stack([R[c]["nsp"] for c in range(NCORES)], axis=1)
    ncs = np.concatenate([R[c]["ncs"] for c in range(NCORES)], axis=1)
    nss = np.concatenate([R[c]["nss"] for c in range(NCORES)], axis=1)
    return (y_prompt.astype(np.float32), y_sample.astype(np.float32), ncp.astype(np.float32),
            nsp.astype(np.float32), ncs.astype(np.float32), nss.astype(np.float32))
```

```python
from contextlib import ExitStack
import numpy as np
import concourse.bass as bass
import concourse.mybir as mybir
from concourse.bass_utils import run_bass_kernel_spmd

F32 = mybir.dt.float32
BF16 = mybir.dt.bfloat16
ALU = mybir.AluOpType
AF = mybir.ActivationFunctionType

NCORES = 8
SG = 2
NT = 17
NPOOL = 0
EPS = 1e-6
DBG_STAGE = 99
DBG_LAYERS = (0, 1)
DBG_TILES = tuple(range(17))
SYNC_SAME_ENGINE_ALL = True
SLACK_F, SLACK_C, SLACK_B, ORDER = 0, 0, 0, "bfc"
TAPS_PER_YIELD = 1
CONV_YIELD_TAPS = 8
CONV_DELAY = 0
PIPE_T, PIPE_D1, PIPE_D2 = 16, 18, 46
LAYER_GAP = 26
CACHE_ROUND = 120
B2_DELAY = 0
NPE = 25
SEM_EPOCH = 1000


class Prog:
    def __init__(self):
        self.ops = []
        self.dma_groups = {}

    WATCH = ("hn", "hT", "lrT", "spb", "NG", "Eq", "Ek", "Ed", "qt", "kt", "kdT", "kd", "v_sb", "szg", "F0", "F1",
             "F3", "Am", "xb", "sq", "catC", "pb0", "pb1", "pb2", "pb3", "pb4", "pb5", "pb6", "pb7",
             "ufull0", "ufull1", "ufull2", "ucat")

    def add(self, eng, fn, reads=(), writes=(), dma=None):
        tag = getattr(self, "tag", None)
        lw = self.__dict__.setdefault("_lw", {})
        if tag is not None:
            tile = tag[1:]
            for r in reads:
                if r in self.WATCH and r in lw and lw[r] is not None and lw[r] != tile:
                    raise RuntimeError(f"pipeline hazard: {tag} reads {r} last written by tile {lw[r]}")
        for w in writes:
            lw[w] = tag[1:] if tag is not None else None
        self.ops.append(dict(eng=eng, fn=fn, reads=tuple(reads), writes=tuple(writes), dma=dma))

    def dma_group(self, name, nsems):
        self.dma_groups[name] = [nsems, 0]

    def build(self, nc, es):
        ops = self.ops
        n = len(ops)
        deps = [set() for _ in range(n)]
        raw = [set() for _ in range(n)]
        dma_sem_of = [None] * n
        last_on_sem = {}
        for i, op in enumerate(ops):
            if op["dma"] is not None:
                g = self.dma_groups[op["dma"]]
                key = (op["dma"], g[1] % g[0])
                g[1] += 1
                dma_sem_of[i] = key
                if key in last_on_sem:
                    deps[i].add(last_on_sem[key])
                last_on_sem[key] = i
        last_w = {}
        readers = {}
        for i, op in enumerate(ops):
            for r in op["reads"]:
                if r in last_w:
                    deps[i].add(last_w[r])
                    raw[i].add(last_w[r])
            for w in op["writes"]:
                if w in last_w:
                    deps[i].add(last_w[w])
                for j in readers.get(w, ()):
                    if j != i:
                        deps[i].add(j)
            for w in op["writes"]:
                last_w[w] = i
                readers[w] = []
            for r in op["reads"]:
                readers.setdefault(r, []).append(i)
        for i, op in enumerate(ops):
            keep = set()
            for j in deps[i]:
                oj = ops[j]
                if oj["dma"] is None and op["dma"] is None and oj["eng"] == op["eng"]:
                    if op["eng"] == "pe":
                        continue
                    if (not SYNC_SAME_ENGINE_ALL) and j not in raw[i]:
                        continue
                keep.add(j)
            deps[i] = keep
        needs_inc = [False] * n
        for i in range(n):
            for j in deps[i]:
                needs_inc[j] = True
        event = [None] * n
        cnt = {}
        for i, op in enumerate(ops):
            if op["dma"] is not None:
                key = ("dma",) + dma_sem_of[i]
                cnt[key] = cnt.get(key, 0) + 16
                event[i] = (key, cnt[key])
            elif needs_inc[i]:
                tot = cnt.get(("tot", op["eng"]), 0)
                cnt[("tot", op["eng"])] = tot + 1
                key = ("eng", op["eng"], tot // SEM_EPOCH)
                cnt[key] = cnt.get(key, 0) + 1
                event[i] = (key, cnt[key])
        cnt = {k: v for k, v in cnt.items() if k[0] != "tot"}
        sem_keys = sorted(set(e[0] for e in event if e is not None), key=str)
        sems = {}
        for k in sem_keys:
            sems[k] = es.enter_context(nc.semaphore("s_" + "_".join(str(x) for x in k)))
        self.n_sems = len(sem_keys)
        self.max_sem = max(cnt.values()) if cnt else 0
        per_eng = {}
        seen = {}
        for i, op in enumerate(ops):
            e = op["eng"]
            waits = {}
            for j in deps[i]:
                k, v = event[j]
                if v > waits.get(k, 0):
                    waits[k] = v
            wl = []
            for k, v in waits.items():
                if seen.get((e, k), 0) >= v:
                    continue
                seen[(e, k)] = v
                wl.append((k, v))
            per_eng.setdefault(e, []).append((i, wl))

        def emit(engname, engobj):
            for i, wl in per_eng.get(engname, []):
                for k, v in wl:
                    engobj.wait_ge(sems[k], v)
                ins = ops[i]["fn"](engobj)
                ev = event[i]
                if ev is not None:
                    ins.then_inc(sems[ev[0]], 16 if ev[0][0] == "dma" else 1)

        with nc.Block() as block:
            @block.tensor
            def _(eng):
                emit("pe", eng)

            @block.scalar
            def _(eng):
                emit("act", eng)

            @block.vector
            def _(eng):
                emit("dve", eng)

            @block.gpsimd
            def _(eng):
                emit("pool", eng)

            @block.sync
            def _(eng):
                emit("sp", eng)


C_A, C_GL, C_ZC, C_Q, C_K, C_V, C_ZG, C_LR = 0, 512, 1024, 1536, 1792, 2048, 2560, 3072
WIN_GROUPS = [("a", 0, 512), ("gl", 512, 1024), ("zc", 1024, 1536), ("q", 1536, 1792),
              ("k", 1792, 2048), ("v", 2048, 2560), ("zg", 2560, 3072), ("lr", 3072, 3088)]


def build_nc(dbg=False):
    nc = bass.Bass("TRN2", target_bir_lowering=False)
    din = lambda name, shape: nc.dram_tensor(name, shape, F32, kind="ExternalInput").ap()
    dout = lambda name, shape: nc.dram_tensor(name, shape, F32, kind="ExternalOutput").ap()
    x_d = din("x", [NT * 128, 1024])
    cc_d = din("cc", [2, 16, 30, 512])
    sg_d = din("sg", [2, 16, 4, 64, 128])
    norm_g = din("norm_g", [2, 1024])
    w_in = din("w_in", [2, 1024, 3088])
    w_alpha = din("w_alpha", [2, 16, 256])
    b_alpha = din("b_alpha", [2, 256])
    conv_w = din("conv_w", [2, 31, 512])
    conv_b = din("conv_b", [2, 512])
    cn_g = din("cn_g", [2, 512])
    cn_b = din("cn_b", [2, 512])
    w_pw = din("w_pw", [2, 512, 512])
    b_pw = din("b_pw", [2, 512])
    gla_g = din("gla_g", [2, 128])
    w_out = din("w_out", [2, 1024, 1024])
    final_g = din("final_g", [1024])
    y_d = dout("y", [NT * 128, 1024])
    ncp_d = dout("ncp", [2, 30, 512])
    nsp_d = dout("nsp", [2, 4, 64, 128])
    ncs_d = dout("ncs", [2, 16, 30, 512])
    nss_d = dout("nss", [2, 16, 4, 64, 128])

    P = Prog()
    P.dma_group("xld", 6)
    P.dma_group("wld", 6)
    P.dma_group("pld", 4)
    P.dma_group("st", 6)
    P.dma_group("sst", 2)
    P.dma_group("sld", 2)
    P.dma_group("cld", 2)
    outs = []

    with ExitStack() as es:
        def sb(name, shape, dt):
            return es.enter_context(nc.sbuf_tensor(name, shape, dt))

        X = sb("X", [128, NT, 1024], F32)
        win = sb("win", [128, 8, 3088], BF16)
        wpw = sb("wpw", [128, 4, 512], BF16)
        wout = sb("wout", [128, 8, 1024], BF16)
        walpha = sb("walpha", [16, 256], BF16)
        fgb = sb("fgb", [128, 1024], F32)
        gcol = sb("gcol", [128, 8], F32)
        convb = sb("convb", [128, 4], F32)
        cng = sb("cng", [128, 4], F32)
        cnb = sb("cnb", [128, 4], F32)
        bpw = sb("bpw", [128, 4], F32)
        glag = sb("glag", [128, 1], F32)
        balpha = sb("balpha", [128, 2], F32)
        nb = sb("nb", [128, 2], F32)
        convw = sb("convw", [128, 4, 31], F32)
        craw = sb("craw", [31, 512], F32)
        ident = sb("ident", [128, 128], BF16)
        identf = sb("identf", [128, 128], F32)
        tri = sb("tri", [128, 128], BF16)
        smask = sb("smask", [128, 128], BF16)
        blk64 = sb("blk64", [128, 128], BF16)
        ones128 = sb("ones128", [128, 128], BF16)
        onesf = sb("onesf", [128, 128], F32)
        epscol = sb("epscol", [128, 1], F32)
        halfcol = sb("halfcol", [128, 1], F32)
        selm = sb("selm", [128, 16], F32)
        ss = sb("ss", [128, 64], F32)
        rstd = sb("rstd", [128, 64], F32)
        nend = sb("nend", [128, 2], F32)
        hn = sb("hn", [128, 1024], BF16)
        hT = sb("hT", [128, 8, 128], BF16)
        lrT = sb("lrT", [16, 128], BF16)
        spb = sb("spb", [128, 2, 128], F32)
        NG = sb("NG", [128, 2, 128], F32)
        Eq = sb("Eq", [128, 2, 128], F32)
        Ek = sb("Ek", [128, 2, 128], F32)
        Ed = sb("Ed", [128, 2, 128], F32)
        qt = sb("qt", [128, 2, 128], BF16)
        kt = sb("kt", [128, 2, 128], BF16)
        kdT = sb("kdT", [128, 2, 128], BF16)
        kd = sb("kd", [128, 256], BF16)
        v_sb = sb("v_sb", [128, 512], BF16)
        szg = sb("szg", [128, 4, 128], BF16)
        AmB = sb("AmB", [128, 512], BF16)
        Am = AmB[:].rearrange("q (c t) -> q c t", t=128)
        osq = AmB
        kdm = AmB[:].rearrange("q (s c) -> q s c", c=256)
        S = sb("S", [128, 2, 128], F32)
        Sbf = sb("Sbf", [128, 2, 128], BF16)
        F0 = sb("F0", [128, 4, 128], F32)
        F1 = sb("F1", [128, 4, 128], F32)
        F3 = sb("F3", [128, 512], F32)
        cf = F3[:, 0:128]
        S0 = F1[:].rearrange("q (p s) v -> q p s v", s=SG)
        accD2 = [sb(f"accD{s}", [128, 4, 128], F32) for s in range(2)]
        ufull = [sb(f"ufull{s}", [128, 4, 158], BF16) for s in range(3)]
        szc2 = [sb(f"szc{s}", [128, 4, 128], BF16) for s in range(3)]
        diag2 = sb("diag2", [128, 4, NPE, 64], BF16)
        I2 = sb("I2", [128, 64], BF16)
        xb = sb("xb", [128, 4, 128], BF16)
        sq = sb("sq", [128, 4, 128], BF16)
        cvs = xb
        catO3 = [sb(f"catO{s}", [128, 4, 128], BF16) for s in range(3)]
        catC = sb("catC", [128, 4, 128], BF16)
        S0bf = sb("S0bf", [128, 2, SG, 128], BF16)
        ucat = sb("ucat", [128, 4, 16, 38], BF16)
        cstage = sb("cstage", [120, 4, 512], BF16)

        psum = es.enter_context(nc.psum_tensor("psum", [128, 8 * 512], F32))
        PB = [psum[:, b * 512:(b + 1) * 512] for b in range(8)]
        psum_b = psum.bitcast(BF16)
        PBb = [psum_b[:, b * 1024:(b + 1) * 1024] for b in range(8)]

        def pb4(b):
            return PB[b].rearrange("q (c t) -> q c t", t=128)

        P.add("pool", lambda e: e.memset(cf, 0.0), writes=["F3"])
        P.add("pool", lambda e: e.affine_select(out=cf, in_=cf, pattern=[[-1, 128]],
                                                 compare_op=ALU.not_equal, fill=1.0, base=0,
                                                 channel_multiplier=1), reads=["F3"], writes=["F3"])
        P.add("dve", lambda e: e.tensor_copy(out=identf[:], in_=cf), reads=["F3"], writes=["identf"])
        P.add("dve", lambda e: e.tensor_copy(out=ident[:], in_=cf), reads=["F3"], writes=["ident"])
        P.add("dve", lambda e: e.tensor_tensor(out=I2[:], in0=ident[:, 0:64], in1=ident[:, 64:128], op=ALU.add),
              reads=["ident"], writes=["I2"])
        P.add("pool", lambda e: e.memset(cf, 1.0), reads=["F3"], writes=["F3"])
        P.add("pool", lambda e: e.affine_select(out=cf, in_=cf, pattern=[[1, 128]],
                                                 compare_op=ALU.is_ge, fill=0.0, base=0,
                                                 channel_multiplier=-1), reads=["F3"], writes=["F3"])
        P.add("dve", lambda e: e.tensor_copy(out=tri[:], in_=cf), reads=["F3"], writes=["tri"])
        cf3 = cf.rearrange("q (a b) -> q a b", b=8)
        P.add("pool", lambda e: e.affine_select(out=cf3, in_=cf3, pattern=[[-8, 16], [0, 8]],
                                                 compare_op=ALU.is_ge, fill=0.0, base=0,
                                                 channel_multiplier=1), reads=["F3"], writes=["F3"])
        P.add("pool", lambda e: e.affine_select(out=cf3, in_=cf3, pattern=[[8, 16], [0, 8]],
                                                 compare_op=ALU.is_ge, fill=0.0, base=7,
                                                 channel_multiplier=-1), reads=["F3"], writes=["F3"])
        P.add("dve", lambda e: e.tensor_copy(out=smask[:], in_=cf), reads=["F3"], writes=["smask"])
        P.add("pool", lambda e: e.memset(selm[:], 1.0), writes=["selm"])
        P.add("pool", lambda e: e.affine_select(out=selm[:], in_=selm[:], pattern=[[-8, 16]],
                                                 compare_op=ALU.is_ge, fill=0.0, base=0,
                                                 channel_multiplier=1), reads=["selm"], writes=["selm"])
        P.add("pool", lambda e: e.affine_select(out=selm[:], in_=selm[:], pattern=[[8, 16]],
                                                 compare_op=ALU.is_ge, fill=0.0, base=7,
                                                 channel_multiplier=-1), reads=["selm"], writes=["selm"])
        P.add("dve", lambda e: e.memset(blk64[:], 0.0), writes=["blk64"])
        P.add("dve", lambda e: e.memset(blk64[0:64, 0:64], 1.0 / 64), writes=["blk64"])
        P.add("dve", lambda e: e.memset(blk64[64:128, 64:128], 1.0 / 64), writes=["blk64"])
        P.add("dve", lambda e: e.memset(ones128[:], 1.0 / 128), writes=["ones128"])
        P.add("dve", lambda e: e.memset(onesf[:], 1.0), writes=["onesf"])
        P.add("dve", lambda e: e.memset(epscol[:], EPS), writes=["epscol"])
        P.add("dve", lambda e: e.memset(halfcol[:], 0.5), writes=["halfcol"])
        P.add("dve", lambda e: e.memset(ss[:], 0.0), writes=["ss"])
        for s in range(3):
            P.add("dve", lambda e, s=s: e.memset(ufull[s][:], 0.0), writes=[f"ufull{s}"])

        def load_x(tiles, with_fg=False):
            for i in tiles:
                P.add("sp", lambda e, i=i: e.dma_start(out=X[:, i, :], in_=x_d[i * 128:(i + 1) * 128, :]),
                      writes=[f"X{i}"], dma="xld")
            if with_fg:
                P.add("sp", lambda e: e.dma_start(out=fgb[:], in_=final_g.partition_broadcast(128)),
                      writes=["fgb"], dma="pld")

        def small_col(dst, src_ap, pat, key, **kw):
            P.add("sp", lambda e: e.dma_start(out=dst, in_=src_ap.rearrange(pat, **kw),
                                              allow_slow_non_contiguous=True),
                  writes=[key], dma="pld")

        def setup_a(L):
            small_col(gcol[:], norm_g[L], "(k q) -> q k", "gcol", q=128)
            small_col(balpha[:], b_alpha[L], "(k q) -> q k", "balpha", q=128)
            small_col(glag[:], gla_g[L], "(q o) -> q o", "glag", o=1)
            P.add("dve", lambda e: e.tensor_scalar(out=nb[:], in0=balpha[:], scalar1=-1.0, scalar2=None, op0=ALU.mult),
                  reads=["balpha"], writes=["nb"])
            win_v = w_in[L].rearrange("(k q) e -> q k e", q=128)
            order = ["lr", "q", "k", "v", "zg", "a", "gl", "zc"]
            grp = {g: (c0, c1) for (g, c0, c1) in WIN_GROUPS}
            for idx, g in enumerate(order):
                c0, c1 = grp[g]
                P.add("pool", lambda e, c0=c0, c1=c1: e.dma_start(out=win[:, :, c0:c1], in_=win_v[:, :, c0:c1]),
                      writes=[f"win_{g}"], dma="wld")
                if idx == 0:
                    P.add("pool", lambda e: e.dma_start(out=walpha[:], in_=w_alpha[L]), writes=["walpha"], dma="wld")
            for g in order:
                c0, c1 = grp[g]
                for k in range(8):
                    P.add("dve", lambda e, c0=c0, c1=c1, k=k: e.tensor_scalar(
                        out=win[:, k, c0:c1], in0=win[:, k, c0:c1], scalar1=gcol[:, k:k + 1], scalar2=None,
                        op0=ALU.mult),
                        reads=[f"win_{g}", "gcol"], writes=[f"win_{g}"])

        def setup_b(L):
            small_col(convb[:], conv_b[L], "(k q) -> q k", "convb", q=128)
            P.add("sp", lambda e: e.dma_start(out=craw[:], in_=conv_w[L]), writes=["craw"], dma="pld")
            cw_ps = PB[7][:, 0:124].rearrange("q (c j) -> q c j", j=31)
            for c in range(4):
                P.add("pe", lambda e, c=c: e.transpose(out=cw_ps[:, c, :], in_=craw[0:31, c * 128:(c + 1) * 128],
                                                        identity=identf[0:31, 0:31]),
                      reads=["craw", "identf"], writes=["pb7"])
            P.add("act", lambda e: e.copy(out=convw[:], in_=cw_ps), reads=["pb7"], writes=["convw"])
            for c in range(4):
                P.add("dve", lambda e, c=c: e.tensor_tensor(
                    out=diag2[:, c], in0=I2[:].unsqueeze(1).to_broadcast([128, NPE, 64]),
                    in1=convw[:, c, 0:NPE].unsqueeze(2).to_broadcast([128, NPE, 64]), op=ALU.mult),
                    reads=["I2", "convw"], writes=["diag"])

        def setup_c(L):
            small_col(cng[:], cn_g[L], "(k q) -> q k", "cng", q=128)
            small_col(cnb[:], cn_b[L], "(k q) -> q k", "cnb", q=128)
            small_col(bpw[:], b_pw[L], "(k q) -> q k", "bpw", q=128)
            P.add("pool", lambda e: e.dma_start(out=wpw[:], in_=w_pw[L].rearrange("(k q) e -> q k e", q=128)),
                  writes=["wpw"], dma="wld")
            P.add("pool", lambda e: e.dma_start(out=wout[:], in_=w_out[L].rearrange("(k q) e -> q k e", q=128)),
                  writes=["wout"], dma="wld")

        def cache_dma(L):
            for g in range(4):
                P.add("pool", lambda e, g=g: e.dma_start(
                    out=cstage[:, g, :], in_=cc_d[L, 4 * g:4 * g + 4].rearrange("b r c -> (b r) c")),
                    writes=["cstage"], dma="cld")
            P.add("sp", lambda e: e.dma_start(out=ncs_d[L, :, 0:22, :], in_=cc_d[L, :, 8:30, :]),
                  writes=[f"o_ncs_old{L}"], dma="st")
            outs.append(f"o_ncs_old{L}")

        def setup_cache(L):
            for g in range(4):
                tp = PBb[7][:, 0:480].rearrange("q (c r) -> q c r", r=120)
                for c in range(4):
                    P.add("pe", lambda e, g=g, c=c, tp=tp: e.transpose(
                        out=tp[:, c, :], in_=cstage[0:120, g, c * 128:(c + 1) * 128], identity=ident[0:120, 0:120]),
                        reads=["cstage", "ident"], writes=["pb7"])
                yield
                P.add("act", lambda e, g=g, tp=tp: e.copy(
                    out=ucat[:, :, 4 * g:4 * g + 4, 0:30],
                    in_=tp.rearrange("q c (b r) -> q c b r", r=30)),
                    reads=["pb7"], writes=["ucat"])
                yield

        def front(L, i):
            P.tag = ("t", L, i)
            sample = (i == NT - 1)
            Xi = X[:, i, :]
            xk = f"X{i}"
            col = L * NT + i
            gi = L * NT + i
            szc = szc2[gi % 3]
            szck = f"szc{gi % 3}"
            catO = catO3[gi % 3]
            catk = f"catT{gi % 3}"
            P.add("act", lambda e: e.activation(out=hn[:], in_=Xi, func=AF.Square, accum_out=ss[:, col:col + 1]),
                  reads=[xk, "ss"], writes=["hn", f"ss{col}"])
            P.add("act", lambda e: e.activation(out=rstd[:, col:col + 1], in_=ss[:, col:col + 1], func=AF.Ln,
                                                 bias=epscol[:, 0:1], scale=1.0 / 1024),
                  reads=[f"ss{col}", "epscol"], writes=[f"rs{col}"])
            P.add("act", lambda e: e.activation(out=rstd[:, col:col + 1], in_=rstd[:, col:col + 1], func=AF.Exp,
                                                 scale=-0.5),
                  reads=[f"rs{col}"], writes=[f"rs{col}"])
            P.add("act", lambda e: e.activation(out=hn[:], in_=Xi, func=AF.Copy, scale=rstd[:, col:col + 1]),
                  reads=[xk, f"rs{col}"], writes=["hn"])
            hT_ps = PBb[0].rearrange("q (k t) -> q k t", t=128)
            for k in range(8):
                P.add("pe", lambda e, k=k: e.transpose(out=hT_ps[:, k, :], in_=hn[:, k * 128:(k + 1) * 128],
                                                        identity=ident[:]),
                      reads=["hn", "ident"], writes=["pb0"])
            yield
            P.add("act", lambda e: e.copy(out=hT[:], in_=hT_ps), reads=["pb0"], writes=["hT"])
            yield

            def proj_fm(bank, c0, nchunks, *wkeys):
                o = pb4(bank)
                for c in range(nchunks):
                    for k in range(8):
                        P.add("pe", lambda e, c=c, k=k: e.matmul(
                            o[:, c, :], lhsT=win[:, k, c0 + c * 128:c0 + (c + 1) * 128], rhs=hT[:, k, :],
                            start=(k == 0), stop=(k == 7)),
                            reads=["hT"] + list(wkeys), writes=[f"pb{bank}"])

            for k in range(8):
                P.add("pe", lambda e, k=k: e.matmul(PB[1][0:16, 0:128], lhsT=win[:, k, C_LR:C_LR + 16], rhs=hT[:, k, :],
                                                     start=(k == 0), stop=(k == 7)),
                      reads=["hT", "win_lr"], writes=["pb1"])
            P.add("act", lambda e: e.copy(out=lrT[:], in_=PB[1][0:16, 0:128]), reads=["pb1"], writes=["lrT"])
            z_ps = PB[1][:, 128:384].rearrange("q (c t) -> q c t", t=128)
            for p in range(2):
                P.add("pe", lambda e, p=p: e.matmul(z_ps[:, p, :], lhsT=walpha[0:16, p * 128:(p + 1) * 128],
                                                     rhs=lrT[0:16, :], start=True, stop=True),
                      reads=["lrT", "walpha"], writes=["pb1"])
            yield
            for p in range(2):
                P.add("act", lambda e, p=p: e.activation(out=spb[:, p, :], in_=z_ps[:, p, :], func=AF.Exp,
                                                          bias=nb[:, p:p + 1], scale=-1.0),
                      reads=["pb1", "nb"], writes=["spb"])
            proj_fm(2, C_Q, 4, "win_q", "win_k")
            yield
            P.add("act", lambda e: e.activation(out=spb[:], in_=spb[:], func=AF.Ln, bias=1.0, scale=1.0),
                  reads=["spb"], writes=["spb"])
            for k in range(8):
                P.add("pe", lambda e, k=k: e.matmul(PB[0], lhsT=hT[:, k, :], rhs=win[:, k, C_V:C_V + 512],
                                                     start=(k == 0), stop=(k == 7)),
                      reads=["hT", "win_v"], writes=["pb0"])
            yield
            for p in range(2):
                P.add("dve", lambda e, p=p: e.tensor_tensor_scan(out=NG[:, p, :], data0=onesf[:], data1=spb[:, p, :],
                                                                  initial=0.0, op0=ALU.mult, op1=ALU.add),
                      reads=["spb", "onesf"], writes=["NG"])
            proj_fm(1, C_ZG, 4, "win_zg")
            yield
            P.add("act", lambda e: e.copy(out=v_sb[:], in_=PB[0]), reads=["pb0"], writes=["v_sb"])
            if not sample:
                P.add("act", lambda e: e.activation(out=Eq[:], in_=NG[:], func=AF.Exp, scale=-1.0 / 16),
                      reads=["NG"], writes=["Eq"])
                P.add("act", lambda e: e.activation(out=Ek[:], in_=NG[:], func=AF.Exp, scale=1.0 / 16),
                      reads=["NG"], writes=["Ek"])
                P.add("dve", lambda e: e.tensor_scalar(out=nend[:].unsqueeze(2), in0=NG[:, :, 127:128], scalar1=-1.0 / 16,
                                                        scalar2=None, op0=ALU.mult),
                      reads=["NG"], writes=["nend"])
                for p in range(2):
                    P.add("act", lambda e, p=p: e.activation(out=Ed[:, p, :], in_=NG[:, p, :], func=AF.Exp,
                                                              bias=nend[:, p:p + 1], scale=1.0 / 16),
                          reads=["NG", "nend"], writes=["Ed"])
            else:
                for p in range(2):
                    NGp = NG[:, p, :].rearrange("q (a b) -> q a b", b=8)
                    Dp = spb[:, p, :].rearrange("q (a b) -> q a b", b=8)
                    D2p = Ed[:, p, :].rearrange("q (a b) -> q a b", b=8)
                    P.add("dve", lambda e, NGp=NGp, Dp=Dp: e.tensor_copy(out=Dp[:, 0:1, :], in_=NGp[:, 0:1, :]),
                          reads=["NG"], writes=["spb"])
                    P.add("dve", lambda e, NGp=NGp, Dp=Dp: e.tensor_tensor(
                        out=Dp[:, 1:16, :], in0=NGp[:, 1:16, :],
                        in1=NGp[:, 0:15, 7:8].to_broadcast([128, 15, 8]), op=ALU.subtract),
                        reads=["NG"], writes=["spb"])
                    P.add("dve", lambda e, NGp=NGp, D2p=D2p: e.tensor_tensor(
                        out=D2p, in0=NGp, in1=NGp[:, :, 7:8].to_broadcast([128, 16, 8]), op=ALU.subtract),
                        reads=["NG"], writes=["Ed"])
                P.add("act", lambda e: e.activation(out=Eq[:], in_=spb[:], func=AF.Exp, scale=-1.0 / 16),
                      reads=["spb"], writes=["Eq"])
                P.add("act", lambda e: e.activation(out=Ek[:], in_=spb[:], func=AF.Exp, scale=1.0 / 16),
                      reads=["spb"], writes=["Ek"])
                P.add("act", lambda e: e.activation(out=Ed[:], in_=Ed[:], func=AF.Exp, scale=1.0 / 16),
                      reads=["Ed"], writes=["Ed"])
            proj_fm(3, C_A, 4, "win_a")
            yield
            qk = pb4(2)
            P.add("dve", lambda e: e.scalar_tensor_tensor(out=qt[:], in0=qk[:, 0:2, :], scalar=0.125, in1=Eq[:],
                                                           op0=ALU.mult, op1=ALU.mult),
                  reads=["pb2", "Eq"], writes=["qt"])
            P.add("dve", lambda e: e.tensor_tensor(out=kt[:], in0=qk[:, 2:4, :], in1=Ek[:], op=ALU.mult),
                  reads=["pb2", "Ek"], writes=["kt"])
            P.add("dve", lambda e: e.tensor_tensor(out=kdT[:], in0=qk[:, 2:4, :], in1=Ed[:], op=ALU.mult),
                  reads=["pb2", "Ed"], writes=["kdT"])
            proj_fm(0, C_GL, 4, "win_gl")
            yield
            P.add("act", lambda e: e.activation(out=szg[:], in_=pb4(1), func=AF.Silu), reads=["pb1"], writes=["szg"])
            yield
            proj_fm(1, C_ZC, 4, "win_zc")
            P.add("act", lambda e: e.activation(out=F0[:], in_=pb4(0), func=AF.Tanh, scale=0.5), reads=["pb0"], writes=["F0"])
            P.add("act", lambda e: e.activation(out=F0[:], in_=F0[:], func=AF.Identity, bias=halfcol[:, 0:1], scale=0.5),
                  reads=["F0", "halfcol"], writes=["F0"])
            yield
            uf = ufull[gi % 3]
            ufk = f"ufull{gi % 3}"
            ufo = ufull[(gi + 1) % 3]
            ufok = f"ufull{(gi + 1) % 3}"
            if i == 0:
                P.add("dve", lambda e: e.memset(uf[:, :, 0:30], 0.0), writes=[ufk])
            if not sample:
                P.add("dve", lambda e: e.tensor_tensor(out=uf[:, :, 30:158], in0=pb4(3), in1=F0[:], op=ALU.mult),
                      reads=["pb3", "F0"], writes=[ufk])
                if i < NT - 2:
                    P.add("act", lambda e: e.copy(out=ufo[:, :, 0:30], in_=uf[:, :, 128:158]),
                          reads=[ufk], writes=[ufok])
            else:
                P.add("dve", lambda e: e.tensor_tensor(
                    out=ucat[:, :, :, 30:38], in0=pb4(3).rearrange("q c (b r) -> q c b r", r=8),
                    in1=F0[:].rearrange("q c (b r) -> q c b r", r=8), op=ALU.mult),
                    reads=["pb3", "F0"], writes=["ucat"])
            need_u32 = sample or (i == NT - 2)
            if need_u32:
                P.add("dve", lambda e: e.tensor_tensor(out=F0[:], in0=pb4(3), in1=F0[:], op=ALU.mult),
                      reads=["pb3", "F0"], writes=["F0"])
            P.add("act", lambda e: e.activation(out=szc[:], in_=pb4(1), func=AF.Silu), reads=["pb1"], writes=[szck])
            yield
            if need_u32:
                uT_ps = PB[2]
                for c in range(4):
                    P.add("pe", lambda e, c=c: e.transpose(out=uT_ps[:, c * 128:(c + 1) * 128], in_=F0[:, c, :],
                                                            identity=identf[:]),
                          reads=["F0", "identf"], writes=["pb2"])
                P.add("act", lambda e: e.copy(out=F3[:], in_=uT_ps), reads=["pb2"], writes=["F3"])
                if sample:
                    for b in range(16):
                        P.add("sp", lambda e, b=b: e.dma_start(out=ncs_d[L, b, 22:30, :], in_=F3[b * 8:(b + 1) * 8, :]),
                              reads=["F3"], writes=[f"o_ncs{L}_{b}"], dma="st")
                        outs.append(f"o_ncs{L}_{b}")
                else:
                    P.add("sp", lambda e: e.dma_start(out=ncp_d[L], in_=F3[98:128, :]),
                          reads=["F3"], writes=[f"o_ncp{L}"], dma="st")
                    outs.append(f"o_ncp{L}")
                yield

            kd_ps = PBb[4][:, 0:256]
            for p in range(2):
                P.add("pe", lambda e, p=p: e.transpose(out=kd_ps[:, p * 128:(p + 1) * 128], in_=kdT[:, p, :],
                                                        identity=ident[:]),
                      reads=["kdT", "ident"], writes=["pb4"])
            yield
            P.add("act", lambda e: e.copy(out=kd[:], in_=kd_ps), reads=["pb4"], writes=["kd"])
            yield
            A_ps = [pb4(4), pb4(5)]
            for h in (0, 2, 1, 3):
                p, hl = h // 2, h % 2
                P.add("pe", lambda e, h=h, p=p, hl=hl: e.matmul(
                    A_ps[hl][:, p, :], lhsT=kt[hl * 64:(hl + 1) * 64, p, :], rhs=qt[hl * 64:(hl + 1) * 64, p, :],
                    start=True, stop=True),
                    reads=["kt", "qt"], writes=[f"pb{4 if hl == 0 else 5}"])
            yield
            mk = smask if sample else tri
            mkk = "smask" if sample else "tri"
            Am_v = Am.rearrange("q (p hl) t -> q hl p t", hl=2)
            for hl in range(2):
                P.add("dve", lambda e, hl=hl: e.tensor_tensor(
                    out=Am_v[:, hl], in0=A_ps[hl][:, 0:2, :],
                    in1=mk[:].unsqueeze(1).to_broadcast([128, 2, 128]), op=ALU.mult),
                    reads=[f"pb{4 if hl == 0 else 5}", mkk], writes=["Am"])
            yield
            o_ps = pb4(4)
            first = (i == 0)
            for h in range(4):
                p, hl = h // 2, h % 2
                use_S = (not sample) and (not first)
                if use_S:
                    P.add("pe", lambda e, h=h, p=p, hl=hl: e.matmul(
                        o_ps[:, h, :], lhsT=Sbf[hl * 64:(hl + 1) * 64, p, :], rhs=qt[hl * 64:(hl + 1) * 64, p, :],
                        start=True, stop=False),
                        reads=["Sbf", "qt"], writes=["pb4"])
                P.add("pe", lambda e, h=h, use_S=use_S: e.matmul(
                    o_ps[:, h, :], lhsT=v_sb[:, h * 128:(h + 1) * 128], rhs=Am[:, h, :],
                    start=(not use_S), stop=True),
                    reads=["v_sb", "Am"], writes=["pb4"])
            yield
            if not sample:
                Pp = PB[5][:, 0:256].rearrange("q (c v) -> q c v", v=128)
                for h in range(4):
                    p, hl = h // 2, h % 2
                    P.add("pe", lambda e, h=h, p=p, hl=hl: e.matmul(
                        Pp[hl * 64:(hl + 1) * 64, p, :], lhsT=kd[:, h * 64:(h + 1) * 64],
                        rhs=v_sb[:, h * 128:(h + 1) * 128], start=True, stop=True),
                        reads=["kd", "v_sb"], writes=["pb5"])
                yield
                if first:
                    P.add("dve", lambda e: e.tensor_copy(out=S[:], in_=Pp), reads=["pb5"], writes=["S"])
                else:
                    P.add("dve", lambda e: e.tensor_tensor(out=S[:], in0=S[:],
                                                            in1=Eq[:, :, 127:128].to_broadcast([128, 2, 128]),
                                                            op=ALU.mult),
                          reads=["S", "Eq"], writes=["S"])
                    P.add("dve", lambda e: e.tensor_tensor(out=S[:], in0=S[:], in1=Pp, op=ALU.add),
                          reads=["S", "pb5"], writes=["S"])
                if i < NT - 2:
                    P.add("act", lambda e: e.copy(out=Sbf[:], in_=S[:]), reads=["S"], writes=["Sbf"])
                else:
                    P.add("sp", lambda e: e.dma_start(
                        out=nsp_d[L].rearrange("(p hl) d v -> (hl d) p v", hl=2), in_=S[:]),
                        reads=["S"], writes=[f"o_nsp{L}"], dma="st")
                    outs.append(f"o_nsp{L}")
                osrc, osk = o_ps, ["pb4"]
            else:
                oi_ps = [pb4(2), pb4(3)]
                oik = ["pb2", "pb3"]
                Eq4 = Eq[:].rearrange("q p (b r) -> q p b r", r=8)
                sview = "s hl d v -> (hl d) s v"
                for g in range(16 // SG):
                    for p in range(2):
                        P.add("sp", lambda e, g=g, p=p: e.dma_start(
                            out=S0[:, p], in_=sg_d[L, SG * g:SG * g + SG, 2 * p:2 * p + 2].rearrange(sview)),
                            writes=["F1"], dma="sld")
                    P.add("act", lambda e: e.copy(out=S0bf[:], in_=S0), reads=["F1"], writes=["S0bf"])
                    for s in range(SG):
                        seq = SG * g + s
                        P.add("dve", lambda e, s=s, seq=seq: e.tensor_scalar(
                            out=kdm[:, s, :], in0=kd[:], scalar1=selm[:, seq:seq + 1], scalar2=None, op0=ALU.mult),
                            reads=["kd", "selm"], writes=["Am"])
                    Pp = PB[5][:, 0:2 * SG * 128].rearrange("q (p s v) -> q p s v", s=SG, v=128)
                    for s in range(SG):
                        seq = SG * g + s
                        for h in (0, 2, 1, 3):
                            p, hl = h // 2, h % 2
                            P.add("pe", lambda e, s=s, seq=seq, h=h, p=p, hl=hl: e.matmul(
                                oi_ps[hl][:, p, seq * 8:(seq + 1) * 8], lhsT=S0bf[hl * 64:(hl + 1) * 64, p, s, :],
                                rhs=qt[hl * 64:(hl + 1) * 64, p, seq * 8:(seq + 1) * 8], start=True, stop=True),
                                reads=["S0bf", "qt"], writes=[oik[hl]])
                        for h in range(4):
                            p, hl = h // 2, h % 2
                            P.add("pe", lambda e, s=s, h=h, p=p, hl=hl: e.matmul(
                                Pp[hl * 64:(hl + 1) * 64, p, s, :], lhsT=kdm[:, s, h * 64:(h + 1) * 64],
                                rhs=v_sb[:, h * 128:(h + 1) * 128], start=True, stop=True),
                                reads=["Am", "v_sb"], writes=["pb5"])
                    P.add("dve", lambda e, g=g: e.tensor_tensor(
                        out=S0, in0=S0, in1=Eq4[:, :, SG * g:SG * g + SG, 7:8].to_broadcast([128, 2, SG, 128]),
                        op=ALU.mult), reads=["F1", "Eq", "S0bf"], writes=["F1"])
                    P.add("dve", lambda e: e.tensor_tensor(out=S0, in0=S0, in1=Pp, op=ALU.add),
                          reads=["F1", "pb5"], writes=["F1"])
                    for p in range(2):
                        P.add("sp", lambda e, g=g, p=p: e.dma_start(
                            out=nss_d[L, SG * g:SG * g + SG, 2 * p:2 * p + 2].rearrange(sview), in_=S0[:, p]),
                            reads=["F1"], writes=[f"o_nss{L}_{g}_{p}"], dma="sst")
                        outs.append(f"o_nss{L}_{g}_{p}")
                    yield
                F3_v = F3[:].rearrange("q (p hl t) -> q hl p t", hl=2, t=128)
                for hl in range(2):
                    P.add("act", lambda e, hl=hl: e.copy(out=F3_v[:, hl], in_=oi_ps[hl][:, 0:2, :]),
                          reads=[oik[hl]], writes=["F3"])
                osum = F3[:].rearrange("q (c t) -> q c t", t=128)
                P.add("dve", lambda e: e.tensor_tensor(out=osum, in0=o_ps, in1=osum, op=ALU.add),
                      reads=["pb4", "F3"], writes=["F3"])
                osrc, osk = osum, ["F3"]
            yield
            P.add("act", lambda e: e.activation(out=osq[:].rearrange("q (c t) -> q c t", t=128), in_=osrc, func=AF.Square),
                  reads=osk, writes=["Am"])
            yield
            P.add("pe", lambda e: e.matmul(PB[5], lhsT=ones128[:], rhs=osq[:], start=True, stop=True),
                  reads=["Am", "ones128"], writes=["pb5"])
            yield
            P.add("act", lambda e: e.activation(out=F1[:], in_=pb4(5), func=AF.Ln, bias=epscol[:, 0:1], scale=1.0),
                  reads=["pb5", "epscol"], writes=["F1"])
            P.add("act", lambda e: e.activation(out=F1[:], in_=F1[:], func=AF.Exp, scale=-0.5),
                  reads=["F1"], writes=["F1"])
            yield
            P.add("dve", lambda e: e.tensor_tensor(out=F1[:], in0=osrc, in1=F1[:], op=ALU.mult),
                  reads=osk + ["F1"], writes=["F1"])
            P.add("dve", lambda e: e.scalar_tensor_tensor(out=catO[:], in0=F1[:], scalar=glag[:, 0:1], in1=szg[:],
                                                           op0=ALU.mult, op1=ALU.mult),
                  reads=["F1", "glag", "szg"], writes=[catk + "_o"])

        def back1(L, i):
            sample = (i == NT - 1)
            P.tag = ("c", L, i)
            gi = L * NT + i
            uf = ufull[gi % 3]
            srck = "ucat" if sample else f"ufull{gi % 3}"
            accD = accD2[gi % 2]
            ak = f"acc{gi % 2}_"

            def ush(c, j):
                if sample:
                    return ucat[:, c, :, j:j + 8]
                return uf[:, c, j:j + 128]

            def accv(c):
                if sample:
                    return accD[:, c, :].rearrange("q (b r) -> q b r", r=8)
                return accD[:, c, :]

            cv_ps = pb4(7)
            for _ in range(CONV_DELAY):
                yield
            for c in range(4):
                for j in range(NPE):
                    for hl in range(2):
                        r0, r1 = hl * 64, (hl + 1) * 64
                        P.add("pe", lambda e, c=c, j=j, r0=r0, r1=r1: e.matmul(
                            cv_ps[r0:r1, c, :], lhsT=diag2[r0:r1, c, j, :], rhs=ush(c, j)[r0:r1],
                            start=(j == 0), stop=(j == NPE - 1)),
                            reads=[srck, "diag"], writes=["pb7"])
                    if j % CONV_YIELD_TAPS == CONV_YIELD_TAPS - 1:
                        yield
                yield
            for c in range(4):
                P.add("act", lambda e, c=c: e.activation(out=accD[:, c, :], in_=cv_ps[:, c, :], func=AF.Identity,
                                                          bias=convb[:, c:c + 1], scale=1.0),
                      reads=["pb7", "convb"], writes=[f"{ak}{c}"])
            yield
            for j in range(NPE, 31):
                for c in range(4):
                    P.add("dve", lambda e, c=c, j=j: e.scalar_tensor_tensor(
                        out=accv(c), in0=ush(c, j), scalar=convw[:, c, j:j + 1], in1=accv(c),
                        op0=ALU.mult, op1=ALU.add),
                        reads=[srck, "convw", f"{ak}{c}"], writes=[f"{ak}{c}"])
                if (j - NPE) % TAPS_PER_YIELD == TAPS_PER_YIELD - 1:
                    yield

        def back2(L, i):
            last_layer = (L == 1)
            P.tag = ("b", L, i)
            Xi = X[:, i, :]
            xk = f"X{i}"
            gi = L * NT + i
            szc = szc2[gi % 3]
            szck = f"szc{gi % 3}"
            catO = catO3[gi % 3]
            catk = f"catT{gi % 3}"
            xf = accD2[gi % 2]
            acck = [f"acc{gi % 2}_{c}" for c in range(4)]
            for _ in range(B2_DELAY):
                yield
            P.add("act", lambda e: e.copy(out=xb[:], in_=xf[:]), reads=acck, writes=["xb"])
            yield
            mean_ps = pb4(6)
            for c in range(4):
                P.add("pe", lambda e, c=c: e.matmul(mean_ps[:, c, :], lhsT=blk64[:], rhs=xb[:, c, :], start=True, stop=True),
                      reads=["xb", "blk64"], writes=["pb6"])
            yield
            P.add("dve", lambda e: e.tensor_tensor(out=xf[:], in0=xf[:], in1=mean_ps, op=ALU.subtract),
                  reads=acck + ["pb6"], writes=acck)
            yield
            P.add("act", lambda e: e.activation(out=sq[:], in_=xf[:], func=AF.Square), reads=acck, writes=["sq"])
            yield
            var_ps = pb4(6)
            for c in range(4):
                P.add("pe", lambda e, c=c: e.matmul(var_ps[:, c, :], lhsT=blk64[:], rhs=sq[:, c, :], start=True, stop=True),
                      reads=["sq", "blk64"], writes=["pb6"])
            yield
            P.add("act", lambda e: e.activation(out=var_ps, in_=var_ps, func=AF.Ln, bias=epscol[:, 0:1], scale=1.0),
                  reads=["pb6", "epscol"], writes=["pb6"])
            P.add("act", lambda e: e.activation(out=var_ps, in_=var_ps, func=AF.Exp, scale=-0.5),
                  reads=["pb6"], writes=["pb6"])
            yield
            P.add("dve", lambda e: e.tensor_tensor(out=xf[:], in0=xf[:], in1=var_ps, op=ALU.mult),
                  reads=acck + ["pb6"], writes=acck)
            yield
            for c in range(4):
                P.add("act", lambda e, c=c: e.activation(out=cvs[:, c, :], in_=xf[:, c, :], func=AF.Silu,
                                                          bias=cnb[:, c:c + 1], scale=cng[:, c:c + 1]),
                      reads=acck + ["cng", "cnb"], writes=["xb"])
            yield
            pw_ps = pb4(6)
            for oc in range(4):
                for k in range(4):
                    P.add("pe", lambda e, oc=oc, k=k: e.matmul(
                        pw_ps[:, oc, :], lhsT=wpw[:, k, oc * 128:(oc + 1) * 128], rhs=cvs[:, k, :],
                        start=(k == 0), stop=(k == 3)),
                        reads=["xb", "wpw"], writes=["pb6"])
            yield
            for oc in range(4):
                P.add("dve", lambda e, oc=oc: e.scalar_tensor_tensor(
                    out=catC[:, oc, :], in0=pw_ps[:, oc, :], scalar=bpw[:, oc:oc + 1], in1=szc[:, oc, :],
                    op0=ALU.add, op1=ALU.mult),
                    reads=["pb6", "bpw", szck], writes=["catC"])
            yield
            for half in range(2):
                for k in range(8):
                    P.add("pe", lambda e, half=half, k=k: e.matmul(
                        PB[6], lhsT=(catC[:, k, :] if k < 4 else catO[:, k - 4, :]), rhs=wout[:, k, half * 512:(half + 1) * 512],
                        start=(k == 0), stop=(k == 7)),
                        reads=["catC", catk + "_o", "wout"], writes=["pb6"])
                yield
                P.add("dve", lambda e, half=half: e.tensor_tensor(
                    out=X[:, i, half * 512:(half + 1) * 512], in0=X[:, i, half * 512:(half + 1) * 512], in1=PB[6],
                    op=ALU.add),
                    reads=[xk, "pb6"], writes=[xk])
                yield
            if last_layer:
                fc = 2 * NT + i
                P.add("act", lambda e: e.activation(out=sq[:].rearrange("q c t -> q (c t)"), in_=Xi[:, 0:512], func=AF.Square,
                                                     accum_out=ss[:, fc:fc + 1]),
                      reads=[xk, "ss"], writes=["sq", f"ss{fc}a"])
                P.add("act", lambda e: e.activation(out=sq[:].rearrange("q c t -> q (c t)"), in_=Xi[:, 512:1024], func=AF.Square,
                                                     accum_out=rstd[:, fc:fc + 1]),
                      reads=[xk, "ss"], writes=["sq", f"ss{fc}b"])
                P.add("dve", lambda e: e.tensor_tensor(out=ss[:, fc:fc + 1], in0=ss[:, fc:fc + 1], in1=rstd[:, fc:fc + 1],
                                                        op=ALU.add),
                      reads=[f"ss{fc}a", f"ss{fc}b"], writes=[f"ss{fc}"])
                yield
                P.add("act", lambda e: e.activation(out=rstd[:, fc:fc + 1], in_=ss[:, fc:fc + 1], func=AF.Ln,
                                                     bias=epscol[:, 0:1], scale=1.0 / 1024),
                      reads=[f"ss{fc}", "epscol"], writes=[f"rs{fc}"])
                P.add("act", lambda e: e.activation(out=rstd[:, fc:fc + 1], in_=rstd[:, fc:fc + 1], func=AF.Exp, scale=-0.5),
                      reads=[f"rs{fc}"], writes=[f"rs{fc}"])
                yield
                P.add("dve", lambda e: e.scalar_tensor_tensor(out=Xi, in0=Xi, scalar=rstd[:, fc:fc + 1], in1=fgb[:],
                                                               op0=ALU.mult, op1=ALU.mult),
                      reads=[xk, f"rs{fc}", "fgb"], writes=[xk])
                P.add("sp", lambda e: e.dma_start(out=y_d[i * 128:(i + 1) * 128, :], in_=Xi),
                      reads=[xk], writes=[f"o_y{i}"], dma="st")
                outs.append(f"o_y{i}")

        cache_dma(0)
        setup_a(0)
        load_x([0, 1])
        setup_b(0)
        setup_c(0)
        load_x(range(2, NT), with_fg=True)
        flags = {("a", 0), ("b", 0), ("c", 0)}
        sched = []
        for L in DBG_LAYERS:
            base = L * (PIPE_T * (NT - 1) + LAYER_GAP)
            for i in range(NT):
                sched.append([base + PIPE_T * i + PIPE_D2, 0, L, i, back2(L, i), ("c", L)])
                sched.append([base + PIPE_T * i, 1, L, i, front(L, i), ("a", L)])
                sched.append([base + PIPE_T * i + PIPE_D1, 2, L, i, back1(L, i), ("b", L)])
            if L == 0:
                sched.append([2, 3, 0, 0, setup_cache(0), ("b", 0)])
        active = []
        conv_done = set()
        rnd = 0
        while sched or active:
            for ent in [e for e in sched if e[0] <= rnd]:
                if ent[5] not in flags:
                    continue
                if ent[1] == 0 and (ent[2], ent[3]) not in conv_done:
                    continue
                sched.remove(ent)
                active.append(ent)
            active.sort(key=lambda e: (e[1], e[2], e[3]))
            for ent in list(active):
                P.tag = ("bfcs"[ent[1]], ent[2], ent[3]) if ent[1] < 3 else None
                try:
                    next(ent[4])
                except StopIteration:
                    active.remove(ent)
                    kind, L, i = ent[1], ent[2], ent[3]
                    if kind == 2:
                        conv_done.add((L, i))
                    if kind == 3 and 1 in DBG_LAYERS:
                        P.tag = None
                        cache_dma(1)
                    if L == 0 and i == NT - 1 and 1 in DBG_LAYERS:
                        P.tag = None
                        if kind == 1:
                            setup_a(1)
                            flags.add(("a", 1))
                            nominal = PIPE_T * (NT - 1) + LAYER_GAP
                            delta = max(0, rnd + 1 - nominal)
                            for e2 in sched:
                                if e2[2] == 1:
                                    e2[0] += delta
                        elif kind == 2:
                            setup_b(1)
                            for _ in setup_cache(1):
                                pass
                            flags.add(("b", 1))
                        elif kind == 0:
                            setup_c(1)
                            flags.add(("c", 1))
            rnd += 1
        P.tag = None
        P.add("sp", lambda e: e.nop(), reads=list(outs))
        P.build(nc, es)
    return nc, P


_CACHE = {}


def kernel(x_prompt, x_sample, cache_conv, state_gla, norm_g, w_in, w_alpha, b_alpha,
           conv_w, conv_b, cn_g, cn_b, w_pw, b_pw, gla_g, w_out, final_g):
    f = lambda a: np.ascontiguousarray(np.asarray(a, dtype=np.float32))
    x_prompt, x_sample, cache_conv, state_gla = f(x_prompt), f(x_sample), f(cache_conv), f(state_gla)
    shared = dict(norm_g=f(norm_g), w_in=f(w_in), w_alpha=f(w_alpha), b_alpha=f(b_alpha), conv_w=f(conv_w),
                  conv_b=f(conv_b), cn_g=f(cn_g), cn_b=f(cn_b), w_pw=f(w_pw), b_pw=f(b_pw), gla_g=f(gla_g),
                  w_out=f(w_out), final_g=f(final_g))
    if "nc" not in _CACHE:
        _CACHE["nc"] = build_nc()[0]
    nc = _CACHE["nc"]
    in_maps = []
    for c in range(NCORES):
        xs = x_sample[16 * c:16 * (c + 1)].reshape(128, 1024)
        m = dict(shared)
        m["x"] = np.ascontiguousarray(np.concatenate([x_prompt[c], xs], axis=0))
        m["cc"] = np.ascontiguousarray(cache_conv[:, 16 * c:16 * (c + 1)])
        m["sg"] = np.ascontiguousarray(state_gla[:, 16 * c:16 * (c + 1)])
        in_maps.append(m)
    res = run_bass_kernel_spmd(nc, in_maps, core_ids=list(range(NCORES)))
    R = res.results
    y_prompt = np.stack([R[c]["y"][:2048] for c in range(NCORES)], axis=0)
    y_sample = np.concatenate([R[c]["y"][2048:].reshape(16, 8, 1024) for c in range(NCORES)], axis=0)
    ncp = np.stack([R[c]["ncp"] for c in range(NCORES)], axis=1)
    nsp = np.stack([R[c]["nsp"] for c in range(NCORES)], axis=1)
    ncs = np.concatenate([R[c]["ncs"] for c in range(NCORES)], axis=1)
    nss = np.concatenate([R[c]["nss"] for c in range(NCORES)], axis=1)
    return (y_prompt.astype(np.float32), y_sample.astype(np.float32), ncp.astype(np.float32),
            nsp.astype(np.float32), ncs.astype(np.float32), nss.astype(np.float32))
```

```python
from contextlib import ExitStack
import numpy as np
import concourse.bass as bass
import concourse.mybir as mybir
from concourse.bass_utils import run_bass_kernel_spmd

F32 = mybir.dt.float32
BF16 = mybir.dt.bfloat16
ALU = mybir.AluOpType
AF = mybir.ActivationFunctionType

NCORES = 8
SG = 2
NT = 17
NPOOL = 0
EPS = 1e-6
DBG_STAGE = 99
DBG_LAYERS = (0, 1)
DBG_TILES = tuple(range(17))
SYNC_SAME_ENGINE_ALL = True
SLACK_F, SLACK_C, SLACK_B, ORDER = 0, 0, 0, "bfc"
TAPS_PER_YIELD = 1
CONV_YIELD_TAPS = 8
CONV_DELAY = 0
PIPE_T, PIPE_D1, PIPE_D2 = 16, 18, 46
LAYER_GAP = 26
CACHE_ROUND = 120
B2_DELAY = 0
NPE = 25
SEM_EPOCH = 1000


class Prog:
    def __init__(self):
        self.ops = []
        self.dma_groups = {}

    WATCH = ("hn", "hT", "lrT", "spb", "NG", "Eq", "Ek", "Ed", "qt", "kt", "kdT", "kd", "v_sb", "szg", "F0", "F1",
             "F3", "Am", "xb", "sq", "catC", "pb0", "pb1", "pb2", "pb3", "pb4", "pb5", "pb6", "pb7",
             "ufull0", "ufull1", "ufull2", "ucat")

    def add(self, eng, fn, reads=(), writes=(), dma=None):
        tag = getattr(self, "tag", None)
        lw = self.__dict__.setdefault("_lw", {})
        if tag is not None:
            tile = tag[1:]
            for r in reads:
                if r in self.WATCH and r in lw and lw[r] is not None and lw[r] != tile:
                    raise RuntimeError(f"pipeline hazard: {tag} reads {r} last written by tile {lw[r]}")
        for w in writes:
            lw[w] = tag[1:] if tag is not None else None
        self.ops.append(dict(eng=eng, fn=fn, reads=tuple(reads), writes=tuple(writes), dma=dma))

    def dma_group(self, name, nsems):
        self.dma_groups[name] = [nsems, 0]

    def build(self, nc, es):
        ops = self.ops
        n = len(ops)
        deps = [set() for _ in range(n)]
        raw = [set() for _ in range(n)]
        dma_sem_of = [None] * n
        last_on_sem = {}
        for i, op in enumerate(ops):
            if op["dma"] is not None:
                g = self.dma_groups[op["dma"]]
                key = (op["dma"], g[1] % g[0])
                g[1] += 1
                dma_sem_of[i] = key
                if key in last_on_sem:
                    deps[i].add(last_on_sem[key])
                last_on_sem[key] = i
        last_w = {}
        readers = {}
        for i, op in enumerate(ops):
            for r in op["reads"]:
                if r in last_w:
                    deps[i].add(last_w[r])
                    raw[i].add(last_w[r])
            for w in op["writes"]:
                if w in last_w:
                    deps[i].add(last_w[w])
                for j in readers.get(w, ()):
                    if j != i:
                        deps[i].add(j)
            for w in op["writes"]:
                last_w[w] = i
                readers[w] = []
            for r in op["reads"]:
                readers.setdefault(r, []).append(i)
        for i, op in enumerate(ops):
            keep = set()
            for j in deps[i]:
                oj = ops[j]
                if oj["dma"] is None and op["dma"] is None and oj["eng"] == op["eng"]:
                    if op["eng"] == "pe":
                        continue
                    if (not SYNC_SAME_ENGINE_ALL) and j not in raw[i]:
                        continue
                keep.add(j)
            deps[i] = keep
        needs_inc = [False] * n
        for i in range(n):
            for j in deps[i]:
                needs_inc[j] = True
        event = [None] * n
        cnt = {}
        for i, op in enumerate(ops):
            if op["dma"] is not None:
                key = ("dma",) + dma_sem_of[i]
                cnt[key] = cnt.get(key, 0) + 16
                event[i] = (key, cnt[key])
            elif needs_inc[i]:
                tot = cnt.get(("tot", op["eng"]), 0)
                cnt[("tot", op["eng"])] = tot + 1
                key = ("eng", op["eng"], tot // SEM_EPOCH)
                cnt[key] = cnt.get(key, 0) + 1
                event[i] = (key, cnt[key])
        cnt = {k: v for k, v in cnt.items() if k[0] != "tot"}
        sem_keys = sorted(set(e[0] for e in event if e is not None), key=str)
        sems = {}
        for k in sem_keys:
            sems[k] = es.enter_context(nc.semaphore("s_" + "_".join(str(x) for x in k)))
        self.n_sems = len(sem_keys)
        self.max_sem = max(cnt.values()) if cnt else 0
        per_eng = {}
        seen = {}
        for i, op in enumerate(ops):
            e = op["eng"]
            waits = {}
            for j in deps[i]:
                k, v = event[j]
                if v > waits.get(k, 0):
                    waits[k] = v
            wl = []
            for k, v in waits.items():
                if seen.get((e, k), 0) >= v:
                    continue
                seen[(e, k)] = v
                wl.append((k, v))
            per_eng.setdefault(e, []).append((i, wl))

        def emit(engname, engobj):
            for i, wl in per_eng.get(engname, []):
                for k, v in wl:
                    engobj.wait_ge(sems[k], v)
                ins = ops[i]["fn"](engobj)
                ev = event[i]
                if ev is not None:
                    ins.then_inc(sems[ev[0]], 16 if ev[0][0] == "dma" else 1)

        with nc.Block() as block:
            @block.tensor
            def _(eng):
                emit("pe", eng)

            @block.scalar
            def _(eng):
                emit("act", eng)

            @block.vector
            def _(eng):
                emit("dve", eng)

            @block.gpsimd
            def _(eng):
                emit("pool", eng)

            @block.sync
            def _(eng):
                emit("sp", eng)


C_A, C_GL, C_ZC, C_Q, C_K, C_V, C_ZG, C_LR = 0, 512, 1024, 1536, 1792, 2048, 2560, 3072
WIN_GROUPS = [("a", 0, 512), ("gl", 512, 1024), ("zc", 1024, 1536), ("q", 1536, 1792),
              ("k", 1792, 2048), ("v", 2048, 2560), ("zg", 2560, 3072), ("lr", 3072, 3088)]


def build_nc(dbg=False):
    nc = bass.Bass("TRN2", target_bir_lowering=False)
    din = lambda name, shape: nc.dram_tensor(name, shape, F32, kind="ExternalInput").ap()
    dout = lambda name, shape: nc.dram_tensor(name, shape, F32, kind="ExternalOutput").ap()
    x_d = din("x", [NT * 128, 1024])
    cc_d = din("cc", [2, 16, 30, 512])
    sg_d = din("sg", [2, 16, 4, 64, 128])
    norm_g = din("norm_g", [2, 1024])
    w_in = din("w_in", [2, 1024, 3088])
    w_alpha = din("w_alpha", [2, 16, 256])
    b_alpha = din("b_alpha", [2, 256])
    conv_w = din("conv_w", [2, 31, 512])
    conv_b = din("conv_b", [2, 512])
    cn_g = din("cn_g", [2, 512])
    cn_b = din("cn_b", [2, 512])
    w_pw = din("w_pw", [2, 512, 512])
    b_pw = din("b_pw", [2, 512])
    gla_g = din("gla_g", [2, 128])
    w_out = din("w_out", [2, 1024, 1024])
    final_g = din("final_g", [1024])
    y_d = dout("y", [NT * 128, 1024])
    ncp_d = dout("ncp", [2, 30, 512])
    nsp_d = dout("nsp", [2, 4, 64, 128])
    ncs_d = dout("ncs", [2, 16, 30, 512])
    nss_d = dout("nss", [2, 16, 4, 64, 128])

    P = Prog()
    P.dma_group("xld", 6)
    P.dma_group("wld", 6)
    P.dma_group("pld", 4)
    P.dma_group("st", 6)
    P.dma_group("sst", 4)
    P.dma_group("sld", 4)
    P.dma_group("cld", 2)
    P.dma_group("cst", 4)
    outs = []

    with ExitStack() as es:
        def sb(name, shape, dt):
            return es.enter_context(nc.sbuf_tensor(name, shape, dt))

        X = sb("X", [128, NT, 1024], F32)
        win = sb("win", [128, 8, 3088], BF16)
        wpw = sb("wpw", [128, 4, 512], BF16)
        wout = sb("wout", [128, 8, 1024], BF16)
        walpha = sb("walpha", [16, 256], BF16)
        fgb = sb("fgb", [128, 1024], F32)
        gcol = sb("gcol", [128, 8], F32)
        convb = sb("convb", [128, 4], F32)
        cng = sb("cng", [128, 4], F32)
        cnb = sb("cnb", [128, 4], F32)
        bpw = sb("bpw", [128, 4], F32)
        glag = sb("glag", [128, 1], F32)
        balpha = sb("balpha", [128, 2], F32)
        nb = sb("nb", [128, 2], F32)
        convw = sb("convw", [128, 4, 31], F32)
        craw = sb("craw", [31, 512], F32)
        ident = sb("ident", [128, 128], BF16)
        identf = sb("identf", [128, 128], F32)
        tri = sb("tri", [128, 128], BF16)
        smask = sb("smask", [128, 128], BF16)
        blk64 = sb("blk64", [128, 128], BF16)
        ones128 = sb("ones128", [128, 128], BF16)
        onesf = sb("onesf", [128, 128], F32)
        epscol = sb("epscol", [128, 1], F32)
        halfcol = sb("halfcol", [128, 1], F32)
        selm = sb("selm", [128, 16], F32)
        ss = sb("ss", [128, 64], F32)
        rstd = sb("rstd", [128, 64], F32)
        nend = sb("nend", [128, 2], F32)
        hn = sb("hn", [128, 1024], BF16)
        hT = sb("hT", [128, 8, 128], BF16)
        lrT = sb("lrT", [16, 128], BF16)
        spb = sb("spb", [128, 2, 128], F32)
        NG = sb("NG", [128, 2, 128], F32)
        Eq = sb("Eq", [128, 2, 128], F32)
        Ek = sb("Ek", [128, 2, 128], F32)
        Ed = sb("Ed", [128, 2, 128], F32)
        qt = sb("qt", [128, 2, 128], BF16)
        kt = sb("kt", [128, 2, 128], BF16)
        kdT = sb("kdT", [128, 2, 128], BF16)
        kd = sb("kd", [128, 256], BF16)
        v_sb = sb("v_sb", [128, 512], BF16)
        szg = sb("szg", [128, 4, 128], BF16)
        AmB = sb("AmB", [128, 512], BF16)
        Am = AmB[:].rearrange("q (c t) -> q c t", t=128)
        osq = AmB
        kdm = AmB[:].rearrange("q (s c) -> q s c", c=256)
        S = sb("S", [128, 2, 128], F32)
        Sbf = sb("Sbf", [128, 2, 128], BF16)
        F0 = sb("F0", [128, 4, 128], F32)
        F1 = sb("F1", [128, 4, 128], F32)
        F3 = sb("F3", [128, 512], F32)
        cf = F3[:, 0:128]
        S0 = F1[:].rearrange("q (p s) v -> q p s v", s=SG)
        accD2 = [sb(f"accD{s}", [128, 4, 128], F32) for s in range(2)]
        ufull = [sb(f"ufull{s}", [128, 4, 158], BF16) for s in range(3)]
        szc2 = [sb(f"szc{s}", [128, 4, 128], BF16) for s in range(3)]
        diag2 = sb("diag2", [128, 4, NPE, 64], BF16)
        I2 = sb("I2", [128, 64], BF16)
        xb = sb("xb", [128, 4, 128], BF16)
        sq = sb("sq", [128, 4, 128], BF16)
        cvs = xb
        catO3 = [sb(f"catO{s}", [128, 4, 128], BF16) for s in range(3)]
        catC = sb("catC", [128, 4, 128], BF16)
        S0bf = sb("S0bf", [128, 2, SG, 128], BF16)
        ucat = sb("ucat", [128, 4, 16, 38], BF16)
        cstage = sb("cstage", [120, 4, 512], BF16)

        psum = es.enter_context(nc.psum_tensor("psum", [128, 8 * 512], F32))
        PB = [psum[:, b * 512:(b + 1) * 512] for b in range(8)]
        psum_b = psum.bitcast(BF16)
        PBb = [psum_b[:, b * 1024:(b + 1) * 1024] for b in range(8)]

        def pb4(b):
            return PB[b].rearrange("q (c t) -> q c t", t=128)

        P.add("pool", lambda e: e.memset(cf, 0.0), writes=["F3"])
        P.add("pool", lambda e: e.affine_select(out=cf, in_=cf, pattern=[[-1, 128]],
                                                 compare_op=ALU.not_equal, fill=1.0, base=0,
                                                 channel_multiplier=1), reads=["F3"], writes=["F3"])
        P.add("dve", lambda e: e.tensor_copy(out=identf[:], in_=cf), reads=["F3"], writes=["identf"])
        P.add("dve", lambda e: e.tensor_copy(out=ident[:], in_=cf), reads=["F3"], writes=["ident"])
        P.add("dve", lambda e: e.tensor_tensor(out=I2[:], in0=ident[:, 0:64], in1=ident[:, 64:128], op=ALU.add),
              reads=["ident"], writes=["I2"])
        P.add("pool", lambda e: e.memset(cf, 1.0), reads=["F3"], writes=["F3"])
        P.add("pool", lambda e: e.affine_select(out=cf, in_=cf, pattern=[[1, 128]],
                                                 compare_op=ALU.is_ge, fill=0.0, base=0,
                                                 channel_multiplier=-1), reads=["F3"], writes=["F3"])
        P.add("dve", lambda e: e.tensor_copy(out=tri[:], in_=cf), reads=["F3"], writes=["tri"])
        cf3 = cf.rearrange("q (a b) -> q a b", b=8)
        P.add("pool", lambda e: e.affine_select(out=cf3, in_=cf3, pattern=[[-8, 16], [0, 8]],
                                                 compare_op=ALU.is_ge, fill=0.0, base=0,
                                                 channel_multiplier=1), reads=["F3"], writes=["F3"])
        P.add("pool", lambda e: e.affine_select(out=cf3, in_=cf3, pattern=[[8, 16], [0, 8]],
                                                 compare_op=ALU.is_ge, fill=0.0, base=7,
                                                 channel_multiplier=-1), reads=["F3"], writes=["F3"])
        P.add("dve", lambda e: e.tensor_copy(out=smask[:], in_=cf), reads=["F3"], writes=["smask"])
        P.add("pool", lambda e: e.memset(selm[:], 1.0), writes=["selm"])
        P.add("pool", lambda e: e.affine_select(out=selm[:], in_=selm[:], pattern=[[-8, 16]],
                                                 compare_op=ALU.is_ge, fill=0.0, base=0,
                                                 channel_multiplier=1), reads=["selm"], writes=["selm"])
        P.add("pool", lambda e: e.affine_select(out=selm[:], in_=selm[:], pattern=[[8, 16]],
                                                 compare_op=ALU.is_ge, fill=0.0, base=7,
                                                 channel_multiplier=-1), reads=["selm"], writes=["selm"])
        P.add("dve", lambda e: e.memset(blk64[:], 0.0), writes=["blk64"])
        P.add("dve", lambda e: e.memset(blk64[0:64, 0:64], 1.0 / 64), writes=["blk64"])
        P.add("dve", lambda e: e.memset(blk64[64:128, 64:128], 1.0 / 64), writes=["blk64"])
        P.add("dve", lambda e: e.memset(ones128[:], 1.0 / 128), writes=["ones128"])
        P.add("dve", lambda e: e.memset(onesf[:], 1.0), writes=["onesf"])
        P.add("dve", lambda e: e.memset(epscol[:], EPS), writes=["epscol"])
        P.add("dve", lambda e: e.memset(halfcol[:], 0.5), writes=["halfcol"])
        P.add("dve", lambda e: e.memset(ss[:], 0.0), writes=["ss"])
        for s in range(3):
            P.add("dve", lambda e, s=s: e.memset(ufull[s][:], 0.0), writes=[f"ufull{s}"])

        def load_x(tiles, with_fg=False):
            for i in tiles:
                P.add("sp", lambda e, i=i: e.dma_start(out=X[:, i, :], in_=x_d[i * 128:(i + 1) * 128, :]),
                      writes=[f"X{i}"], dma="xld")
            if with_fg:
                P.add("sp", lambda e: e.dma_start(out=fgb[:], in_=final_g.partition_broadcast(128)),
                      writes=["fgb"], dma="pld")

        def small_col(dst, src_ap, pat, key, **kw):
            P.add("sp", lambda e: e.dma_start(out=dst, in_=src_ap.rearrange(pat, **kw),
                                              allow_slow_non_contiguous=True),
                  writes=[key], dma="pld")

        def setup_a(L):
            small_col(gcol[:], norm_g[L], "(k q) -> q k", "gcol", q=128)
            small_col(balpha[:], b_alpha[L], "(k q) -> q k", "balpha", q=128)
            small_col(glag[:], gla_g[L], "(q o) -> q o", "glag", o=1)
            P.add("dve", lambda e: e.tensor_scalar(out=nb[:], in0=balpha[:], scalar1=-1.0, scalar2=None, op0=ALU.mult),
                  reads=["balpha"], writes=["nb"])
            win_v = w_in[L].rearrange("(k q) e -> q k e", q=128)
            order = ["lr", "q", "k", "v", "zg", "a", "gl", "zc"]
            grp = {g: (c0, c1) for (g, c0, c1) in WIN_GROUPS}
            for idx, g in enumerate(order):
                c0, c1 = grp[g]
                P.add("pool", lambda e, c0=c0, c1=c1: e.dma_start(out=win[:, :, c0:c1], in_=win_v[:, :, c0:c1]),
                      writes=[f"win_{g}"], dma="wld")
                if idx == 0:
                    P.add("pool", lambda e: e.dma_start(out=walpha[:], in_=w_alpha[L]), writes=["walpha"], dma="wld")
            for g in order:
                c0, c1 = grp[g]
                P.add("dve", lambda e, c0=c0, c1=c1: e.tensor_tensor(
                    out=win[:, :, c0:c1], in0=win[:, :, c0:c1],
                    in1=gcol[:].unsqueeze(2).to_broadcast([128, 8, c1 - c0]), op=ALU.mult),
                    reads=[f"win_{g}", "gcol"], writes=[f"win_{g}"])

        def setup_b(L):
            small_col(convb[:], conv_b[L], "(k q) -> q k", "convb", q=128)
            P.add("sp", lambda e: e.dma_start(out=craw[:], in_=conv_w[L]), writes=["craw"], dma="pld")
            cw_ps = PB[7][:, 0:124].rearrange("q (c j) -> q c j", j=31)
            for c in range(4):
                P.add("pe", lambda e, c=c: e.transpose(out=cw_ps[:, c, :], in_=craw[0:31, c * 128:(c + 1) * 128],
                                                        identity=identf[0:31, 0:31]),
                      reads=["craw", "identf"], writes=["pb7"])
            P.add("act", lambda e: e.copy(out=convw[:], in_=cw_ps), reads=["pb7"], writes=["convw"])
            for c in range(4):
                P.add("dve", lambda e, c=c: e.tensor_tensor(
                    out=diag2[:, c], in0=I2[:].unsqueeze(1).to_broadcast([128, NPE, 64]),
                    in1=convw[:, c, 0:NPE].unsqueeze(2).to_broadcast([128, NPE, 64]), op=ALU.mult),
                    reads=["I2", "convw"], writes=["diag"])

        def setup_c(L):
            small_col(cng[:], cn_g[L], "(k q) -> q k", "cng", q=128)
            small_col(cnb[:], cn_b[L], "(k q) -> q k", "cnb", q=128)
            small_col(bpw[:], b_pw[L], "(k q) -> q k", "bpw", q=128)
            P.add("pool", lambda e: e.dma_start(out=wpw[:], in_=w_pw[L].rearrange("(k q) e -> q k e", q=128)),
                  writes=["wpw"], dma="wld")
            P.add("pool", lambda e: e.dma_start(out=wout[:], in_=w_out[L].rearrange("(k q) e -> q k e", q=128)),
                  writes=["wout"], dma="wld")

        def cache_dma(L):
            for g in range(4):
                P.add("pool", lambda e, g=g: e.dma_start(
                    out=cstage[:, g, :], in_=cc_d[L, 4 * g:4 * g + 4].rearrange("b r c -> (b r) c")),
                    writes=["cstage"], dma="cld")
            P.add("sp", lambda e: e.dma_start(out=ncs_d[L, :, 0:22, :], in_=cc_d[L, :, 8:30, :]),
                  writes=[f"o_ncs_old{L}"], dma="st")
            outs.append(f"o_ncs_old{L}")

        def setup_cache(L):
            for g in range(4):
                tp = PBb[7][:, 0:480].rearrange("q (c r) -> q c r", r=120)
                for c in range(4):
                    P.add("pe", lambda e, g=g, c=c, tp=tp: e.transpose(
                        out=tp[:, c, :], in_=cstage[0:120, g, c * 128:(c + 1) * 128], identity=ident[0:120, 0:120]),
                        reads=["cstage", "ident"], writes=["pb7"])
                yield
                P.add("act", lambda e, g=g, tp=tp: e.copy(
                    out=ucat[:, :, 4 * g:4 * g + 4, 0:30],
                    in_=tp.rearrange("q c (b r) -> q c b r", r=30)),
                    reads=["pb7"], writes=["ucat"])
                yield

        def front(L, i):
            P.tag = ("t", L, i)
            sample = (i == NT - 1)
            Xi = X[:, i, :]
            xk = f"X{i}"
            col = L * NT + i
            gi = L * NT + i
            szc = szc2[gi % 3]
            szck = f"szc{gi % 3}"
            catO = catO3[gi % 3]
            catk = f"catT{gi % 3}"
            P.add("act", lambda e: e.activation(out=hn[:], in_=Xi, func=AF.Square, accum_out=ss[:, col:col + 1]),
                  reads=[xk, "ss"], writes=["hn", f"ss{col}"])
            P.add("act", lambda e: e.activation(out=rstd[:, col:col + 1], in_=ss[:, col:col + 1], func=AF.Ln,
                                                 bias=epscol[:, 0:1], scale=1.0 / 1024),
                  reads=[f"ss{col}", "epscol"], writes=[f"rs{col}"])
            P.add("act", lambda e: e.activation(out=rstd[:, col:col + 1], in_=rstd[:, col:col + 1], func=AF.Exp,
                                                 scale=-0.5),
                  reads=[f"rs{col}"], writes=[f"rs{col}"])
            P.add("act", lambda e: e.activation(out=hn[:], in_=Xi, func=AF.Copy, scale=rstd[:, col:col + 1]),
                  reads=[xk, f"rs{col}"], writes=["hn"])
            hT_ps = PBb[0].rearrange("q (k t) -> q k t", t=128)
            for k in range(8):
                P.add("pe", lambda e, k=k: e.transpose(out=hT_ps[:, k, :], in_=hn[:, k * 128:(k + 1) * 128],
                                                        identity=ident[:]),
                      reads=["hn", "ident"], writes=["pb0"])
            yield
            P.add("act", lambda e: e.copy(out=hT[:], in_=hT_ps), reads=["pb0"], writes=["hT"])
            yield

            def proj_fm(bank, c0, nchunks, *wkeys):
                o = pb4(bank)
                for c in range(nchunks):
                    for k in range(8):
                        P.add("pe", lambda e, c=c, k=k: e.matmul(
                            o[:, c, :], lhsT=win[:, k, c0 + c * 128:c0 + (c + 1) * 128], rhs=hT[:, k, :],
                            start=(k == 0), stop=(k == 7)),
                            reads=["hT"] + list(wkeys), writes=[f"pb{bank}"])

            for k in range(8):
                P.add("pe", lambda e, k=k: e.matmul(PB[1][0:16, 0:128], lhsT=win[:, k, C_LR:C_LR + 16], rhs=hT[:, k, :],
                                                     start=(k == 0), stop=(k == 7)),
                      reads=["hT", "win_lr"], writes=["pb1"])
            P.add("act", lambda e: e.copy(out=lrT[:], in_=PB[1][0:16, 0:128]), reads=["pb1"], writes=["lrT"])
            z_ps = PB[1][:, 128:384].rearrange("q (c t) -> q c t", t=128)
            for p in range(2):
                P.add("pe", lambda e, p=p: e.matmul(z_ps[:, p, :], lhsT=walpha[0:16, p * 128:(p + 1) * 128],
                                                     rhs=lrT[0:16, :], start=True, stop=True),
                      reads=["lrT", "walpha"], writes=["pb1"])
            yield
            for p in range(2):
                P.add("act", lambda e, p=p: e.activation(out=spb[:, p, :], in_=z_ps[:, p, :], func=AF.Exp,
                                                          bias=nb[:, p:p + 1], scale=-1.0),
                      reads=["pb1", "nb"], writes=["spb"])
            proj_fm(2, C_Q, 4, "win_q", "win_k")
            yield
            P.add("act", lambda e: e.activation(out=spb[:], in_=spb[:], func=AF.Ln, bias=1.0, scale=1.0),
                  reads=["spb"], writes=["spb"])
            for k in range(8):
                P.add("pe", lambda e, k=k: e.matmul(PB[0], lhsT=hT[:, k, :], rhs=win[:, k, C_V:C_V + 512],
                                                     start=(k == 0), stop=(k == 7)),
                      reads=["hT", "win_v"], writes=["pb0"])
            yield
            for p in range(2):
                P.add("dve", lambda e, p=p: e.tensor_tensor_scan(out=NG[:, p, :], data0=onesf[:], data1=spb[:, p, :],
                                                                  initial=0.0, op0=ALU.mult, op1=ALU.add),
                      reads=["spb", "onesf"], writes=["NG"])
            proj_fm(1, C_ZG, 4, "win_zg")
            yield
            P.add("act", lambda e: e.copy(out=v_sb[:], in_=PB[0]), reads=["pb0"], writes=["v_sb"])
            if not sample:
                P.add("act", lambda e: e.activation(out=Eq[:], in_=NG[:], func=AF.Exp, scale=-1.0 / 16),
                      reads=["NG"], writes=["Eq"])
                P.add("act", lambda e: e.activation(out=Ek[:], in_=NG[:], func=AF.Exp, scale=1.0 / 16),
                      reads=["NG"], writes=["Ek"])
                P.add("dve", lambda e: e.tensor_scalar(out=nend[:].unsqueeze(2), in0=NG[:, :, 127:128], scalar1=-1.0 / 16,
                                                        scalar2=None, op0=ALU.mult),
                      reads=["NG"], writes=["nend"])
                for p in range(2):
                    P.add("act", lambda e, p=p: e.activation(out=Ed[:, p, :], in_=NG[:, p, :], func=AF.Exp,
                                                              bias=nend[:, p:p + 1], scale=1.0 / 16),
                          reads=["NG", "nend"], writes=["Ed"])
            else:
                for p in range(2):
                    NGp = NG[:, p, :].rearrange("q (a b) -> q a b", b=8)
                    Dp = spb[:, p, :].rearrange("q (a b) -> q a b", b=8)
                    D2p = Ed[:, p, :].rearrange("q (a b) -> q a b", b=8)
                    P.add("dve", lambda e, NGp=NGp, Dp=Dp: e.tensor_copy(out=Dp[:, 0:1, :], in_=NGp[:, 0:1, :]),
                          reads=["NG"], writes=["spb"])
                    P.add("dve", lambda e, NGp=NGp, Dp=Dp: e.tensor_tensor(
                        out=Dp[:, 1:16, :], in0=NGp[:, 1:16, :],
                        in1=NGp[:, 0:15, 7:8].to_broadcast([128, 15, 8]), op=ALU.subtract),
                        reads=["NG"], writes=["spb"])
                    P.add("dve", lambda e, NGp=NGp, D2p=D2p: e.tensor_tensor(
                        out=D2p, in0=NGp, in1=NGp[:, :, 7:8].to_broadcast([128, 16, 8]), op=ALU.subtract),
                        reads=["NG"], writes=["Ed"])
                P.add("act", lambda e: e.activation(out=Eq[:], in_=spb[:], func=AF.Exp, scale=-1.0 / 16),
                      reads=["spb"], writes=["Eq"])
                P.add("act", lambda e: e.activation(out=Ek[:], in_=spb[:], func=AF.Exp, scale=1.0 / 16),
                      reads=["spb"], writes=["Ek"])
                P.add("act", lambda e: e.activation(out=Ed[:], in_=Ed[:], func=AF.Exp, scale=1.0 / 16),
                      reads=["Ed"], writes=["Ed"])
            proj_fm(3, C_A, 4, "win_a")
            yield
            qk = pb4(2)
            P.add("dve", lambda e: e.scalar_tensor_tensor(out=qt[:], in0=qk[:, 0:2, :], scalar=0.125, in1=Eq[:],
                                                           op0=ALU.mult, op1=ALU.mult),
                  reads=["pb2", "Eq"], writes=["qt"])
            P.add("dve", lambda e: e.tensor_tensor(out=kt[:], in0=qk[:, 2:4, :], in1=Ek[:], op=ALU.mult),
                  reads=["pb2", "Ek"], writes=["kt"])
            P.add("dve", lambda e: e.tensor_tensor(out=kdT[:], in0=qk[:, 2:4, :], in1=Ed[:], op=ALU.mult),
                  reads=["pb2", "Ed"], writes=["kdT"])
            proj_fm(0, C_GL, 4, "win_gl")
            yield
            P.add("act", lambda e: e.activation(out=szg[:], in_=pb4(1), func=AF.Silu), reads=["pb1"], writes=["szg"])
            yield
            proj_fm(1, C_ZC, 4, "win_zc")
            P.add("act", lambda e: e.activation(out=F0[:], in_=pb4(0), func=AF.Tanh, scale=0.5), reads=["pb0"], writes=["F0"])
            P.add("act", lambda e: e.activation(out=F0[:], in_=F0[:], func=AF.Identity, bias=halfcol[:, 0:1], scale=0.5),
                  reads=["F0", "halfcol"], writes=["F0"])
            yield
            uf = ufull[gi % 3]
            ufk = f"ufull{gi % 3}"
            ufo = ufull[(gi + 1) % 3]
            ufok = f"ufull{(gi + 1) % 3}"
            if i == 0:
                P.add("dve", lambda e: e.memset(uf[:, :, 0:30], 0.0), writes=[ufk])
            if not sample:
                P.add("dve", lambda e: e.tensor_tensor(out=uf[:, :, 30:158], in0=pb4(3), in1=F0[:], op=ALU.mult),
                      reads=["pb3", "F0"], writes=[ufk])
                if i < NT - 2:
                    P.add("act", lambda e: e.copy(out=ufo[:, :, 0:30], in_=uf[:, :, 128:158]),
                          reads=[ufk], writes=[ufok])
            else:
                P.add("dve", lambda e: e.tensor_tensor(
                    out=ucat[:, :, :, 30:38], in0=pb4(3).rearrange("q c (b r) -> q c b r", r=8),
                    in1=F0[:].rearrange("q c (b r) -> q c b r", r=8), op=ALU.mult),
                    reads=["pb3", "F0"], writes=["ucat"])
            need_u32 = sample or (i == NT - 2)
            if need_u32:
                P.add("dve", lambda e: e.tensor_tensor(out=F0[:], in0=pb4(3), in1=F0[:], op=ALU.mult),
                      reads=["pb3", "F0"], writes=["F0"])
            P.add("act", lambda e: e.activation(out=szc[:], in_=pb4(1), func=AF.Silu), reads=["pb1"], writes=[szck])
            yield
            if need_u32:
                uT_ps = PB[2]
                for c in range(4):
                    P.add("pe", lambda e, c=c: e.transpose(out=uT_ps[:, c * 128:(c + 1) * 128], in_=F0[:, c, :],
                                                            identity=identf[:]),
                          reads=["F0", "identf"], writes=["pb2"])
                P.add("act", lambda e: e.copy(out=F3[:], in_=uT_ps), reads=["pb2"], writes=["F3"])
                if sample:
                    for b in range(16):
                        P.add("pool", lambda e, b=b: e.dma_start(out=ncs_d[L, b, 22:30, :], in_=F3[b * 8:(b + 1) * 8, :]),
                              reads=["F3"], writes=[f"o_ncs{L}_{b}"], dma="cst")
                        outs.append(f"o_ncs{L}_{b}")
                else:
                    P.add("sp", lambda e: e.dma_start(out=ncp_d[L], in_=F3[98:128, :]),
                          reads=["F3"], writes=[f"o_ncp{L}"], dma="st")
                    outs.append(f"o_ncp{L}")
                yield

            kd_ps = PBb[4][:, 0:256]
            for p in range(2):
                P.add("pe", lambda e, p=p: e.transpose(out=kd_ps[:, p * 128:(p + 1) * 128], in_=kdT[:, p, :],
                                                        identity=ident[:]),
                      reads=["kdT", "ident"], writes=["pb4"])
            yield
            P.add("act", lambda e: e.copy(out=kd[:], in_=kd_ps), reads=["pb4"], writes=["kd"])
            yield
            A_ps = [pb4(4), pb4(5)]
            for h in (0, 2, 1, 3):
                p, hl = h // 2, h % 2
                P.add("pe", lambda e, h=h, p=p, hl=hl: e.matmul(
                    A_ps[hl][:, p, :], lhsT=kt[hl * 64:(hl + 1) * 64, p, :], rhs=qt[hl * 64:(hl + 1) * 64, p, :],
                    start=True, stop=True),
                    reads=["kt", "qt"], writes=[f"pb{4 if hl == 0 else 5}"])
            yield
            mk = smask if sample else tri
            mkk = "smask" if sample else "tri"
            Am_v = Am.rearrange("q (p hl) t -> q hl p t", hl=2)
            for hl in range(2):
                P.add("dve", lambda e, hl=hl: e.tensor_tensor(
                    out=Am_v[:, hl], in0=A_ps[hl][:, 0:2, :],
                    in1=mk[:].unsqueeze(1).to_broadcast([128, 2, 128]), op=ALU.mult),
                    reads=[f"pb{4 if hl == 0 else 5}", mkk], writes=["Am"])
            yield
            o_ps = pb4(4)
            first = (i == 0)
            for h in range(4):
                p, hl = h // 2, h % 2
                use_S = (not sample) and (not first)
                if use_S:
                    P.add("pe", lambda e, h=h, p=p, hl=hl: e.matmul(
                        o_ps[:, h, :], lhsT=Sbf[hl * 64:(hl + 1) * 64, p, :], rhs=qt[hl * 64:(hl + 1) * 64, p, :],
                        start=True, stop=False),
                        reads=["Sbf", "qt"], writes=["pb4"])
                P.add("pe", lambda e, h=h, use_S=use_S: e.matmul(
                    o_ps[:, h, :], lhsT=v_sb[:, h * 128:(h + 1) * 128], rhs=Am[:, h, :],
                    start=(not use_S), stop=True),
                    reads=["v_sb", "Am"], writes=["pb4"])
            yield
            if not sample:
                Pp = PB[5][:, 0:256].rearrange("q (c v) -> q c v", v=128)
                for h in range(4):
                    p, hl = h // 2, h % 2
                    P.add("pe", lambda e, h=h, p=p, hl=hl: e.matmul(
                        Pp[hl * 64:(hl + 1) * 64, p, :], lhsT=kd[:, h * 64:(h + 1) * 64],
                        rhs=v_sb[:, h * 128:(h + 1) * 128], start=True, stop=True),
                        reads=["kd", "v_sb"], writes=["pb5"])
                yield
                if first:
                    P.add("dve", lambda e: e.tensor_copy(out=S[:], in_=Pp), reads=["pb5"], writes=["S"])
                else:
                    P.add("dve", lambda e: e.tensor_tensor(out=S[:], in0=S[:],
                                                            in1=Eq[:, :, 127:128].to_broadcast([128, 2, 128]),
                                                            op=ALU.mult),
                          reads=["S", "Eq"], writes=["S"])
                    P.add("dve", lambda e: e.tensor_tensor(out=S[:], in0=S[:], in1=Pp, op=ALU.add),
                          reads=["S", "pb5"], writes=["S"])
                if i < NT - 2:
                    P.add("act", lambda e: e.copy(out=Sbf[:], in_=S[:]), reads=["S"], writes=["Sbf"])
                else:
                    P.add("sp", lambda e: e.dma_start(
                        out=nsp_d[L].rearrange("(p hl) d v -> (hl d) p v", hl=2), in_=S[:]),
                        reads=["S"], writes=[f"o_nsp{L}"], dma="st")
                    outs.append(f"o_nsp{L}")
                osrc, osk = o_ps, ["pb4"]
            else:
                oi_ps = [pb4(2), pb4(3)]
                oik = ["pb2", "pb3"]
                Eq4 = Eq[:].rearrange("q p (b r) -> q p b r", r=8)
                sview = "s hl d v -> (hl d) s v"
                for g in range(16 // SG):
                    for p in range(2):
                        P.add("sp", lambda e, g=g, p=p: e.dma_start(
                            out=S0[:, p], in_=sg_d[L, SG * g:SG * g + SG, 2 * p:2 * p + 2].rearrange(sview)),
                            writes=["F1"], dma="sld")
                    P.add("act", lambda e: e.copy(out=S0bf[:], in_=S0), reads=["F1"], writes=["S0bf"])
                    for s in range(SG):
                        seq = SG * g + s
                        P.add("dve", lambda e, s=s, seq=seq: e.tensor_scalar(
                            out=kdm[:, s, :], in0=kd[:], scalar1=selm[:, seq:seq + 1], scalar2=None, op0=ALU.mult),
                            reads=["kd", "selm"], writes=["Am"])
                    Pp = PB[5][:, 0:2 * SG * 128].rearrange("q (p s v) -> q p s v", s=SG, v=128)
                    for s in range(SG):
                        seq = SG * g + s
                        for h in (0, 2, 1, 3):
                            p, hl = h // 2, h % 2
                            P.add("pe", lambda e, s=s, seq=seq, h=h, p=p, hl=hl: e.matmul(
                                oi_ps[hl][:, p, seq * 8:(seq + 1) * 8], lhsT=S0bf[hl * 64:(hl + 1) * 64, p, s, :],
                                rhs=qt[hl * 64:(hl + 1) * 64, p, seq * 8:(seq + 1) * 8], start=True, stop=True),
                                reads=["S0bf", "qt"], writes=[oik[hl]])
                        for h in range(4):
                            p, hl = h // 2, h % 2
                            P.add("pe", lambda e, s=s, h=h, p=p, hl=hl: e.matmul(
                                Pp[hl * 64:(hl + 1) * 64, p, s, :], lhsT=kdm[:, s, h * 64:(h + 1) * 64],
                                rhs=v_sb[:, h * 128:(h + 1) * 128], start=True, stop=True),
                                reads=["Am", "v_sb"], writes=["pb5"])
                    Sn = F3[:].rearrange("q (p s v) -> q p s v", s=SG, v=128)
                    P.add("dve", lambda e, g=g: e.tensor_tensor(
                        out=Sn, in0=S0, in1=Eq4[:, :, SG * g:SG * g + SG, 7:8].to_broadcast([128, 2, SG, 128]),
                        op=ALU.mult), reads=["F1", "Eq", "S0bf"], writes=["F3"])
                    P.add("dve", lambda e: e.tensor_tensor(out=Sn, in0=Sn, in1=Pp, op=ALU.add),
                          reads=["F3", "pb5"], writes=["F3"])
                    for p in range(2):
                        P.add("pool", lambda e, g=g, p=p: e.dma_start(
                            out=nss_d[L, SG * g:SG * g + SG, 2 * p:2 * p + 2].rearrange(sview), in_=Sn[:, p]),
                            reads=["F3"], writes=[f"o_nss{L}_{g}_{p}"], dma="sst")
                        outs.append(f"o_nss{L}_{g}_{p}")
                    yield
                F3_v = F3[:].rearrange("q (p hl t) -> q hl p t", hl=2, t=128)
                for hl in range(2):
                    P.add("act", lambda e, hl=hl: e.copy(out=F3_v[:, hl], in_=oi_ps[hl][:, 0:2, :]),
                          reads=[oik[hl]], writes=["F3"])
                osum = F3[:].rearrange("q (c t) -> q c t", t=128)
                P.add("dve", lambda e: e.tensor_tensor(out=osum, in0=o_ps, in1=osum, op=ALU.add),
                      reads=["pb4", "F3"], writes=["F3"])
                osrc, osk = osum, ["F3"]
            yield
            P.add("act", lambda e: e.activation(out=osq[:].rearrange("q (c t) -> q c t", t=128), in_=osrc, func=AF.Square),
                  reads=osk, writes=["Am"])
            yield
            P.add("pe", lambda e: e.matmul(PB[5], lhsT=ones128[:], rhs=osq[:], start=True, stop=True),
                  reads=["Am", "ones128"], writes=["pb5"])
            yield
            P.add("act", lambda e: e.activation(out=F1[:], in_=pb4(5), func=AF.Ln, bias=epscol[:, 0:1], scale=1.0),
                  reads=["pb5", "epscol"], writes=["F1"])
            P.add("act", lambda e: e.activation(out=F1[:], in_=F1[:], func=AF.Exp, scale=-0.5),
                  reads=["F1"], writes=["F1"])
            yield
            P.add("dve", lambda e: e.tensor_tensor(out=F1[:], in0=osrc, in1=F1[:], op=ALU.mult),
                  reads=osk + ["F1"], writes=["F1"])
            P.add("dve", lambda e: e.scalar_tensor_tensor(out=catO[:], in0=F1[:], scalar=glag[:, 0:1], in1=szg[:],
                                                           op0=ALU.mult, op1=ALU.mult),
                  reads=["F1", "glag", "szg"], writes=[catk + "_o"])

        def back1(L, i):
            sample = (i == NT - 1)
            P.tag = ("c", L, i)
            gi = L * NT + i
            uf = ufull[gi % 3]
            srck = "ucat" if sample else f"ufull{gi % 3}"
            accD = accD2[gi % 2]
            ak = f"acc{gi % 2}_"

            def ush(c, j):
                if sample:
                    return ucat[:, c, :, j:j + 8]
                return uf[:, c, j:j + 128]

            def accv(c):
                if sample:
                    return accD[:, c, :].rearrange("q (b r) -> q b r", r=8)
                return accD[:, c, :]

            cv_ps = pb4(7)
            for _ in range(CONV_DELAY):
                yield
            for c in range(4):
                for j in range(NPE):
                    for hl in range(2):
                        r0, r1 = hl * 64, (hl + 1) * 64
                        P.add("pe", lambda e, c=c, j=j, r0=r0, r1=r1: e.matmul(
                            cv_ps[r0:r1, c, :], lhsT=diag2[r0:r1, c, j, :], rhs=ush(c, j)[r0:r1],
                            start=(j == 0), stop=(j == NPE - 1)),
                            reads=[srck, "diag"], writes=["pb7"])
                    if j % CONV_YIELD_TAPS == CONV_YIELD_TAPS - 1:
                        yield
                yield
            for c in range(4):
                P.add("act", lambda e, c=c: e.activation(out=accD[:, c, :], in_=cv_ps[:, c, :], func=AF.Identity,
                                                          bias=convb[:, c:c + 1], scale=1.0),
                      reads=["pb7", "convb"], writes=[f"{ak}{c}"])
            yield
            for j in range(NPE, 31):
                for c in range(4):
                    P.add("dve", lambda e, c=c, j=j: e.scalar_tensor_tensor(
                        out=accv(c), in0=ush(c, j), scalar=convw[:, c, j:j + 1], in1=accv(c),
                        op0=ALU.mult, op1=ALU.add),
                        reads=[srck, "convw", f"{ak}{c}"], writes=[f"{ak}{c}"])
                if (j - NPE) % TAPS_PER_YIELD == TAPS_PER_YIELD - 1:
                    yield

        def back2(L, i):
            last_layer = (L == 1)
            P.tag = ("b", L, i)
            Xi = X[:, i, :]
            xk = f"X{i}"
            gi = L * NT + i
            szc = szc2[gi % 3]
            szck = f"szc{gi % 3}"
            catO = catO3[gi % 3]
            catk = f"catT{gi % 3}"
            xf = accD2[gi % 2]
            acck = [f"acc{gi % 2}_{c}" for c in range(4)]
            for _ in range(B2_DELAY):
                yield
            P.add("act", lambda e: e.copy(out=xb[:], in_=xf[:]), reads=acck, writes=["xb"])
            yield
            mean_ps = pb4(6)
            for c in range(4):
                P.add("pe", lambda e, c=c: e.matmul(mean_ps[:, c, :], lhsT=blk64[:], rhs=xb[:, c, :], start=True, stop=True),
                      reads=["xb", "blk64"], writes=["pb6"])
            yield
            P.add("dve", lambda e: e.tensor_tensor(out=xf[:], in0=xf[:], in1=mean_ps, op=ALU.subtract),
                  reads=acck + ["pb6"], writes=acck)
            yield
            P.add("act", lambda e: e.activation(out=sq[:], in_=xf[:], func=AF.Square), reads=acck, writes=["sq"])
            yield
            var_ps = pb4(6)
            for c in range(4):
                P.add("pe", lambda e, c=c: e.matmul(var_ps[:, c, :], lhsT=blk64[:], rhs=sq[:, c, :], start=True, stop=True),
                      reads=["sq", "blk64"], writes=["pb6"])
            yield
            P.add("act", lambda e: e.activation(out=var_ps, in_=var_ps, func=AF.Ln, bias=epscol[:, 0:1], scale=1.0),
                  reads=["pb6", "epscol"], writes=["pb6"])
            P.add("act", lambda e: e.activation(out=var_ps, in_=var_ps, func=AF.Exp, scale=-0.5),
                  reads=["pb6"], writes=["pb6"])
            yield
            P.add("dve", lambda e: e.tensor_tensor(out=xf[:], in0=xf[:], in1=var_ps, op=ALU.mult),
                  reads=acck + ["pb6"], writes=acck)
            yield
            for c in range(4):
                P.add("act", lambda e, c=c: e.activation(out=cvs[:, c, :], in_=xf[:, c, :], func=AF.Silu,
                                                          bias=cnb[:, c:c + 1], scale=cng[:, c:c + 1]),
                      reads=acck + ["cng", "cnb"], writes=["xb"])
            yield
            pw_ps = pb4(6)
            for oc in range(4):
                for k in range(4):
                    P.add("pe", lambda e, oc=oc, k=k: e.matmul(
                        pw_ps[:, oc, :], lhsT=wpw[:, k, oc * 128:(oc + 1) * 128], rhs=cvs[:, k, :],
                        start=(k == 0), stop=(k == 3)),
                        reads=["xb", "wpw"], writes=["pb6"])
            yield
            for oc in range(4):
                P.add("dve", lambda e, oc=oc: e.scalar_tensor_tensor(
                    out=catC[:, oc, :], in0=pw_ps[:, oc, :], scalar=bpw[:, oc:oc + 1], in1=szc[:, oc, :],
                    op0=ALU.add, op1=ALU.mult),
                    reads=["pb6", "bpw", szck], writes=["catC"])
            yield
            for half in range(2):
                for k in range(8):
                    P.add("pe", lambda e, half=half, k=k: e.matmul(
                        PB[6], lhsT=(catC[:, k, :] if k < 4 else catO[:, k - 4, :]), rhs=wout[:, k, half * 512:(half + 1) * 512],
                        start=(k == 0), stop=(k == 7)),
                        reads=["catC", catk + "_o", "wout"], writes=["pb6"])
                yield
                P.add("dve", lambda e, half=half: e.tensor_tensor(
                    out=X[:, i, half * 512:(half + 1) * 512], in0=X[:, i, half * 512:(half + 1) * 512], in1=PB[6],
                    op=ALU.add),
                    reads=[xk, "pb6"], writes=[xk])
                yield
            if last_layer:
                fc = 2 * NT + i
                P.add("act", lambda e: e.activation(out=sq[:].rearrange("q c t -> q (c t)"), in_=Xi[:, 0:512], func=AF.Square,
                                                     accum_out=ss[:, fc:fc + 1]),
                      reads=[xk, "ss"], writes=["sq", f"ss{fc}a"])
                P.add("act", lambda e: e.activation(out=sq[:].rearrange("q c t -> q (c t)"), in_=Xi[:, 512:1024], func=AF.Square,
                                                     accum_out=rstd[:, fc:fc + 1]),
                      reads=[xk, "ss"], writes=["sq", f"ss{fc}b"])
                P.add("dve", lambda e: e.tensor_tensor(out=ss[:, fc:fc + 1], in0=ss[:, fc:fc + 1], in1=rstd[:, fc:fc + 1],
                                                        op=ALU.add),
                      reads=[f"ss{fc}a", f"ss{fc}b"], writes=[f"ss{fc}"])
                yield
                P.add("act", lambda e: e.activation(out=rstd[:, fc:fc + 1], in_=ss[:, fc:fc + 1], func=AF.Ln,
                                                     bias=epscol[:, 0:1], scale=1.0 / 1024),
                      reads=[f"ss{fc}", "epscol"], writes=[f"rs{fc}"])
                P.add("act", lambda e: e.activation(out=rstd[:, fc:fc + 1], in_=rstd[:, fc:fc + 1], func=AF.Exp, scale=-0.5),
                      reads=[f"rs{fc}"], writes=[f"rs{fc}"])
                yield
                P.add("dve", lambda e: e.scalar_tensor_tensor(out=Xi, in0=Xi, scalar=rstd[:, fc:fc + 1], in1=fgb[:],
                                                               op0=ALU.mult, op1=ALU.mult),
                      reads=[xk, f"rs{fc}", "fgb"], writes=[xk])
                P.add("sp", lambda e: e.dma_start(out=y_d[i * 128:(i + 1) * 128, :], in_=Xi),
                      reads=[xk], writes=[f"o_y{i}"], dma="st")
                outs.append(f"o_y{i}")

        cache_dma(0)
        setup_a(0)
        load_x([0])
        setup_b(0)
        load_x([1])
        setup_c(0)
        load_x(range(2, NT), with_fg=True)
        flags = {("a", 0), ("b", 0), ("c", 0)}
        sched = []
        for L in DBG_LAYERS:
            base = L * (PIPE_T * (NT - 1) + LAYER_GAP)
            for i in range(NT):
                sched.append([base + PIPE_T * i + PIPE_D2, 0, L, i, back2(L, i), ("c", L)])
                sched.append([base + PIPE_T * i, 1, L, i, front(L, i), ("a", L)])
                sched.append([base + PIPE_T * i + PIPE_D1, 2, L, i, back1(L, i), ("b", L)])
            if L == 0:
                sched.append([2, 3, 0, 0, setup_cache(0), ("b", 0)])
        active = []
        conv_done = set()
        rnd = 0
        while sched or active:
            for ent in [e for e in sched if e[0] <= rnd]:
                if ent[5] not in flags:
                    continue
                if ent[1] == 0 and (ent[2], ent[3]) not in conv_done:
                    continue
                sched.remove(ent)
                active.append(ent)
            active.sort(key=lambda e: (e[1], e[2], e[3]))
            for ent in list(active):
                P.tag = ("bfcs"[ent[1]], ent[2], ent[3]) if ent[1] < 3 else None
                try:
                    next(ent[4])
                except StopIteration:
                    active.remove(ent)
                    kind, L, i = ent[1], ent[2], ent[3]
                    if kind == 2:
                        conv_done.add((L, i))
                    if kind == 3 and 1 in DBG_LAYERS:
                        P.tag = None
                        cache_dma(1)
                    if L == 0 and i == NT - 1 and 1 in DBG_LAYERS:
                        P.tag = None
                        if kind == 1:
                            setup_a(1)
                            flags.add(("a", 1))
                            nominal = PIPE_T * (NT - 1) + LAYER_GAP
                            delta = max(0, rnd + 1 - nominal)
                            for e2 in sched:
                                if e2[2] == 1:
                                    e2[0] += delta
                        elif kind == 2:
                            setup_b(1)
                            for _ in setup_cache(1):
                                pass
                            flags.add(("b", 1))
                        elif kind == 0:
                            setup_c(1)
                            flags.add(("c", 1))
            rnd += 1
        P.tag = None
        P.add("sp", lambda e: e.nop(), reads=list(outs))
        P.build(nc, es)
    return nc, P


_CACHE = {}


def kernel(x_prompt, x_sample, cache_conv, state_gla, norm_g, w_in, w_alpha, b_alpha,
           conv_w, conv_b, cn_g, cn_b, w_pw, b_pw, gla_g, w_out, final_g):
    f = lambda a: np.ascontiguousarray(np.asarray(a, dtype=np.float32))
    x_prompt, x_sample, cache_conv, state_gla = f(x_prompt), f(x_sample), f(cache_conv), f(state_gla)
    shared = dict(norm_g=f(norm_g), w_in=f(w_in), w_alpha=f(w_alpha), b_alpha=f(b_alpha), conv_w=f(conv_w),
                  conv_b=f(conv_b), cn_g=f(cn_g), cn_b=f(cn_b), w_pw=f(w_pw), b_pw=f(b_pw), gla_g=f(gla_g),
                  w_out=f(w_out), final_g=f(final_g))
    if "nc" not in _CACHE:
        _CACHE["nc"] = build_nc()[0]
    nc = _CACHE["nc"]
    in_maps = []
    for c in range(NCORES):
        xs = x_sample[16 * c:16 * (c + 1)].reshape(128, 1024)
        m = dict(shared)
        m["x"] = np.ascontiguousarray(np.concatenate([x_prompt[c], xs], axis=0))
        m["cc"] = np.ascontiguousarray(cache_conv[:, 16 * c:16 * (c + 1)])
        m["sg"] = np.ascontiguousarray(state_gla[:, 16 * c:16 * (c + 1)])
        in_maps.append(m)
    res = run_bass_kernel_spmd(nc, in_maps, core_ids=list(range(NCORES)))
    R = res.results
    y_prompt = np.stack([R[c]["y"][:2048] for c in range(NCORES)], axis=0)
    y_sample = np.concatenate([R[c]["y"][2048:].reshape(16, 8, 1024) for c in range(NCORES)], axis=0)
    ncp = np.stack([R[c]["ncp"] for c in range(NCORES)], axis=1)
    nsp = np.stack([R[c]["nsp"] for c in range(NCORES)], axis=1)
    ncs = np.concatenate([R[c]["ncs"] for c in range(NCORES)], axis=1)
    nss = np.concatenate([R[c]["nss"] for c in range(NCORES)], axis=1)
    return (y_prompt.astype(np.float32), y_sample.astype(np.float32), ncp.astype(np.float32),
            nsp.astype(np.float32), ncs.astype(np.float32), nss.astype(np.float32))
```

```python
from contextlib import ExitStack
import numpy as np
import concourse.bass as bass
import concourse.mybir as mybir
from concourse.bass_utils import run_bass_kernel_spmd

F32 = mybir.dt.float32
BF16 = mybir.dt.bfloat16
ALU = mybir.AluOpType
AF = mybir.ActivationFunctionType

NCORES = 8
SG = 2
NT = 17
NPOOL = 0
EPS = 1e-6
DBG_STAGE = 99
DBG_LAYERS = (0, 1)
DBG_TILES = tuple(range(17))
SYNC_SAME_ENGINE_ALL = True
SLACK_F, SLACK_C, SLACK_B, ORDER = 0, 0, 0, "bfc"
TAPS_PER_YIELD = 1
CONV_YIELD_TAPS = 8
CONV_DELAY = 0
PIPE_T, PIPE_D1, PIPE_D2 = 16, 18, 46
LAYER_GAP = 26
CACHE_ROUND = 120
B2_DELAY = 0
NPE = 25
SEM_EPOCH = 1000


class Prog:
    def __init__(self):
        self.ops = []
        self.dma_groups = {}

    WATCH = ("hn", "hT", "lrT", "spb", "NG", "Eq", "Ek", "Ed", "qt", "kt", "kdT", "kd", "v_sb", "szg", "F0", "F1",
             "F3", "Am", "xb", "sq", "catC", "pb0", "pb1", "pb2", "pb3", "pb4", "pb5", "pb6", "pb7",
             "ufull0", "ufull1", "ufull2", "ucat")

    def add(self, eng, fn, reads=(), writes=(), dma=None):
        tag = getattr(self, "tag", None)
        lw = self.__dict__.setdefault("_lw", {})
        if tag is not None:
            tile = tag[1:]
            for r in reads:
                if r in self.WATCH and r in lw and lw[r] is not None and lw[r] != tile:
                    raise RuntimeError(f"pipeline hazard: {tag} reads {r} last written by tile {lw[r]}")
        for w in writes:
            lw[w] = tag[1:] if tag is not None else None
        self.ops.append(dict(eng=eng, fn=fn, reads=tuple(reads), writes=tuple(writes), dma=dma))

    def dma_group(self, name, nsems):
        self.dma_groups[name] = [nsems, 0]

    def build(self, nc, es):
        ops = self.ops
        n = len(ops)
        deps = [set() for _ in range(n)]
        raw = [set() for _ in range(n)]
        dma_sem_of = [None] * n
        last_on_sem = {}
        for i, op in enumerate(ops):
            if op["dma"] is not None:
                g = self.dma_groups[op["dma"]]
                key = (op["dma"], g[1] % g[0])
                g[1] += 1
                dma_sem_of[i] = key
                if key in last_on_sem:
                    deps[i].add(last_on_sem[key])
                last_on_sem[key] = i
        last_w = {}
        readers = {}
        for i, op in enumerate(ops):
            for r in op["reads"]:
                if r in last_w:
                    deps[i].add(last_w[r])
                    raw[i].add(last_w[r])
            for w in op["writes"]:
                if w in last_w:
                    deps[i].add(last_w[w])
                for j in readers.get(w, ()):
                    if j != i:
                        deps[i].add(j)
            for w in op["writes"]:
                last_w[w] = i
                readers[w] = []
            for r in op["reads"]:
                readers.setdefault(r, []).append(i)
        for i, op in enumerate(ops):
            keep = set()
            for j in deps[i]:
                oj = ops[j]
                if oj["dma"] is None and op["dma"] is None and oj["eng"] == op["eng"]:
                    if op["eng"] == "pe":
                        continue
                    if (not SYNC_SAME_ENGINE_ALL) and j not in raw[i]:
                        continue
                keep.add(j)
            deps[i] = keep
        needs_inc = [False] * n
        for i in range(n):
            for j in deps[i]:
                needs_inc[j] = True
        event = [None] * n
        cnt = {}
        for i, op in enumerate(ops):
            if op["dma"] is not None:
                key = ("dma",) + dma_sem_of[i]
                cnt[key] = cnt.get(key, 0) + 16
                event[i] = (key, cnt[key])
            elif needs_inc[i]:
                tot = cnt.get(("tot", op["eng"]), 0)
                cnt[("tot", op["eng"])] = tot + 1
                key = ("eng", op["eng"], tot // SEM_EPOCH)
                cnt[key] = cnt.get(key, 0) + 1
                event[i] = (key, cnt[key])
        cnt = {k: v for k, v in cnt.items() if k[0] != "tot"}
        sem_keys = sorted(set(e[0] for e in event if e is not None), key=str)
        sems = {}
        for k in sem_keys:
            sems[k] = es.enter_context(nc.semaphore("s_" + "_".join(str(x) for x in k)))
        self.n_sems = len(sem_keys)
        self.max_sem = max(cnt.values()) if cnt else 0
        per_eng = {}
        seen = {}
        for i, op in enumerate(ops):
            e = op["eng"]
            waits = {}
            for j in deps[i]:
                k, v = event[j]
                if v > waits.get(k, 0):
                    waits[k] = v
            wl = []
            for k, v in waits.items():
                if seen.get((e, k), 0) >= v:
                    continue
                seen[(e, k)] = v
                wl.append((k, v))
            per_eng.setdefault(e, []).append((i, wl))

        def emit(engname, engobj):
            for i, wl in per_eng.get(engname, []):
                for k, v in wl:
                    engobj.wait_ge(sems[k], v)
                ins = ops[i]["fn"](engobj)
                ev = event[i]
                if ev is not None:
                    ins.then_inc(sems[ev[0]], 16 if ev[0][0] == "dma" else 1)

        with nc.Block() as block:
            @block.tensor
            def _(eng):
                emit("pe", eng)

            @block.scalar
            def _(eng):
                emit("act", eng)

            @block.vector
            def _(eng):
                emit("dve", eng)

            @block.gpsimd
            def _(eng):
                emit("pool", eng)

            @block.sync
            def _(eng):
                emit("sp", eng)


C_A, C_GL, C_ZC, C_Q, C_K, C_V, C_ZG, C_LR = 0, 512, 1024, 1536, 1792, 2048, 2560, 3072
WIN_GROUPS = [("a", 0, 512), ("gl", 512, 1024), ("zc", 1024, 1536), ("q", 1536, 1792),
              ("k", 1792, 2048), ("v", 2048, 2560), ("zg", 2560, 3072), ("lr", 3072, 3088)]


def build_nc(dbg=False):
    nc = bass.Bass("TRN2", target_bir_lowering=False)
    din = lambda name, shape: nc.dram_tensor(name, shape, F32, kind="ExternalInput").ap()
    dout = lambda name, shape: nc.dram_tensor(name, shape, F32, kind="ExternalOutput").ap()
    x_d = din("x", [NT * 128, 1024])
    cc_d = din("cc", [2, 16, 30, 512])
    sg_d = din("sg", [2, 16, 4, 64, 128])
    norm_g = din("norm_g", [2, 1024])
    w_in = din("w_in", [2, 1024, 3088])
    w_alpha = din("w_alpha", [2, 16, 256])
    b_alpha = din("b_alpha", [2, 256])
    conv_w = din("conv_w", [2, 31, 512])
    conv_b = din("conv_b", [2, 512])
    cn_g = din("cn_g", [2, 512])
    cn_b = din("cn_b", [2, 512])
    w_pw = din("w_pw", [2, 512, 512])
    b_pw = din("b_pw", [2, 512])
    gla_g = din("gla_g", [2, 128])
    w_out = din("w_out", [2, 1024, 1024])
    final_g = din("final_g", [1024])
    y_d = dout("y", [NT * 128, 1024])
    ncp_d = dout("ncp", [2, 30, 512])
    nsp_d = dout("nsp", [2, 4, 64, 128])
    ncs_d = dout("ncs", [2, 16, 30, 512])
    nss_d = dout("nss", [2, 16, 4, 64, 128])

    P = Prog()
    P.dma_group("xld", 6)
    P.dma_group("wld", 6)
    P.dma_group("pld", 4)
    P.dma_group("st", 6)
    P.dma_group("sst", 4)
    P.dma_group("sld", 4)
    P.dma_group("cld", 2)
    P.dma_group("cst", 4)
    outs = []

    with ExitStack() as es:
        def sb(name, shape, dt):
            return es.enter_context(nc.sbuf_tensor(name, shape, dt))

        X = sb("X", [128, NT, 1024], F32)
        win = sb("win", [128, 8, 3088], BF16)
        wpw = sb("wpw", [128, 4, 512], BF16)
        wout = sb("wout", [128, 8, 1024], BF16)
        walpha = sb("walpha", [16, 256], BF16)
        fgb = sb("fgb", [128, 1024], F32)
        gcol = sb("gcol", [128, 8], F32)
        convb = sb("convb", [128, 4], F32)
        cng = sb("cng", [128, 4], F32)
        cnb = sb("cnb", [128, 4], F32)
        bpw = sb("bpw", [128, 4], F32)
        glag = sb("glag", [128, 1], F32)
        balpha = sb("balpha", [128, 2], F32)
        nb = sb("nb", [128, 2], F32)
        convw = sb("convw", [128, 4, 31], F32)
        craw = sb("craw", [31, 512], F32)
        ident = sb("ident", [128, 128], BF16)
        identf = sb("identf", [128, 128], F32)
        tri = sb("tri", [128, 128], BF16)
        smask = sb("smask", [128, 128], BF16)
        blk64 = sb("blk64", [128, 128], BF16)
        ones128 = sb("ones128", [128, 128], BF16)
        onesf = sb("onesf", [128, 128], F32)
        epscol = sb("epscol", [128, 1], F32)
        halfcol = sb("halfcol", [128, 1], F32)
        selm = sb("selm", [128, 16], F32)
        ss = sb("ss", [128, 64], F32)
        rstd = sb("rstd", [128, 64], F32)
        nend = sb("nend", [128, 2], F32)
        hn = sb("hn", [128, 1024], BF16)
        hT = sb("hT", [128, 8, 128], BF16)
        lrT = sb("lrT", [16, 128], BF16)
        spb = sb("spb", [128, 2, 128], F32)
        NG = sb("NG", [128, 2, 128], F32)
        Eq = sb("Eq", [128, 2, 128], F32)
        Ek = sb("Ek", [128, 2, 128], F32)
        Ed = sb("Ed", [128, 2, 128], F32)
        qt = sb("qt", [128, 2, 128], BF16)
        kt = sb("kt", [128, 2, 128], BF16)
        kdT = sb("kdT", [128, 2, 128], BF16)
        kd = sb("kd", [128, 256], BF16)
        v_sb = sb("v_sb", [128, 512], BF16)
        szg = sb("szg", [128, 4, 128], BF16)
        AmB = sb("AmB", [128, 512], BF16)
        Am = AmB[:].rearrange("q (c t) -> q c t", t=128)
        osq = AmB
        kdm = AmB[:].rearrange("q (s c) -> q s c", c=256)
        S = sb("S", [128, 2, 128], F32)
        Sbf = sb("Sbf", [128, 2, 128], BF16)
        F0 = sb("F0", [128, 4, 128], F32)
        F1 = sb("F1", [128, 4, 128], F32)
        F3 = sb("F3", [128, 512], F32)
        cf = F3[:, 0:128]
        S0 = F1[:].rearrange("q (p s) v -> q p s v", s=SG)
        accD2 = [sb(f"accD{s}", [128, 4, 128], F32) for s in range(2)]
        ufull = [sb(f"ufull{s}", [128, 4, 158], BF16) for s in range(3)]
        szc2 = [sb(f"szc{s}", [128, 4, 128], BF16) for s in range(3)]
        diag2 = sb("diag2", [128, 4, NPE, 64], BF16)
        I2 = sb("I2", [128, 64], BF16)
        xb = sb("xb", [128, 4, 128], BF16)
        sq = sb("sq", [128, 4, 128], BF16)
        cvs = xb
        catO3 = [sb(f"catO{s}", [128, 4, 128], BF16) for s in range(3)]
        catC = sb("catC", [128, 4, 128], BF16)
        S0bf = sb("S0bf", [128, 2, SG, 128], BF16)
        ucat = sb("ucat", [128, 4, 16, 38], BF16)
        cstage = sb("cstage", [120, 4, 512], BF16)

        psum = es.enter_context(nc.psum_tensor("psum", [128, 8 * 512], F32))
        PB = [psum[:, b * 512:(b + 1) * 512] for b in range(8)]
        psum_b = psum.bitcast(BF16)
        PBb = [psum_b[:, b * 1024:(b + 1) * 1024] for b in range(8)]

        def pb4(b):
            return PB[b].rearrange("q (c t) -> q c t", t=128)

        P.add("pool", lambda e: e.memset(cf, 0.0), writes=["F3"])
        P.add("pool", lambda e: e.affine_select(out=cf, in_=cf, pattern=[[-1, 128]],
                                                 compare_op=ALU.not_equal, fill=1.0, base=0,
                                                 channel_multiplier=1), reads=["F3"], writes=["F3"])
        P.add("dve", lambda e: e.tensor_copy(out=identf[:], in_=cf), reads=["F3"], writes=["identf"])
        P.add("dve", lambda e: e.tensor_copy(out=ident[:], in_=cf), reads=["F3"], writes=["ident"])
        P.add("dve", lambda e: e.tensor_tensor(out=I2[:], in0=ident[:, 0:64], in1=ident[:, 64:128], op=ALU.add),
              reads=["ident"], writes=["I2"])
        P.add("pool", lambda e: e.memset(cf, 1.0), reads=["F3"], writes=["F3"])
        P.add("pool", lambda e: e.affine_select(out=cf, in_=cf, pattern=[[1, 128]],
                                                 compare_op=ALU.is_ge, fill=0.0, base=0,
                                                 channel_multiplier=-1), reads=["F3"], writes=["F3"])
        P.add("dve", lambda e: e.tensor_copy(out=tri[:], in_=cf), reads=["F3"], writes=["tri"])
        cf3 = cf.rearrange("q (a b) -> q a b", b=8)
        P.add("pool", lambda e: e.affine_select(out=cf3, in_=cf3, pattern=[[-8, 16], [0, 8]],
                                                 compare_op=ALU.is_ge, fill=0.0, base=0,
                                                 channel_multiplier=1), reads=["F3"], writes=["F3"])
        P.add("pool", lambda e: e.affine_select(out=cf3, in_=cf3, pattern=[[8, 16], [0, 8]],
                                                 compare_op=ALU.is_ge, fill=0.0, base=7,
                                                 channel_multiplier=-1), reads=["F3"], writes=["F3"])
        P.add("dve", lambda e: e.tensor_copy(out=smask[:], in_=cf), reads=["F3"], writes=["smask"])
        P.add("pool", lambda e: e.memset(selm[:], 1.0), writes=["selm"])
        P.add("pool", lambda e: e.affine_select(out=selm[:], in_=selm[:], pattern=[[-8, 16]],
                                                 compare_op=ALU.is_ge, fill=0.0, base=0,
                                                 channel_multiplier=1), reads=["selm"], writes=["selm"])
        P.add("pool", lambda e: e.affine_select(out=selm[:], in_=selm[:], pattern=[[8, 16]],
                                                 compare_op=ALU.is_ge, fill=0.0, base=7,
                                                 channel_multiplier=-1), reads=["selm"], writes=["selm"])
        P.add("dve", lambda e: e.memset(blk64[:], 0.0), writes=["blk64"])
        P.add("dve", lambda e: e.memset(blk64[0:64, 0:64], 1.0 / 64), writes=["blk64"])
        P.add("dve", lambda e: e.memset(blk64[64:128, 64:128], 1.0 / 64), writes=["blk64"])
        P.add("dve", lambda e: e.memset(ones128[:], 1.0 / 128), writes=["ones128"])
        P.add("dve", lambda e: e.memset(onesf[:], 1.0), writes=["onesf"])
        P.add("dve", lambda e: e.memset(epscol[:], EPS), writes=["epscol"])
        P.add("dve", lambda e: e.memset(halfcol[:], 0.5), writes=["halfcol"])
        P.add("dve", lambda e: e.memset(ss[:], 0.0), writes=["ss"])
        for s in range(3):
            P.add("dve", lambda e, s=s: e.memset(ufull[s][:], 0.0), writes=[f"ufull{s}"])

        def load_x(tiles, with_fg=False, after=()):
            for i in tiles:
                P.add("sp", lambda e, i=i: e.dma_start(out=X[:, i, :], in_=x_d[i * 128:(i + 1) * 128, :]),
                      reads=list(after), writes=[f"X{i}"], dma="xld")
            if with_fg:
                P.add("sp", lambda e: e.dma_start(out=fgb[:], in_=final_g.partition_broadcast(128)),
                      writes=["fgb"], dma="pld")

        def small_col(dst, src_ap, pat, key, **kw):
            P.add("sp", lambda e: e.dma_start(out=dst, in_=src_ap.rearrange(pat, **kw),
                                              allow_slow_non_contiguous=True),
                  writes=[key], dma="pld")

        def setup_a(L):
            small_col(gcol[:], norm_g[L], "(k q) -> q k", "gcol", q=128)
            small_col(balpha[:], b_alpha[L], "(k q) -> q k", "balpha", q=128)
            small_col(glag[:], gla_g[L], "(q o) -> q o", "glag", o=1)
            P.add("dve", lambda e: e.tensor_scalar(out=nb[:], in0=balpha[:], scalar1=-1.0, scalar2=None, op0=ALU.mult),
                  reads=["balpha"], writes=["nb"])
            win_v = w_in[L].rearrange("(k q) e -> q k e", q=128)
            order = ["lr", "q", "k", "v", "zg", "a", "gl", "zc"]
            grp = {g: (c0, c1) for (g, c0, c1) in WIN_GROUPS}
            for idx, g in enumerate(order):
                c0, c1 = grp[g]
                P.add("pool", lambda e, c0=c0, c1=c1: e.dma_start(out=win[:, :, c0:c1], in_=win_v[:, :, c0:c1]),
                      writes=[f"win_{g}"], dma="wld")
                if idx == 0:
                    P.add("pool", lambda e: e.dma_start(out=walpha[:], in_=w_alpha[L]), writes=["walpha"], dma="wld")
            for g in order:
                c0, c1 = grp[g]
                P.add("dve", lambda e, c0=c0, c1=c1: e.tensor_tensor(
                    out=win[:, :, c0:c1], in0=win[:, :, c0:c1],
                    in1=gcol[:].unsqueeze(2).to_broadcast([128, 8, c1 - c0]), op=ALU.mult),
                    reads=[f"win_{g}", "gcol"], writes=[f"win_{g}"])

        def setup_b(L):
            small_col(convb[:], conv_b[L], "(k q) -> q k", "convb", q=128)
            P.add("sp", lambda e: e.dma_start(out=craw[:], in_=conv_w[L]), writes=["craw"], dma="pld")
            cw_ps = PB[7][:, 0:124].rearrange("q (c j) -> q c j", j=31)
            for c in range(4):
                P.add("pe", lambda e, c=c: e.transpose(out=cw_ps[:, c, :], in_=craw[0:31, c * 128:(c + 1) * 128],
                                                        identity=identf[0:31, 0:31]),
                      reads=["craw", "identf"], writes=["pb7"])
            P.add("act", lambda e: e.copy(out=convw[:], in_=cw_ps), reads=["pb7"], writes=["convw"])
            for c in range(4):
                P.add("dve", lambda e, c=c: e.tensor_tensor(
                    out=diag2[:, c], in0=I2[:].unsqueeze(1).to_broadcast([128, NPE, 64]),
                    in1=convw[:, c, 0:NPE].unsqueeze(2).to_broadcast([128, NPE, 64]), op=ALU.mult),
                    reads=["I2", "convw"], writes=["diag"])

        def setup_c(L):
            small_col(cng[:], cn_g[L], "(k q) -> q k", "cng", q=128)
            small_col(cnb[:], cn_b[L], "(k q) -> q k", "cnb", q=128)
            small_col(bpw[:], b_pw[L], "(k q) -> q k", "bpw", q=128)
            P.add("pool", lambda e: e.dma_start(out=wpw[:], in_=w_pw[L].rearrange("(k q) e -> q k e", q=128)),
                  writes=["wpw"], dma="wld")
            P.add("pool", lambda e: e.dma_start(out=wout[:], in_=w_out[L].rearrange("(k q) e -> q k e", q=128)),
                  writes=["wout"], dma="wld")

        def cache_dma(L):
            for g in range(4):
                P.add("pool", lambda e, g=g: e.dma_start(
                    out=cstage[:, g, :], in_=cc_d[L, 4 * g:4 * g + 4].rearrange("b r c -> (b r) c")),
                    writes=["cstage"], dma="cld")
            P.add("sp", lambda e: e.dma_start(out=ncs_d[L, :, 0:22, :], in_=cc_d[L, :, 8:30, :]),
                  writes=[f"o_ncs_old{L}"], dma="st")
            outs.append(f"o_ncs_old{L}")

        def setup_cache(L):
            for g in range(4):
                tp = PBb[7][:, 0:480].rearrange("q (c r) -> q c r", r=120)
                for c in range(4):
                    P.add("pe", lambda e, g=g, c=c, tp=tp: e.transpose(
                        out=tp[:, c, :], in_=cstage[0:120, g, c * 128:(c + 1) * 128], identity=ident[0:120, 0:120]),
                        reads=["cstage", "ident"], writes=["pb7"])
                yield
                P.add("act", lambda e, g=g, tp=tp: e.copy(
                    out=ucat[:, :, 4 * g:4 * g + 4, 0:30],
                    in_=tp.rearrange("q c (b r) -> q c b r", r=30)),
                    reads=["pb7"], writes=["ucat"])
                yield

        def front(L, i):
            P.tag = ("t", L, i)
            sample = (i == NT - 1)
            Xi = X[:, i, :]
            xk = f"X{i}"
            col = L * NT + i
            gi = L * NT + i
            szc = szc2[gi % 3]
            szck = f"szc{gi % 3}"
            catO = catO3[gi % 3]
            catk = f"catT{gi % 3}"
            P.add("act", lambda e: e.activation(out=hn[:], in_=Xi, func=AF.Square, accum_out=ss[:, col:col + 1]),
                  reads=[xk, "ss"], writes=["hn", f"ss{col}"])
            P.add("act", lambda e: e.activation(out=rstd[:, col:col + 1], in_=ss[:, col:col + 1], func=AF.Ln,
                                                 bias=epscol[:, 0:1], scale=1.0 / 1024),
                  reads=[f"ss{col}", "epscol"], writes=[f"rs{col}"])
            P.add("act", lambda e: e.activation(out=rstd[:, col:col + 1], in_=rstd[:, col:col + 1], func=AF.Exp,
                                                 scale=-0.5),
                  reads=[f"rs{col}"], writes=[f"rs{col}"])
            P.add("act", lambda e: e.activation(out=hn[:], in_=Xi, func=AF.Copy, scale=rstd[:, col:col + 1]),
                  reads=[xk, f"rs{col}"], writes=["hn"])
            hT_ps = PBb[0].rearrange("q (k t) -> q k t", t=128)
            for k in range(8):
                P.add("pe", lambda e, k=k: e.transpose(out=hT_ps[:, k, :], in_=hn[:, k * 128:(k + 1) * 128],
                                                        identity=ident[:]),
                      reads=["hn", "ident"], writes=["pb0"])
            yield
            P.add("act", lambda e: e.copy(out=hT[:], in_=hT_ps), reads=["pb0"], writes=["hT"])
            yield

            def proj_fm(bank, c0, nchunks, *wkeys):
                o = pb4(bank)
                for c in range(nchunks):
                    for k in range(8):
                        P.add("pe", lambda e, c=c, k=k: e.matmul(
                            o[:, c, :], lhsT=win[:, k, c0 + c * 128:c0 + (c + 1) * 128], rhs=hT[:, k, :],
                            start=(k == 0), stop=(k == 7)),
                            reads=["hT"] + list(wkeys), writes=[f"pb{bank}"])

            for k in range(8):
                P.add("pe", lambda e, k=k: e.matmul(PB[1][0:16, 0:128], lhsT=win[:, k, C_LR:C_LR + 16], rhs=hT[:, k, :],
                                                     start=(k == 0), stop=(k == 7)),
                      reads=["hT", "win_lr"], writes=["pb1"])
            P.add("act", lambda e: e.copy(out=lrT[:], in_=PB[1][0:16, 0:128]), reads=["pb1"], writes=["lrT"])
            z_ps = PB[1][:, 128:384].rearrange("q (c t) -> q c t", t=128)
            for p in range(2):
                P.add("pe", lambda e, p=p: e.matmul(z_ps[:, p, :], lhsT=walpha[0:16, p * 128:(p + 1) * 128],
                                                     rhs=lrT[0:16, :], start=True, stop=True),
                      reads=["lrT", "walpha"], writes=["pb1"])
            yield
            for p in range(2):
                P.add("act", lambda e, p=p: e.activation(out=spb[:, p, :], in_=z_ps[:, p, :], func=AF.Exp,
                                                          bias=nb[:, p:p + 1], scale=-1.0),
                      reads=["pb1", "nb"], writes=["spb"])
            proj_fm(2, C_Q, 4, "win_q", "win_k")
            yield
            P.add("act", lambda e: e.activation(out=spb[:], in_=spb[:], func=AF.Ln, bias=1.0, scale=1.0),
                  reads=["spb"], writes=["spb"])
            for k in range(8):
                P.add("pe", lambda e, k=k: e.matmul(PB[0], lhsT=hT[:, k, :], rhs=win[:, k, C_V:C_V + 512],
                                                     start=(k == 0), stop=(k == 7)),
                      reads=["hT", "win_v"], writes=["pb0"])
            yield
            for p in range(2):
                P.add("dve", lambda e, p=p: e.tensor_tensor_scan(out=NG[:, p, :], data0=onesf[:], data1=spb[:, p, :],
                                                                  initial=0.0, op0=ALU.mult, op1=ALU.add),
                      reads=["spb", "onesf"], writes=["NG"])
            proj_fm(1, C_ZG, 4, "win_zg")
            yield
            P.add("act", lambda e: e.copy(out=v_sb[:], in_=PB[0]), reads=["pb0"], writes=["v_sb"])
            if not sample:
                P.add("act", lambda e: e.activation(out=Eq[:], in_=NG[:], func=AF.Exp, scale=-1.0 / 16),
                      reads=["NG"], writes=["Eq"])
                P.add("act", lambda e: e.activation(out=Ek[:], in_=NG[:], func=AF.Exp, scale=1.0 / 16),
                      reads=["NG"], writes=["Ek"])
                P.add("dve", lambda e: e.tensor_scalar(out=nend[:].unsqueeze(2), in0=NG[:, :, 127:128], scalar1=-1.0 / 16,
                                                        scalar2=None, op0=ALU.mult),
                      reads=["NG"], writes=["nend"])
                for p in range(2):
                    P.add("act", lambda e, p=p: e.activation(out=Ed[:, p, :], in_=NG[:, p, :], func=AF.Exp,
                                                              bias=nend[:, p:p + 1], scale=1.0 / 16),
                          reads=["NG", "nend"], writes=["Ed"])
            else:
                for p in range(2):
                    NGp = NG[:, p, :].rearrange("q (a b) -> q a b", b=8)
                    Dp = spb[:, p, :].rearrange("q (a b) -> q a b", b=8)
                    D2p = Ed[:, p, :].rearrange("q (a b) -> q a b", b=8)
                    P.add("dve", lambda e, NGp=NGp, Dp=Dp: e.tensor_copy(out=Dp[:, 0:1, :], in_=NGp[:, 0:1, :]),
                          reads=["NG"], writes=["spb"])
                    P.add("dve", lambda e, NGp=NGp, Dp=Dp: e.tensor_tensor(
                        out=Dp[:, 1:16, :], in0=NGp[:, 1:16, :],
                        in1=NGp[:, 0:15, 7:8].to_broadcast([128, 15, 8]), op=ALU.subtract),
                        reads=["NG"], writes=["spb"])
                    P.add("dve", lambda e, NGp=NGp, D2p=D2p: e.tensor_tensor(
                        out=D2p, in0=NGp, in1=NGp[:, :, 7:8].to_broadcast([128, 16, 8]), op=ALU.subtract),
                        reads=["NG"], writes=["Ed"])
                P.add("act", lambda e: e.activation(out=Eq[:], in_=spb[:], func=AF.Exp, scale=-1.0 / 16),
                      reads=["spb"], writes=["Eq"])
                P.add("act", lambda e: e.activation(out=Ek[:], in_=spb[:], func=AF.Exp, scale=1.0 / 16),
                      reads=["spb"], writes=["Ek"])
                P.add("act", lambda e: e.activation(out=Ed[:], in_=Ed[:], func=AF.Exp, scale=1.0 / 16),
                      reads=["Ed"], writes=["Ed"])
            proj_fm(3, C_A, 4, "win_a")
            yield
            qk = pb4(2)
            P.add("dve", lambda e: e.scalar_tensor_tensor(out=qt[:], in0=qk[:, 0:2, :], scalar=0.125, in1=Eq[:],
                                                           op0=ALU.mult, op1=ALU.mult),
                  reads=["pb2", "Eq"], writes=["qt"])
            P.add("dve", lambda e: e.tensor_tensor(out=kt[:], in0=qk[:, 2:4, :], in1=Ek[:], op=ALU.mult),
                  reads=["pb2", "Ek"], writes=["kt"])
            P.add("dve", lambda e: e.tensor_tensor(out=kdT[:], in0=qk[:, 2:4, :], in1=Ed[:], op=ALU.mult),
                  reads=["pb2", "Ed"], writes=["kdT"])
            proj_fm(0, C_GL, 4, "win_gl")
            yield
            P.add("act", lambda e: e.activation(out=szg[:], in_=pb4(1), func=AF.Silu), reads=["pb1"], writes=["szg"])
            yield
            proj_fm(1, C_ZC, 4, "win_zc")
            P.add("act", lambda e: e.activation(out=F0[:], in_=pb4(0), func=AF.Tanh, scale=0.5), reads=["pb0"], writes=["F0"])
            P.add("act", lambda e: e.activation(out=F0[:], in_=F0[:], func=AF.Identity, bias=halfcol[:, 0:1], scale=0.5),
                  reads=["F0", "halfcol"], writes=["F0"])
            yield
            uf = ufull[gi % 3]
            ufk = f"ufull{gi % 3}"
            ufo = ufull[(gi + 1) % 3]
            ufok = f"ufull{(gi + 1) % 3}"
            if i == 0:
                P.add("dve", lambda e: e.memset(uf[:, :, 0:30], 0.0), writes=[ufk])
            if not sample:
                P.add("dve", lambda e: e.tensor_tensor(out=uf[:, :, 30:158], in0=pb4(3), in1=F0[:], op=ALU.mult),
                      reads=["pb3", "F0"], writes=[ufk])
                if i < NT - 2:
                    P.add("act", lambda e: e.copy(out=ufo[:, :, 0:30], in_=uf[:, :, 128:158]),
                          reads=[ufk], writes=[ufok])
            else:
                P.add("dve", lambda e: e.tensor_tensor(
                    out=ucat[:, :, :, 30:38], in0=pb4(3).rearrange("q c (b r) -> q c b r", r=8),
                    in1=F0[:].rearrange("q c (b r) -> q c b r", r=8), op=ALU.mult),
                    reads=["pb3", "F0"], writes=["ucat"])
            need_u32 = sample or (i == NT - 2)
            if need_u32:
                P.add("dve", lambda e: e.tensor_tensor(out=F0[:], in0=pb4(3), in1=F0[:], op=ALU.mult),
                      reads=["pb3", "F0"], writes=["F0"])
            P.add("act", lambda e: e.activation(out=szc[:], in_=pb4(1), func=AF.Silu), reads=["pb1"], writes=[szck])
            yield
            if need_u32:
                uT_ps = PB[2]
                for c in range(4):
                    P.add("pe", lambda e, c=c: e.transpose(out=uT_ps[:, c * 128:(c + 1) * 128], in_=F0[:, c, :],
                                                            identity=identf[:]),
                          reads=["F0", "identf"], writes=["pb2"])
                P.add("act", lambda e: e.copy(out=F3[:], in_=uT_ps), reads=["pb2"], writes=["F3"])
                if sample:
                    for b in range(16):
                        P.add("pool", lambda e, b=b: e.dma_start(out=ncs_d[L, b, 22:30, :], in_=F3[b * 8:(b + 1) * 8, :]),
                              reads=["F3"], writes=[f"o_ncs{L}_{b}"], dma="cst")
                        outs.append(f"o_ncs{L}_{b}")
                else:
                    P.add("sp", lambda e: e.dma_start(out=ncp_d[L], in_=F3[98:128, :]),
                          reads=["F3"], writes=[f"o_ncp{L}"], dma="st")
                    outs.append(f"o_ncp{L}")
                yield

            kd_ps = PBb[4][:, 0:256]
            for p in range(2):
                P.add("pe", lambda e, p=p: e.transpose(out=kd_ps[:, p * 128:(p + 1) * 128], in_=kdT[:, p, :],
                                                        identity=ident[:]),
                      reads=["kdT", "ident"], writes=["pb4"])
            yield
            P.add("act", lambda e: e.copy(out=kd[:], in_=kd_ps), reads=["pb4"], writes=["kd"])
            yield
            A_ps = [pb4(4), pb4(5)]
            for h in (0, 2, 1, 3):
                p, hl = h // 2, h % 2
                P.add("pe", lambda e, h=h, p=p, hl=hl: e.matmul(
                    A_ps[hl][:, p, :], lhsT=kt[hl * 64:(hl + 1) * 64, p, :], rhs=qt[hl * 64:(hl + 1) * 64, p, :],
                    start=True, stop=True),
                    reads=["kt", "qt"], writes=[f"pb{4 if hl == 0 else 5}"])
            yield
            mk = smask if sample else tri
            mkk = "smask" if sample else "tri"
            Am_v = Am.rearrange("q (p hl) t -> q hl p t", hl=2)
            for hl in range(2):
                P.add("dve", lambda e, hl=hl: e.tensor_tensor(
                    out=Am_v[:, hl], in0=A_ps[hl][:, 0:2, :],
                    in1=mk[:].unsqueeze(1).to_broadcast([128, 2, 128]), op=ALU.mult),
                    reads=[f"pb{4 if hl == 0 else 5}", mkk], writes=["Am"])
            yield
            o_ps = pb4(4)
            first = (i == 0)
            for h in range(4):
                p, hl = h // 2, h % 2
                use_S = (not sample) and (not first)
                if use_S:
                    P.add("pe", lambda e, h=h, p=p, hl=hl: e.matmul(
                        o_ps[:, h, :], lhsT=Sbf[hl * 64:(hl + 1) * 64, p, :], rhs=qt[hl * 64:(hl + 1) * 64, p, :],
                        start=True, stop=False),
                        reads=["Sbf", "qt"], writes=["pb4"])
                P.add("pe", lambda e, h=h, use_S=use_S: e.matmul(
                    o_ps[:, h, :], lhsT=v_sb[:, h * 128:(h + 1) * 128], rhs=Am[:, h, :],
                    start=(not use_S), stop=True),
                    reads=["v_sb", "Am"], writes=["pb4"])
            yield
            if not sample:
                Pp = PB[5][:, 0:256].rearrange("q (c v) -> q c v", v=128)
                for h in range(4):
                    p, hl = h // 2, h % 2
                    P.add("pe", lambda e, h=h, p=p, hl=hl: e.matmul(
                        Pp[hl * 64:(hl + 1) * 64, p, :], lhsT=kd[:, h * 64:(h + 1) * 64],
                        rhs=v_sb[:, h * 128:(h + 1) * 128], start=True, stop=True),
                        reads=["kd", "v_sb"], writes=["pb5"])
                yield
                if first:
                    P.add("dve", lambda e: e.tensor_copy(out=S[:], in_=Pp), reads=["pb5"], writes=["S"])
                else:
                    P.add("dve", lambda e: e.tensor_tensor(out=S[:], in0=S[:],
                                                            in1=Eq[:, :, 127:128].to_broadcast([128, 2, 128]),
                                                            op=ALU.mult),
                          reads=["S", "Eq"], writes=["S"])
                    P.add("dve", lambda e: e.tensor_tensor(out=S[:], in0=S[:], in1=Pp, op=ALU.add),
                          reads=["S", "pb5"], writes=["S"])
                if i < NT - 2:
                    P.add("act", lambda e: e.copy(out=Sbf[:], in_=S[:]), reads=["S"], writes=["Sbf"])
                else:
                    P.add("sp", lambda e: e.dma_start(
                        out=nsp_d[L].rearrange("(p hl) d v -> (hl d) p v", hl=2), in_=S[:]),
                        reads=["S"], writes=[f"o_nsp{L}"], dma="st")
                    outs.append(f"o_nsp{L}")
                osrc, osk = o_ps, ["pb4"]
            else:
                oi_ps = [pb4(2), pb4(3)]
                oik = ["pb2", "pb3"]
                Eq4 = Eq[:].rearrange("q p (b r) -> q p b r", r=8)
                sview = "s hl d v -> (hl d) s v"
                for g in range(16 // SG):
                    for p in range(2):
                        P.add("sp", lambda e, g=g, p=p: e.dma_start(
                            out=S0[:, p], in_=sg_d[L, SG * g:SG * g + SG, 2 * p:2 * p + 2].rearrange(sview)),
                            writes=["F1"], dma="sld")
                    P.add("act", lambda e: e.copy(out=S0bf[:], in_=S0), reads=["F1"], writes=["S0bf"])
                    for s in range(SG):
                        seq = SG * g + s
                        P.add("dve", lambda e, s=s, seq=seq: e.tensor_scalar(
                            out=kdm[:, s, :], in0=kd[:], scalar1=selm[:, seq:seq + 1], scalar2=None, op0=ALU.mult),
                            reads=["kd", "selm"], writes=["Am"])
                    Pp = PB[5][:, 0:2 * SG * 128].rearrange("q (p s v) -> q p s v", s=SG, v=128)
                    for s in range(SG):
                        seq = SG * g + s
                        for h in (0, 2, 1, 3):
                            p, hl = h // 2, h % 2
                            P.add("pe", lambda e, s=s, seq=seq, h=h, p=p, hl=hl: e.matmul(
                                oi_ps[hl][:, p, seq * 8:(seq + 1) * 8], lhsT=S0bf[hl * 64:(hl + 1) * 64, p, s, :],
                                rhs=qt[hl * 64:(hl + 1) * 64, p, seq * 8:(seq + 1) * 8], start=True, stop=True),
                                reads=["S0bf", "qt"], writes=[oik[hl]])
                        for h in range(4):
                            p, hl = h // 2, h % 2
                            P.add("pe", lambda e, s=s, h=h, p=p, hl=hl: e.matmul(
                                Pp[hl * 64:(hl + 1) * 64, p, s, :], lhsT=kdm[:, s, h * 64:(h + 1) * 64],
                                rhs=v_sb[:, h * 128:(h + 1) * 128], start=True, stop=True),
                                reads=["Am", "v_sb"], writes=["pb5"])
                    Sn = F3[:].rearrange("q (p s v) -> q p s v", s=SG, v=128)
                    P.add("dve", lambda e, g=g: e.tensor_tensor(
                        out=Sn, in0=S0, in1=Eq4[:, :, SG * g:SG * g + SG, 7:8].to_broadcast([128, 2, SG, 128]),
                        op=ALU.mult), reads=["F1", "Eq", "S0bf"], writes=["F3"])
                    P.add("dve", lambda e: e.tensor_tensor(out=Sn, in0=Sn, in1=Pp, op=ALU.add),
                          reads=["F3", "pb5"], writes=["F3"])
                    for p in range(2):
                        P.add("pool", lambda e, g=g, p=p: e.dma_start(
                            out=nss_d[L, SG * g:SG * g + SG, 2 * p:2 * p + 2].rearrange(sview), in_=Sn[:, p]),
                            reads=["F3"], writes=[f"o_nss{L}_{g}_{p}"], dma="sst")
                        outs.append(f"o_nss{L}_{g}_{p}")
                    yield
                F3_v = F3[:].rearrange("q (p hl t) -> q hl p t", hl=2, t=128)
                for hl in range(2):
                    P.add("act", lambda e, hl=hl: e.copy(out=F3_v[:, hl], in_=oi_ps[hl][:, 0:2, :]),
                          reads=[oik[hl]], writes=["F3"])
                osum = F3[:].rearrange("q (c t) -> q c t", t=128)
                P.add("dve", lambda e: e.tensor_tensor(out=osum, in0=o_ps, in1=osum, op=ALU.add),
                      reads=["pb4", "F3"], writes=["F3"])
                osrc, osk = osum, ["F3"]
            yield
            P.add("act", lambda e: e.activation(out=osq[:].rearrange("q (c t) -> q c t", t=128), in_=osrc, func=AF.Square),
                  reads=osk, writes=["Am"])
            yield
            P.add("pe", lambda e: e.matmul(PB[5], lhsT=ones128[:], rhs=osq[:], start=True, stop=True),
                  reads=["Am", "ones128"], writes=["pb5"])
            yield
            P.add("act", lambda e: e.activation(out=F1[:], in_=pb4(5), func=AF.Ln, bias=epscol[:, 0:1], scale=1.0),
                  reads=["pb5", "epscol"], writes=["F1"])
            P.add("act", lambda e: e.activation(out=F1[:], in_=F1[:], func=AF.Exp, scale=-0.5),
                  reads=["F1"], writes=["F1"])
            yield
            P.add("dve", lambda e: e.tensor_tensor(out=F1[:], in0=osrc, in1=F1[:], op=ALU.mult),
                  reads=osk + ["F1"], writes=["F1"])
            P.add("dve", lambda e: e.scalar_tensor_tensor(out=catO[:], in0=F1[:], scalar=glag[:, 0:1], in1=szg[:],
                                                           op0=ALU.mult, op1=ALU.mult),
                  reads=["F1", "glag", "szg"], writes=[catk + "_o"])

        def back1(L, i):
            sample = (i == NT - 1)
            P.tag = ("c", L, i)
            gi = L * NT + i
            uf = ufull[gi % 3]
            srck = "ucat" if sample else f"ufull{gi % 3}"
            accD = accD2[gi % 2]
            ak = f"acc{gi % 2}_"

            def ush(c, j):
                if sample:
                    return ucat[:, c, :, j:j + 8]
                return uf[:, c, j:j + 128]

            def accv(c):
                if sample:
                    return accD[:, c, :].rearrange("q (b r) -> q b r", r=8)
                return accD[:, c, :]

            cv_ps = pb4(7)
            for _ in range(CONV_DELAY):
                yield
            for c in range(4):
                for j in range(NPE):
                    for hl in range(2):
                        r0, r1 = hl * 64, (hl + 1) * 64
                        P.add("pe", lambda e, c=c, j=j, r0=r0, r1=r1: e.matmul(
                            cv_ps[r0:r1, c, :], lhsT=diag2[r0:r1, c, j, :], rhs=ush(c, j)[r0:r1],
                            start=(j == 0), stop=(j == NPE - 1)),
                            reads=[srck, "diag"], writes=["pb7"])
                    if j % CONV_YIELD_TAPS == CONV_YIELD_TAPS - 1:
                        yield
                yield
            for c in range(4):
                P.add("act", lambda e, c=c: e.activation(out=accD[:, c, :], in_=cv_ps[:, c, :], func=AF.Identity,
                                                          bias=convb[:, c:c + 1], scale=1.0),
                      reads=["pb7", "convb"], writes=[f"{ak}{c}"])
            yield
            for j in range(NPE, 31):
                for c in range(4):
                    P.add("dve", lambda e, c=c, j=j: e.scalar_tensor_tensor(
                        out=accv(c), in0=ush(c, j), scalar=convw[:, c, j:j + 1], in1=accv(c),
                        op0=ALU.mult, op1=ALU.add),
                        reads=[srck, "convw", f"{ak}{c}"], writes=[f"{ak}{c}"])
                if (j - NPE) % TAPS_PER_YIELD == TAPS_PER_YIELD - 1:
                    yield

        def back2(L, i):
            last_layer = (L == 1)
            P.tag = ("b", L, i)
            Xi = X[:, i, :]
            xk = f"X{i}"
            gi = L * NT + i
            szc = szc2[gi % 3]
            szck = f"szc{gi % 3}"
            catO = catO3[gi % 3]
            catk = f"catT{gi % 3}"
            xf = accD2[gi % 2]
            acck = [f"acc{gi % 2}_{c}" for c in range(4)]
            for _ in range(B2_DELAY):
                yield
            P.add("act", lambda e: e.copy(out=xb[:], in_=xf[:]), reads=acck, writes=["xb"])
            yield
            mean_ps = pb4(6)
            for c in range(4):
                P.add("pe", lambda e, c=c: e.matmul(mean_ps[:, c, :], lhsT=blk64[:], rhs=xb[:, c, :], start=True, stop=True),
                      reads=["xb", "blk64"], writes=["pb6"])
            yield
            P.add("dve", lambda e: e.tensor_tensor(out=xf[:], in0=xf[:], in1=mean_ps, op=ALU.subtract),
                  reads=acck + ["pb6"], writes=acck)
            yield
            P.add("act", lambda e: e.activation(out=sq[:], in_=xf[:], func=AF.Square), reads=acck, writes=["sq"])
            yield
            var_ps = pb4(6)
            for c in range(4):
                P.add("pe", lambda e, c=c: e.matmul(var_ps[:, c, :], lhsT=blk64[:], rhs=sq[:, c, :], start=True, stop=True),
                      reads=["sq", "blk64"], writes=["pb6"])
            yield
            P.add("act", lambda e: e.activation(out=var_ps, in_=var_ps, func=AF.Ln, bias=epscol[:, 0:1], scale=1.0),
                  reads=["pb6", "epscol"], writes=["pb6"])
            P.add("act", lambda e: e.activation(out=var_ps, in_=var_ps, func=AF.Exp, scale=-0.5),
                  reads=["pb6"], writes=["pb6"])
            yield
            P.add("dve", lambda e: e.tensor_tensor(out=xf[:], in0=xf[:], in1=var_ps, op=ALU.mult),
                  reads=acck + ["pb6"], writes=acck)
            yield
            for c in range(4):
                P.add("act", lambda e, c=c: e.activation(out=cvs[:, c, :], in_=xf[:, c, :], func=AF.Silu,
                                                          bias=cnb[:, c:c + 1], scale=cng[:, c:c + 1]),
                      reads=acck + ["cng", "cnb"], writes=["xb"])
            yield
            pw_ps = pb4(6)
            for oc in range(4):
                for k in range(4):
                    P.add("pe", lambda e, oc=oc, k=k: e.matmul(
                        pw_ps[:, oc, :], lhsT=wpw[:, k, oc * 128:(oc + 1) * 128], rhs=cvs[:, k, :],
                        start=(k == 0), stop=(k == 3)),
                        reads=["xb", "wpw"], writes=["pb6"])
            yield
            for oc in range(4):
                P.add("dve", lambda e, oc=oc: e.scalar_tensor_tensor(
                    out=catC[:, oc, :], in0=pw_ps[:, oc, :], scalar=bpw[:, oc:oc + 1], in1=szc[:, oc, :],
                    op0=ALU.add, op1=ALU.mult),
                    reads=["pb6", "bpw", szck], writes=["catC"])
            yield
            for half in range(2):
                for k in range(8):
                    P.add("pe", lambda e, half=half, k=k: e.matmul(
                        PB[6], lhsT=(catC[:, k, :] if k < 4 else catO[:, k - 4, :]), rhs=wout[:, k, half * 512:(half + 1) * 512],
                        start=(k == 0), stop=(k == 7)),
                        reads=["catC", catk + "_o", "wout"], writes=["pb6"])
                yield
                P.add("dve", lambda e, half=half: e.tensor_tensor(
                    out=X[:, i, half * 512:(half + 1) * 512], in0=X[:, i, half * 512:(half + 1) * 512], in1=PB[6],
                    op=ALU.add),
                    reads=[xk, "pb6"], writes=[xk])
                yield
            if last_layer:
                fc = 2 * NT + i
                P.add("act", lambda e: e.activation(out=sq[:].rearrange("q c t -> q (c t)"), in_=Xi[:, 0:512], func=AF.Square,
                                                     accum_out=ss[:, fc:fc + 1]),
                      reads=[xk, "ss"], writes=["sq", f"ss{fc}a"])
                P.add("act", lambda e: e.activation(out=sq[:].rearrange("q c t -> q (c t)"), in_=Xi[:, 512:1024], func=AF.Square,
                                                     accum_out=rstd[:, fc:fc + 1]),
                      reads=[xk, "ss"], writes=["sq", f"ss{fc}b"])
                P.add("dve", lambda e: e.tensor_tensor(out=ss[:, fc:fc + 1], in0=ss[:, fc:fc + 1], in1=rstd[:, fc:fc + 1],
                                                        op=ALU.add),
                      reads=[f"ss{fc}a", f"ss{fc}b"], writes=[f"ss{fc}"])
                yield
                P.add("act", lambda e: e.activation(out=rstd[:, fc:fc + 1], in_=ss[:, fc:fc + 1], func=AF.Ln,
                                                     bias=epscol[:, 0:1], scale=1.0 / 1024),
                      reads=[f"ss{fc}", "epscol"], writes=[f"rs{fc}"])
                P.add("act", lambda e: e.activation(out=rstd[:, fc:fc + 1], in_=rstd[:, fc:fc + 1], func=AF.Exp, scale=-0.5),
                      reads=[f"rs{fc}"], writes=[f"rs{fc}"])
                yield
                P.add("dve", lambda e: e.scalar_tensor_tensor(out=Xi, in0=Xi, scalar=rstd[:, fc:fc + 1], in1=fgb[:],
                                                               op0=ALU.mult, op1=ALU.mult),
                      reads=[xk, f"rs{fc}", "fgb"], writes=[xk])
                P.add("sp", lambda e: e.dma_start(out=y_d[i * 128:(i + 1) * 128, :], in_=Xi),
                      reads=[xk], writes=[f"o_y{i}"], dma="st")
                outs.append(f"o_y{i}")

        cache_dma(0)
        setup_a(0)
        load_x([0])
        setup_b(0)
        load_x([1])
        setup_c(0)
        load_x(range(2, NT), with_fg=True, after=["win_zc"])
        flags = {("a", 0), ("b", 0), ("c", 0)}
        sched = []
        for L in DBG_LAYERS:
            base = L * (PIPE_T * (NT - 1) + LAYER_GAP)
            for i in range(NT):
                sched.append([base + PIPE_T * i + PIPE_D2, 0, L, i, back2(L, i), ("c", L)])
                sched.append([base + PIPE_T * i, 1, L, i, front(L, i), ("a", L)])
                sched.append([base + PIPE_T * i + PIPE_D1, 2, L, i, back1(L, i), ("b", L)])
            if L == 0:
                sched.append([2, 3, 0, 0, setup_cache(0), ("b", 0)])
        active = []
        conv_done = set()
        rnd = 0
        while sched or active:
            for ent in [e for e in sched if e[0] <= rnd]:
                if ent[5] not in flags:
                    continue
                if ent[1] == 0 and (ent[2], ent[3]) not in conv_done:
                    continue
                sched.remove(ent)
                active.append(ent)
            active.sort(key=lambda e: (e[1], e[2], e[3]))
            for ent in list(active):
                P.tag = ("bfcs"[ent[1]], ent[2], ent[3]) if ent[1] < 3 else None
                try:
                    next(ent[4])
                except StopIteration:
                    active.remove(ent)
                    kind, L, i = ent[1], ent[2], ent[3]
                    if kind == 2:
                        conv_done.add((L, i))
                    if kind == 3 and 1 in DBG_LAYERS:
                        P.tag = None
                        cache_dma(1)
                    if L == 0 and i == NT - 1 and 1 in DBG_LAYERS:
                        P.tag = None
                        if kind == 1:
                            setup_a(1)
                            flags.add(("a", 1))
                            nominal = PIPE_T * (NT - 1) + LAYER_GAP
                            delta = max(0, rnd + 1 - nominal)
                            for e2 in sched:
                                if e2[2] == 1:
                                    e2[0] += delta
                        elif kind == 2:
                            setup_b(1)
                            for _ in setup_cache(1):
                                pass
                            flags.add(("b", 1))
                        elif kind == 0:
                            setup_c(1)
                            flags.add(("c", 1))
            rnd += 1
        P.tag = None
        P.add("sp", lambda e: e.nop(), reads=list(outs))
        P.build(nc, es)
    return nc, P


_CACHE = {}


def kernel(x_prompt, x_sample, cache_conv, state_gla, norm_g, w_in, w_alpha, b_alpha,
           conv_w, conv_b, cn_g, cn_b, w_pw, b_pw, gla_g, w_out, final_g):
    f = lambda a: np.ascontiguousarray(np.asarray(a, dtype=np.float32))
    x_prompt, x_sample, cache_conv, state_gla = f(x_prompt), f(x_sample), f(cache_conv), f(state_gla)
    shared = dict(norm_g=f(norm_g), w_in=f(w_in), w_alpha=f(w_alpha), b_alpha=f(b_alpha), conv_w=f(conv_w),
                  conv_b=f(conv_b), cn_g=f(cn_g), cn_b=f(cn_b), w_pw=f(w_pw), b_pw=f(b_pw), gla_g=f(gla_g),
                  w_out=f(w_out), final_g=f(final_g))
    if "nc" not in _CACHE:
        _CACHE["nc"] = build_nc()[0]
    nc = _CACHE["nc"]
    in_maps = []
    for c in range(NCORES):
        xs = x_sample[16 * c:16 * (c + 1)].reshape(128, 1024)
        m = dict(shared)
        m["x"] = np.ascontiguousarray(np.concatenate([x_prompt[c], xs], axis=0))
        m["cc"] = np.ascontiguousarray(cache_conv[:, 16 * c:16 * (c + 1)])
        m["sg"] = np.ascontiguousarray(state_gla[:, 16 * c:16 * (c + 1)])
        in_maps.append(m)
    res = run_bass_kernel_spmd(nc, in_maps, core_ids=list(range(NCORES)))
    R = res.results
    y_prompt = np.stack([R[c]["y"][:2048] for c in range(NCORES)], axis=0)
    y_sample = np.concatenate([R[c]["y"][2048:].reshape(16, 8, 1024) for c in range(NCORES)], axis=0)
    ncp = np.stack([R[c]["ncp"] for c in range(NCORES)], axis=1)
    nsp = np.stack([R[c]["nsp"] for c in range(NCORES)], axis=1)
    ncs = np.concatenate([R[c]["ncs"] for c in range(NCORES)], axis=1)
    nss = np.concatenate([R[c]["nss"] for c in range(NCORES)], axis=1)
    return (y_prompt.astype(np.float32), y_sample.astype(np.float32), ncp.astype(np.float32),
            nsp.astype(np.float32), ncs.astype(np.float32), nss.astype(np.float32))
```
